# Optimizing a Trainium2 kernel written in Bass

```python
import math
import jax, jax.numpy as jnp
from jax import lax
import numpy as np

D_MODEL = 1024
BATCH = 8
SEQ = 4096
DEPTH = 4

CTX_LEN = 256
GRID_W = 64

N_HEADS = 8
QK_NOPE_DIM = 128
QK_ROPE_DIM = 64
V_HEAD_DIM = 128
Q_LORA_RANK = 256
KV_LORA_RANK = 256
ROPE_BASE = 10000.0
Q_BLOCK = 128
MLA_SCALE = (QK_NOPE_DIM + QK_ROPE_DIM) ** -0.5
MLA_WIDTH = N_HEADS * V_HEAD_DIM

S5_WIDTH = 1024
S5_GROUP = 16
S5_GROUPS = S5_WIDTH // S5_GROUP
S5_STATE = 64
DT_MIN = 1e-3
DT_MAX = 1e-1

D_FF = 2816
FFN_RESIDUAL_WEIGHT = 0.5
N_ADA = 9
EPS = 1e-6

IN_SPLITS = (Q_LORA_RANK, KV_LORA_RANK, QK_ROPE_DIM, S5_WIDTH, D_MODEL)
IN_COLS = Q_LORA_RANK + KV_LORA_RANK + QK_ROPE_DIM + S5_WIDTH + 2 * D_MODEL

kernel_name = "hybrid_mla_s5_macaron_dit"


def rmsnorm(x, g):
    xf = x.astype(jnp.float32)
    y = xf * lax.rsqrt(jnp.mean(xf * xf, axis=-1, keepdims=True) + EPS)
    return (y * g.astype(jnp.float32)).astype(x.dtype)


def sublayer_in(h, g_pre, mod, s):
    return rmsnorm(h, g_pre) * (1 + mod[:, 3 * s + 1]) + mod[:, 3 * s]


def sublayer_out(h, y, g_post, mod, s, weight):
    return h + weight * mod[:, 3 * s + 2] * rmsnorm(y, g_post)


def swiglu(h, w_gate, w_up, w_down):
    return (jax.nn.silu(h @ w_gate) * (h @ w_up)) @ w_down


def axial_rope_tables(n_tokens):
    rows = n_tokens // GRID_W
    row = jnp.repeat(jnp.arange(rows, dtype=jnp.float32), GRID_W)
    col = jnp.tile(jnp.arange(GRID_W, dtype=jnp.float32), rows)
    n_freq = QK_ROPE_DIM // 4
    inv_freq = ROPE_BASE ** (-jnp.arange(n_freq, dtype=jnp.float32) / n_freq)
    ang_r = row[:, None] * inv_freq
    ang_c = col[:, None] * inv_freq
    ang = jnp.concatenate([ang_r, ang_r, ang_c, ang_c], axis=-1)
    return jnp.cos(ang), jnp.sin(ang)


def rotate_blocks(x):
    xb = x.reshape(x.shape[:-1] + (2, 2, QK_ROPE_DIM // 4))
    rot = jnp.stack([-xb[..., 1, :], xb[..., 0, :]], axis=-2)
    return rot.reshape(x.shape)


def apply_rope(x, cos, sin):
    return x * cos + rotate_blocks(x) * sin


def split_in_proj(z):
    idx = np.cumsum(IN_SPLITS).tolist()
    return jnp.split(z, idx, axis=-1)


def mla_queries(c_q, q_norm, w_uq):
    B, L = c_q.shape[:2]
    q = (rmsnorm(c_q, q_norm) @ w_uq).reshape(B, L, N_HEADS, QK_NOPE_DIM + QK_ROPE_DIM)
    return q[..., :QK_NOPE_DIM], q[..., QK_NOPE_DIM:]


def mla_keys_values(c_kv, kv_norm, w_ukv):
    B, L = c_kv.shape[:2]
    kv = (rmsnorm(c_kv, kv_norm) @ w_ukv).reshape(B, L, N_HEADS, QK_NOPE_DIM + V_HEAD_DIM)
    return kv[..., :QK_NOPE_DIM], kv[..., QK_NOPE_DIM:]


def mla_attend(q_n, q_r, k_n, k_r, v):
    s = jnp.einsum('bqhd,bkhd->bhqk', q_n, k_n) + jnp.einsum('bqhr,bkr->bhqk', q_r, k_r)
    p = jax.nn.softmax(s.astype(jnp.float32) * MLA_SCALE, axis=-1).astype(v.dtype)
    return jnp.einsum('bhqk,bkhd->bqhd', p, v)


def s5_discretise(a_re, a_im, log_dt, b_re, b_im):
    a = lax.complex(a_re.astype(jnp.float32), a_im.astype(jnp.float32))
    dt = jnp.exp(log_dt.astype(jnp.float32))[:, None]
    a_bar = jnp.exp(a * dt)
    b = lax.complex(b_re.astype(jnp.float32), b_im.astype(jnp.float32))
    b_bar = ((a_bar - 1.0) / a)[..., None] * b
    return a_bar, b_bar


def ssm_combine(e1, e2):
    a1, b1 = e1
    a2, b2 = e2
    return a1 * a2, a2 * b1 + b2


def s5_scan(u, a_bar, b_bar, reverse, init):
    L = u.shape[1]
    bu = jnp.einsum('blgc,gnc->blgn', u.astype(jnp.float32).astype(jnp.complex64), b_bar)
    if init is not None:
        first = L - 1 if reverse else 0
        bu = bu.at[:, first].add(a_bar * init)
    a = jnp.broadcast_to(a_bar, (1, L) + a_bar.shape)
    _, xs = lax.associative_scan(ssm_combine, (a, bu), reverse=reverse, axis=1)
    return xs


def s5_readout(xs, c_mat):
    return jnp.einsum('blgn,gcn->blgc', xs, c_mat).real


def s5_bidirectional(u_x, u_c, a_re, a_im, log_dt, b_re, b_im, c_re, c_im, d_skip, need_ctx_out):
    B, L, _ = u_x.shape
    Lc = u_c.shape[1]
    ux = u_x.reshape(B, L, S5_GROUPS, S5_GROUP)
    uc = u_c.reshape(B, Lc, S5_GROUPS, S5_GROUP)
    d32 = d_skip.astype(jnp.float32)
    y_x = d32 * u_x.astype(jnp.float32)
    y_c = d32 * u_c.astype(jnp.float32) if need_ctx_out else None
    for direction in range(2):
        reverse = direction == 1
        a_bar, b_bar = s5_discretise(a_re[direction], a_im[direction], log_dt[direction],
                                     b_re[direction], b_im[direction])
        c_mat = lax.complex(c_re[direction].astype(jnp.float32), c_im[direction].astype(jnp.float32))
        xs_c = s5_scan(uc, a_bar, b_bar, reverse, None)
        s_ctx = xs_c[:, 0] if reverse else xs_c[:, -1]
        xs_x = s5_scan(ux, a_bar, b_bar, reverse, s_ctx)
        y_x = y_x + s5_readout(xs_x, c_mat).reshape(B, L, S5_WIDTH)
        if need_ctx_out:
            y_c = y_c + s5_readout(xs_c, c_mat).reshape(B, Lc, S5_WIDTH)
    y_x = y_x.astype(u_x.dtype)
    if need_ctx_out:
        y_c = y_c.astype(u_c.dtype)
    return y_x, y_c


def s5_glu(y, glu_w, glu_b):
    z = jax.nn.gelu(y)
    return z * jax.nn.sigmoid(z @ glu_w + glu_b)


def token_mixer(h_x, h_c, cos, sin, need_ctx_out, w_in, q_norm, w_uq, kv_norm, w_ukv, w_o_mla,
                a_re, a_im, log_dt, b_re, b_im, c_re, c_im, d_skip, glu_w, glu_b, w_o_s5, w_out):
    B, L, _ = h_x.shape
    Lc = h_c.shape[1]
    cq_x, ckv_x, kr_x, u_x, gm_x, gs_x = split_in_proj(h_x @ w_in)
    cq_c, ckv_c, kr_c, u_c, gm_c, gs_c = split_in_proj(h_c @ w_in)

    qn_x, qr_x = mla_queries(cq_x, q_norm, w_uq)
    qr_x = apply_rope(qr_x, cos[:, None, :], sin[:, None, :])
    kr_x = apply_rope(kr_x, cos, sin)
    kn_x, v_x = mla_keys_values(ckv_x, kv_norm, w_ukv)
    kn_c, v_c = mla_keys_values(ckv_c, kv_norm, w_ukv)
    kn_all = jnp.concatenate([kn_x, kn_c], axis=1)
    kr_all = jnp.concatenate([kr_x, kr_c], axis=1)
    v_all = jnp.concatenate([v_x, v_c], axis=1)
    nb = L // Q_BLOCK

    def to_blocks(t):
        return jnp.moveaxis(t.reshape((B, nb, Q_BLOCK) + t.shape[2:]), 1, 0)

    o_x = lax.map(lambda qb: mla_attend(qb[0], qb[1], kn_all, kr_all, v_all),
                  (to_blocks(qn_x), to_blocks(qr_x)))
    o_x = jnp.moveaxis(o_x, 0, 1).reshape(B, L, MLA_WIDTH)
    y_mla_x = o_x @ w_o_mla

    s_x, s_c = s5_bidirectional(u_x, u_c, a_re, a_im, log_dt, b_re, b_im, c_re, c_im, d_skip, need_ctx_out)
    y_s5_x = s5_glu(s_x, glu_w, glu_b) @ w_o_s5

    y_x = (jax.nn.sigmoid(gm_x) * y_mla_x + jax.nn.sigmoid(gs_x) * y_s5_x) @ w_out
    if not need_ctx_out:
        return y_x, None
    qn_c, qr_c = mla_queries(cq_c, q_norm, w_uq)
    o_c = mla_attend(qn_c, qr_c, kn_c, kr_c, v_c).reshape(B, Lc, MLA_WIDTH)
    y_mla_c = o_c @ w_o_mla
    y_s5_c = s5_glu(s_c, glu_w, glu_b) @ w_o_s5
    y_c = (jax.nn.sigmoid(gm_c) * y_mla_c + jax.nn.sigmoid(gs_c) * y_s5_c) @ w_out
    return y_x, y_c


def setup_inputs(seed: int = 0) -> dict:
    key = jax.random.key(seed)
    ks = jax.random.split(key, 32)
    f32 = jnp.float32

    def nrm(k, shape, scale):
        return jax.random.normal(k, shape, f32) * scale

    def gain(k, shape):
        return 1.0 + 0.05 * jax.random.normal(k, shape, f32)

    ssm_shape = (DEPTH, 2, S5_GROUPS, S5_STATE)
    n_idx = jnp.arange(S5_STATE, dtype=f32)
    a_re = -0.5 + 0.01 * jax.random.normal(ks[14], ssm_shape, f32)
    a_im = jnp.pi * n_idx + 0.01 * jax.random.normal(ks[15], ssm_shape, f32)
    return {
        'x': nrm(ks[0], (BATCH, SEQ, D_MODEL), 1.0),
        'c': nrm(ks[1], (BATCH, D_MODEL), 1.0),
        'ctx': nrm(ks[2], (BATCH, CTX_LEN, D_MODEL), 1.0),
        'c_ctx': nrm(ks[3], (D_MODEL,), 1.0),
        'ada_w': nrm(ks[4], (DEPTH, D_MODEL, N_ADA * D_MODEL), 0.02),
        'ada_b': nrm(ks[5], (DEPTH, N_ADA * D_MODEL), 0.01),
        'norm_pre': gain(ks[6], (DEPTH, 3, D_MODEL)),
        'norm_post': gain(ks[7], (DEPTH, 3, D_MODEL)),
        'ffn_w_gate': nrm(ks[8], (DEPTH, 2, D_MODEL, D_FF), D_MODEL ** -0.5),
        'ffn_w_up': nrm(ks[9], (DEPTH, 2, D_MODEL, D_FF), D_MODEL ** -0.5),
        'ffn_w_down': nrm(ks[10], (DEPTH, 2, D_FF, D_MODEL), D_FF ** -0.5),
        'w_in': nrm(ks[11], (DEPTH, D_MODEL, IN_COLS), D_MODEL ** -0.5),
        'q_norm': gain(ks[12], (DEPTH, Q_LORA_RANK)),
        'w_uq': nrm(ks[13], (DEPTH, Q_LORA_RANK, N_HEADS * (QK_NOPE_DIM + QK_ROPE_DIM)), Q_LORA_RANK ** -0.5),
        'kv_norm': gain(ks[16], (DEPTH, KV_LORA_RANK)),
        'w_ukv': nrm(ks[17], (DEPTH, KV_LORA_RANK, N_HEADS * (QK_NOPE_DIM + V_HEAD_DIM)), KV_LORA_RANK ** -0.5),
        'w_o_mla': nrm(ks[18], (DEPTH, MLA_WIDTH, D_MODEL), MLA_WIDTH ** -0.5),
        's5_a_re': a_re,
        's5_a_im': a_im,
        's5_log_dt': jax.random.uniform(ks[19], (DEPTH, 2, S5_GROUPS), f32,
                                        minval=math.log(DT_MIN), maxval=math.log(DT_MAX)),
        's5_b_re': nrm(ks[20], (DEPTH, 2, S5_GROUPS, S5_STATE, S5_GROUP), (2 * S5_GROUP) ** -0.5),
        's5_b_im': nrm(ks[21], (DEPTH, 2, S5_GROUPS, S5_STATE, S5_GROUP), (2 * S5_GROUP) ** -0.5),
        's5_c_re': nrm(ks[22], (DEPTH, 2, S5_GROUPS, S5_GROUP, S5_STATE), S5_STATE ** -0.5),
        's5_c_im': nrm(ks[23], (DEPTH, 2, S5_GROUPS, S5_GROUP, S5_STATE), S5_STATE ** -0.5),
        's5_d': nrm(ks[24], (DEPTH, S5_WIDTH), 1.0),
        'glu_w': nrm(ks[25], (DEPTH, S5_WIDTH, S5_WIDTH), S5_WIDTH ** -0.5),
        'glu_b': nrm(ks[26], (DEPTH, S5_WIDTH), 0.01),
        'w_o_s5': nrm(ks[27], (DEPTH, S5_WIDTH, D_MODEL), S5_WIDTH ** -0.5),
        'w_out': nrm(ks[28], (DEPTH, D_MODEL, D_MODEL), D_MODEL ** -0.5),
    }


def reference(x, c, ctx, c_ctx, ada_w, ada_b, norm_pre, norm_post, ffn_w_gate, ffn_w_up, ffn_w_down,
              w_in, q_norm, w_uq, kv_norm, w_ukv, w_o_mla, s5_a_re, s5_a_im, s5_log_dt,
              s5_b_re, s5_b_im, s5_c_re, s5_c_im, s5_d, glu_w, glu_b, w_o_s5, w_out):
    B, L, _ = x.shape
    cos, sin = axial_rope_tables(L)
    cos = cos.astype(x.dtype)
    sin = sin.astype(x.dtype)
    silu_c = jax.nn.silu(c)
    silu_cc = jax.nn.silu(c_ctx)
    for l in range(DEPTH):
        last = l == DEPTH - 1
        mod_x = (silu_c @ ada_w[l] + ada_b[l]).reshape(B, N_ADA, 1, D_MODEL)
        mod_c = (silu_cc @ ada_w[l] + ada_b[l]).reshape(1, N_ADA, 1, D_MODEL)

        y = swiglu(sublayer_in(x, norm_pre[l, 0], mod_x, 0), ffn_w_gate[l, 0], ffn_w_up[l, 0], ffn_w_down[l, 0])
        x = sublayer_out(x, y, norm_post[l, 0], mod_x, 0, FFN_RESIDUAL_WEIGHT)
        y = swiglu(sublayer_in(ctx, norm_pre[l, 0], mod_c, 0), ffn_w_gate[l, 0], ffn_w_up[l, 0], ffn_w_down[l, 0])
        ctx = sublayer_out(ctx, y, norm_post[l, 0], mod_c, 0, FFN_RESIDUAL_WEIGHT)

        y_x, y_c = token_mixer(
            sublayer_in(x, norm_pre[l, 1], mod_x, 1), sublayer_in(ctx, norm_pre[l, 1], mod_c, 1),
            cos, sin, not last, w_in[l], q_norm[l], w_uq[l], kv_norm[l], w_ukv[l], w_o_mla[l],
            s5_a_re[l], s5_a_im[l], s5_log_dt[l], s5_b_re[l], s5_b_im[l], s5_c_re[l], s5_c_im[l],
            s5_d[l], glu_w[l], glu_b[l], w_o_s5[l], w_out[l])
        x = sublayer_out(x, y_x, norm_post[l, 1], mod_x, 1, 1.0)

        y = swiglu(sublayer_in(x, norm_pre[l, 2], mod_x, 2), ffn_w_gate[l, 1], ffn_w_up[l, 1], ffn_w_down[l, 1])
        x = sublayer_out(x, y, norm_post[l, 2], mod_x, 2, FFN_RESIDUAL_WEIGHT)
        if not last:
            ctx = sublayer_out(ctx, y_c, norm_post[l, 1], mod_c, 1, 1.0)
            y = swiglu(sublayer_in(ctx, norm_pre[l, 2], mod_c, 2), ffn_w_gate[l, 1], ffn_w_up[l, 1], ffn_w_down[l, 1])
            ctx = sublayer_out(ctx, y, norm_post[l, 2], mod_c, 2, FFN_RESIDUAL_WEIGHT)
    return x
```

```python
import math
from contextlib import ExitStack
import numpy as np
import ml_dtypes
import concourse.bass as bass
import concourse.mybir as mybir
from concourse.bass_utils import run_bass_kernel_spmd

F32 = mybir.dt.float32
BF16 = mybir.dt.bfloat16
AF = mybir.ActivationFunctionType
ALU = mybir.AluOpType

D = 1024
DFF = 2816
NFF = 22
H = 8
LC = 256
EPS = 1e-6
NADA = 9
INC = 3648
SCALE = (128 + 64) ** -0.5
SQD = 32.0


class Sem:
    def __init__(self, h, inc):
        self.h = h
        self.inc = inc
        self.issued = 0


class Buf:
    def __init__(self):
        self.w = None
        self.r = {}


class Sched:
    def __init__(self, nc, stack, ndsem=40):
        self.nc = nc
        self.engs = {'pe': nc.tensor, 'act': nc.scalar, 'dve': nc.vector, 'pool': nc.gpsimd, 'sp': nc.sync}
        self.esem = {}
        for k in self.engs:
            self.esem[k] = Sem(stack.enter_context(nc.semaphore("e_" + k)), 1)
        self.dfree = [Sem(stack.enter_context(nc.semaphore("d%d" % i)), 16) for i in range(ndsem)]
        self.dall = list(self.dfree)
        self.dmap = {}
        self.seen = {k: {} for k in self.engs}
        self.nins = 0

    def _wait(self, eng, deps):
        E = self.engs[eng]
        seen = self.seen[eng]
        for s, n in deps.items():
            if s is self.esem['pe'] and eng == 'pe':
                continue
            val = n * s.inc if s.inc == 1 else s.issued * 16
            if seen.get(s, 0) < val:
                E.wait_ge(s.h, val)
                seen[s] = val
                self.nins += 1

    def _deps(self, reads, writes):
        deps = {}
        for b in reads:
            if b.w is not None:
                s, n = b.w
                if deps.get(s, 0) < n:
                    deps[s] = n
        for b in writes:
            if b.w is not None:
                s, n = b.w
                if deps.get(s, 0) < n:
                    deps[s] = n
            for s, n in b.r.items():
                if deps.get(s, 0) < n:
                    deps[s] = n
        return deps

    def _mark(self, s, reads, writes):
        n = s.issued
        for b in reads:
            b.r[s] = n
        for b in writes:
            b.w = (s, n)
            b.r = {}

    def op(self, eng, fn, reads=(), writes=()):
        self._wait(eng, self._deps(reads, writes))
        ins = fn(self.engs[eng])
        s = self.esem[eng]
        ins.then_inc(s.h, 1)
        s.issued += 1
        self.nins += 1
        self._mark(s, reads, writes)

    def dsem(self, key):
        s = self.dmap.get(key)
        if s is None:
            s = self.dfree.pop()
            self.dmap[key] = s
        return s

    def dma(self, out, in_, reads=(), writes=(), key=None, eng='sp', slow=False):
        self._wait(eng, self._deps(reads, writes))
        s = self.dsem(key)
        E = self.engs[eng]
        if slow:
            ins = E.dma_start(out=out, in_=in_, allow_slow_non_contiguous=True)
        else:
            ins = E.dma_start(out=out, in_=in_)
        ins.then_inc(s.h, 16)
        s.issued += 1
        self.nins += 1
        self._mark(s, reads, writes)

    def barrier(self):
        sems = list(self.esem.values()) + self.dall
        for eng in self.engs:
            deps = {s: s.issued for s in sems if s.issued > 0 and s is not self.esem[eng]}
            E = self.engs[eng]
            seen = self.seen[eng]
            for s, n in deps.items():
                val = n * s.inc
                if seen.get(s, 0) < val:
                    E.wait_ge(s.h, val)
                    seen[s] = val
                    self.nins += 1
        for k, s in self.dmap.items():
            self.dfree.append(s)
        self.dmap = {}


class Ctx:
    pass


def build_program(L, depth, dbg=None):
    nc = bass.Bass("TRN2", target_bir_lowering=False)
    NT = L // 128
    NKB = L // 1024
    LT = L + LC
    NCH = L // 8
    NCC = LC // 8

    def din(name, shape, dt=F32):
        return nc.dram_tensor(name, list(shape), dt, kind="ExternalInput")

    g = Ctx()
    g.dbg = dbg
    g.x = din("x", [L, D]); g.c = din("c", [1, D]); g.ctx = din("ctx", [LC, D]); g.c_ctx = din("c_ctx", [1, D])
    g.ada_w = din("ada_w", [depth, D, NADA * D]); g.ada_b = din("ada_b", [depth, NADA * D])
    g.norm_pre = din("norm_pre", [depth, 3, D]); g.norm_post = din("norm_post", [depth, 3, D])
    g.wg = din("ffn_w_gate", [depth, 2, D, DFF]); g.wu = din("ffn_w_up", [depth, 2, D, DFF])
    g.wd = din("ffn_w_down", [depth, 2, DFF, D])
    g.w_in = din("w_in", [depth, D, INC]); g.q_norm = din("q_norm", [depth, 256])
    g.w_uq = din("w_uq", [depth, 256, 1536]); g.kv_norm = din("kv_norm", [depth, 256])
    g.w_ukv = din("w_ukv", [depth, 256, 2048]); g.w_o_mla = din("w_o_mla", [depth, D, D])
    g.a_re = din("s5_a_re", [depth, 2, 64, 64]); g.a_im = din("s5_a_im", [depth, 2, 64, 64])
    g.log_dt = din("s5_log_dt", [depth, 2, 64])
    g.b_re = din("s5_b_re", [depth, 2, 64, 64, 16]); g.b_im = din("s5_b_im", [depth, 2, 64, 64, 16])
    g.c_re = din("s5_c_re", [depth, 2, 64, 16, 64]); g.c_im = din("s5_c_im", [depth, 2, 64, 16, 64])
    g.s5_d = din("s5_d", [depth, D]); g.glu_w = din("glu_w", [depth, D, D]); g.glu_b = din("glu_b", [depth, D])
    g.w_o_s5 = din("w_o_s5", [depth, D, D]); g.w_out = din("w_out", [depth, D, D])
    g.ropec = din("rope_cos", [128, L]); g.ropes = din("rope_sin", [128, L])
    g.identf = din("identf", [128, 128]); g.maskl = din("maskl", [128, 128]); g.masku = din("masku", [128, 128])
    out = nc.dram_tensor("out", [L, D], F32, kind="ExternalOutput")

    def scr(name, shape, dt):
        if dbg is not None and name in dbg:
            return nc.dram_tensor(name, list(shape), dt, kind="ExternalOutput")
        return nc.dram_tensor(name, list(shape), dt)

    g.xs = scr("xs", [L, D], F32); g.cs = scr("cs", [LC, D], F32)
    g.mod = scr("mod", [depth, 2, NADA * D], F32)
    g.aT = scr("aT", [NFF, 128, LT], BF16)
    g.hT = scr("hT", [8, 128, LT], BF16)
    g.qn = scr("qn", [H, 128, LT], BF16); g.qr = scr("qr", [4, 128, LT], BF16)
    g.kn = scr("kn", [H, 128, LT], BF16); g.kr = scr("kr", [128, LT], BF16)
    g.v = scr("v", [LT // 128, 128, D], BF16)
    g.oT = scr("oT", [H, 128, LT], BF16)
    g.zT = scr("zT", [8, 128, LT], BF16)
    if dbg is not None and "dbg_gx" in dbg:
        g.dbg_gx = nc.dram_tensor("dbg_gx", [64, 128, (L // 1024) * 128], F32, kind="ExternalOutput")

    def AP(t, off, dims):
        return bass.AP(t, off, [list(d) for d in dims])

    def xtile_ap(t, stream, i):
        if stream == 0:
            kb, tau = divmod(i, 8)
            return AP(t, (1024 * kb + tau) * D, [[8 * D, 128], [1, D]])
        return AP(t, i * 128 * D, [[D, 128], [1, D]])

    with ExitStack() as gs:
        S = Sched(nc, gs)
        sb_id = gs.enter_context(nc.sbuf_tensor("identb", [128, 128], BF16))
        sb_idf = gs.enter_context(nc.sbuf_tensor("identf_sb", [128, 128], F32))
        b_id = Buf()
        S.dma(sb_idf[:], g.identf.ap()[:, :], writes=[b_id], key=b_id)
        S.op('pool', lambda e: e.tensor_copy(out=sb_id[:], in_=sb_idf[:]), reads=[b_id], writes=[b_id])
        g.ident = sb_id; g.identf_sb = sb_idf; g.b_id = b_id

        uid = [0]

        def T(st, name, shape, dt):
            uid[0] += 1
            return st.enter_context(nc.sbuf_tensor("%s_u%d" % (name, uid[0]), list(shape), dt))

        def P(st, name, shape, dt=F32):
            uid[0] += 1
            return st.enter_context(nc.psum_tensor("%s_u%d" % (name, uid[0]), list(shape), dt))

        def rstd_ops(ss_ap, tmp_ap, out_ap, bss, brs, n):
            S.op('dve', lambda e: e.tensor_scalar(out=tmp_ap, in0=ss_ap, scalar1=1.0 / n, scalar2=EPS, op0=ALU.mult, op1=ALU.add), reads=[bss], writes=[brs])
            S.op('act', lambda e: e.activation(out=tmp_ap, in_=tmp_ap, func=AF.Sqrt), reads=[brs], writes=[brs])
            S.op('dve', lambda e: e.reciprocal(out=out_ap, in_=tmp_ap), reads=[brs], writes=[brs])

        def load_bc(st, name, src_ap, key_b):
            t = T(st, name, [128, D], F32)
            S.dma(t[:], src_ap.partition_broadcast(128), writes=[key_b], key=key_b)
            return t

        def stream_info(stream):
            if stream == 0:
                return NT, 0
            return LC // 128, L

        def phase_mod():
            with ExitStack() as st:
                cT = T(st, "cT", [128, 8, 2], F32); bcT = Buf()
                S.dma(cT[:, :, 0], g.c.ap().rearrange("o (j p) -> p (o j)", p=128), writes=[bcT], key=bcT, slow=True)
                S.dma(cT[:, :, 1], g.c_ctx.ap().rearrange("o (j p) -> p (o j)", p=128), writes=[bcT], key=bcT, slow=True)
                S.op('act', lambda e: e.activation(out=cT[:], in_=cT[:], func=AF.Silu), reads=[bcT], writes=[bcT])
                GW = 1536
                stg = [[T(st, "adst%d_%d" % (s_, k), [128, GW], F32) for k in range(8)] for s_ in range(2)]
                bst = [[Buf() for k in range(8)] for s_ in range(2)]
                adab = T(st, "adab", [2, NADA * D], F32); badab = Buf()
                modsb = [T(st, "modsb%d" % i, [2, GW], F32) for i in range(2)]; bmod = [Buf(), Buf()]
                ps = [P(st, "psm%d" % i, [128, 512]) for i in range(3)]; bps = [Buf() for _ in range(3)]
                gi = 0
                for l in range(depth):
                    for r_ in range(2):
                        S.dma(adab[r_:r_ + 1, :], g.ada_b.ap()[l:l + 1, :], reads=[], writes=[badab], key=badab)
                    for grp in range(NADA * D // GW):
                        sl = gi % 2
                        for k in range(8):
                            S.dma(stg[sl][k][:], g.ada_w.ap()[l, k * 128:(k + 1) * 128, grp * GW:(grp + 1) * GW], writes=[bst[sl][k]], key=bst[sl][k])
                        for nt in range(3):
                            for k in range(8):
                                S.op('pe', lambda e, k=k, nt=nt, sl=sl: e.matmul(ps[nt][0:2, :], cT[:, k, :], stg[sl][k][:, nt * 512:(nt + 1) * 512], start=(k == 0), stop=(k == 7)),
                                     reads=[bcT, bst[sl][k]], writes=[bps[nt]])
                            S.op('dve', lambda e, nt=nt, sl=sl, grp=grp: e.tensor_tensor(out=modsb[sl][:, nt * 512:(nt + 1) * 512], in0=ps[nt][0:2, :], in1=adab[:, grp * GW + nt * 512: grp * GW + (nt + 1) * 512], op=ALU.add),
                                 reads=[bps[nt], badab], writes=[bmod[sl]])
                        S.dma(g.mod.ap()[l, :, grp * GW:(grp + 1) * GW], modsb[sl][:], reads=[bmod[sl]], key=bmod[sl])
                        gi += 1
                S.barrier()

        def load_pre(st, l, sub, stream, tag):
            bsc = Buf(); bsh = Buf(); bg = Buf()
            sc = load_bc(st, "sc" + tag, g.mod.ap()[l, stream:stream + 1, (3 * sub + 1) * D:(3 * sub + 2) * D], bsc)
            sh = load_bc(st, "sh" + tag, g.mod.ap()[l, stream:stream + 1, (3 * sub) * D:(3 * sub + 1) * D], bsh)
            gp = load_bc(st, "gp" + tag, g.norm_pre.ap()[l, sub:sub + 1, :], bg)
            S.op('dve', lambda e: e.scalar_tensor_tensor(out=sc[:], in0=sc[:], scalar=1.0, in1=gp[:], op0=ALU.add, op1=ALU.mult), reads=[bsc, bg], writes=[bsc])
            return sc, bsc, sh, bsh

        def load_post(st, l, sub, stream, weight, tag):
            bga = Buf(); bg = Buf()
            ga = load_bc(st, "ga" + tag, g.mod.ap()[l, stream:stream + 1, (3 * sub + 2) * D:(3 * sub + 3) * D], bga)
            gp = load_bc(st, "gq" + tag, g.norm_post.ap()[l, sub:sub + 1, :], bg)
            S.op('dve', lambda e: e.scalar_tensor_tensor(out=ga[:], in0=ga[:], scalar=float(weight), in1=gp[:], op0=ALU.mult, op1=ALU.mult), reads=[bga, bg], writes=[bga])
            return ga, bga

        class PreNorm:
            def __init__(self, st, tag, nhb=4):
                self.xt = [T(st, "pn_x%s%d" % (tag, i), [128, D], F32) for i in range(2)]
                self.bx = [Buf(), Buf()]
                self.junk = T(st, "pn_j" + tag, [128, D], BF16); self.bj = Buf()
                self.ss = [T(st, "pn_ss%s%d" % (tag, i), [128, 4], F32) for i in range(2)]; self.bss = [Buf(), Buf()]
                self.tmp = [T(st, "pn_t%s%d" % (tag, i), [128, D], F32) for i in range(2)]; self.bt = [Buf(), Buf()]
                self.hb = [T(st, "pn_h%s%d" % (tag, i), [128, D], BF16) for i in range(nhb)]; self.bh = [Buf() for _ in range(nhb)]
                self.pt = [P(st, "pn_pt%s%d" % (tag, i), [128, 8, 128], BF16) for i in range(2)]; self.bpt = [Buf(), Buf()]
                self.n = 0
                self.m = 0

            def run1(self, src_ap, gsc, bgs, sh, bsh):
                i = self.n % 2
                j = self.n % len(self.hb)
                self.n += 1
                xt, bx, ss, bss, tmp, bt, hb, bh = self.xt[i], self.bx[i], self.ss[i], self.bss[i], self.tmp[i], self.bt[i], self.hb[j], self.bh[j]
                S.dma(xt[:], src_ap, writes=[bx], key=bx)
                S.op('act', lambda e: e.activation(out=self.junk[:], in_=xt[:], func=AF.Square, accum_out=ss[:, 0:1]), reads=[bx], writes=[self.bj, bss])
                rstd_ops(ss[:, 0:1], ss[:, 1:2], ss[:, 2:3], bss, bss, D)
                S.op('dve', lambda e: e.scalar_tensor_tensor(out=tmp[:], in0=xt[:], scalar=ss[:, 2:3], in1=gsc[:], op0=ALU.mult, op1=ALU.mult), reads=[bx, bss, bgs], writes=[bt])
                S.op('pool', lambda e: e.tensor_tensor(out=hb[:], in0=tmp[:], in1=sh[:], op=ALU.add), reads=[bt, bsh], writes=[bh])
                return j

            def run2(self, j, hT, bhT, col0):
                k = self.m % 2
                self.m += 1
                hb, bh, pt, bpt = self.hb[j], self.bh[j], self.pt[k], self.bpt[k]
                for jj in range(8):
                    S.op('pe', lambda e, jj=jj: e.transpose(pt[:, jj, :], hb[:, jj * 128:(jj + 1) * 128], g.ident[:]), reads=[bh, g.b_id], writes=[bpt])
                S.op('act', lambda e: e.copy(out=hT[:, :, col0:col0 + 128], in_=pt[:]), reads=[bpt], writes=[bhT])

            def run(self, src_ap, gsc, bgs, sh, bsh, hT, bhT, col0):
                self.run2(self.run1(src_ap, gsc, bgs, sh, bsh), hT, bhT, col0)

        class PostNorm:
            def __init__(self, st, tag):
                self.xt = [T(st, "po_x%s%d" % (tag, i), [128, D], F32) for i in range(2)]; self.bx = [Buf(), Buf()]
                self.junk = T(st, "po_j" + tag, [128, 512], BF16); self.bj = Buf()
                self.ss = [T(st, "po_ss%s%d" % (tag, i), [128, 8], F32) for i in range(2)]; self.bss = [Buf(), Buf()]
                self.tmp = [T(st, "po_t%s%d" % (tag, i), [128, D], F32) for i in range(2)]; self.bt = [Buf(), Buf()]
                self.n = 0

            def run(self, src_ap, dst_ap, py, bpy, gg, bgg):
                i = self.n % 2
                self.n += 1
                xt, bx, ss, bss, tmp, bt = self.xt[i], self.bx[i], self.ss[i], self.bss[i], self.tmp[i], self.bt[i]
                S.dma(xt[:], src_ap, writes=[bx], key=bx)
                for hf in range(2):
                    S.op('act', lambda e, hf=hf: e.activation(out=self.junk[:], in_=py[hf][:], func=AF.Square, accum_out=ss[:, hf:hf + 1]), reads=[bpy[hf]], writes=[self.bj, bss])
                S.op('dve', lambda e: e.tensor_tensor(out=ss[:, 2:3], in0=ss[:, 0:1], in1=ss[:, 1:2], op=ALU.add), reads=[bss], writes=[bss])
                rstd_ops(ss[:, 2:3], ss[:, 3:4], ss[:, 4:5], bss, bss, D)
                for hf in range(2):
                    S.op('dve', lambda e, hf=hf: e.scalar_tensor_tensor(out=tmp[:, hf * 512:(hf + 1) * 512], in0=py[hf][:], scalar=ss[:, 4:5], in1=gg[:, hf * 512:(hf + 1) * 512], op0=ALU.mult, op1=ALU.mult),
                         reads=[bpy[hf], bss, bgg], writes=[bt])
                S.op('pool', lambda e: e.tensor_tensor(out=tmp[:], in0=tmp[:], in1=xt[:], op=ALU.add), reads=[bt, bx], writes=[bt])
                S.dma(dst_ap, tmp[:], reads=[bt], key=bt)

        class WLoad:
            def __init__(self, st, width, tag, n=3):
                self.stg = [T(st, "wst%s%d" % (tag, i), [128, width], F32) for i in range(n)]
                self.b = [Buf() for _ in range(n)]
                self.n = 0
                self.width = width

            def load(self, dst_ap, src_ap, bdst, rows=128, cols=None, eng=None, scale_ap=None, bscale=None, neg=False):
                i = self.n % len(self.stg)
                self.n += 1
                cols = self.width if cols is None else cols
                stg, b = self.stg[i], self.b[i]
                S.dma(stg[0:rows, 0:cols], src_ap, writes=[b], key=b)
                if scale_ap is not None:
                    S.op('dve', lambda e: e.tensor_scalar(out=dst_ap, in0=stg[0:rows, 0:cols], scalar1=scale_ap, scalar2=None, op0=ALU.mult), reads=[b, bscale], writes=[bdst])
                else:
                    eng = eng or ('pool' if self.n % 2 else 'dve')
                    S.op(eng, lambda e: e.tensor_copy(out=dst_ap, in_=stg[0:rows, 0:cols]), reads=[b], writes=[bdst])

        g.T = T; g.P = P; g.S = S; g.AP = AP; g.xtile_ap = xtile_ap; g.load_pre = load_pre; g.load_post = load_post
        g.PreNorm = PreNorm; g.PostNorm = PostNorm; g.WLoad = WLoad; g.rstd_ops = rstd_ops; g.stream_info = stream_info
        g.L = L; g.NT = NT; g.LT = LT; g.depth = depth; g.nc = nc

        def phase_ffn_a(l, j, sub, src, do_ctx=True):
            with ExitStack() as st:
                wgs = T(st, "wg_sb", [128, 8, DFF], BF16); wus = T(st, "wu_sb", [128, 8, DFF], BF16)
                bw = Buf()
                wl = WLoad(st, 1408, "fa")
                for k in range(8):
                    for hf in range(2):
                        wl.load(wgs[:, k, hf * 1408:(hf + 1) * 1408], g.wg.ap()[l, j, k * 128:(k + 1) * 128, hf * 1408:(hf + 1) * 1408], bw)
                        wl.load(wus[:, k, hf * 1408:(hf + 1) * 1408], g.wu.ap()[l, j, k * 128:(k + 1) * 128, hf * 1408:(hf + 1) * 1408], bw)
                pn = PreNorm(st, "fa")
                hT = [T(st, "fa_hT%d" % i, [128, 8, 512], BF16) for i in range(2)]; bhT = [Buf(), Buf()]
                psg = [P(st, "psg%d" % i, [128, 512]) for i in range(2)]; bpg = [Buf(), Buf()]
                psu = [P(st, "psu%d" % i, [128, 512]) for i in range(2)]; bpu = [Buf(), Buf()]
                sg = [T(st, "fa_sg%d" % i, [128, 512], F32) for i in range(2)]; bsg = [Buf(), Buf()]
                ao = [T(st, "fa_ao%d" % i, [128, 512], BF16) for i in range(4)]; bao = [Buf() for _ in range(4)]
                blk = 0
                cnt = 0
                for stream in ([0, 1] if do_ctx else [0]):
                    with ExitStack() as st2:
                        gsc, bgs, sh, bsh = load_pre(st2, l, sub, stream, "fa%d" % stream)
                        ntile, cbase = stream_info(stream)
                        sap = src[stream]
                        blocks = [(b0, min(4, ntile - b0)) for b0 in range(0, ntile, 4)]
                        pend = [pn.run1(xtile_ap(sap, stream, blocks[0][0] + i), gsc, bgs, sh, bsh) for i in range(blocks[0][1])]
                        for bi, (b0, nt_) in enumerate(blocks):
                            ncol = nt_ * 128
                            hb_, bhb = hT[blk % 2], bhT[blk % 2]
                            blk += 1
                            for i, j_ in enumerate(pend):
                                pn.run2(j_, hb_, bhb, i * 128)
                            if bi + 1 < len(blocks):
                                pend = [pn.run1(xtile_ap(sap, stream, blocks[bi + 1][0] + i), gsc, bgs, sh, bsh) for i in range(blocks[bi + 1][1])]
                            for f in range(NFF):
                                q = cnt % 2
                                for k in range(8):
                                    S.op('pe', lambda e, k=k, f=f, q=q: e.matmul(psg[q][:, 0:ncol], wgs[:, k, f * 128:(f + 1) * 128], hb_[:, k, 0:ncol], start=(k == 0), stop=(k == 7)), reads=[bw, bhb], writes=[bpg[q]])
                                for k in range(8):
                                    S.op('pe', lambda e, k=k, f=f, q=q: e.matmul(psu[q][:, 0:ncol], wus[:, k, f * 128:(f + 1) * 128], hb_[:, k, 0:ncol], start=(k == 0), stop=(k == 7)), reads=[bw, bhb], writes=[bpu[q]])
                                S.op('act', lambda e, q=q: e.activation(out=sg[q][:, 0:ncol], in_=psg[q][:, 0:ncol], func=AF.Silu), reads=[bpg[q]], writes=[bsg[q]])
                                a_ = cnt % 4
                                S.op('dve', lambda e, q=q, a_=a_: e.tensor_tensor(out=ao[a_][:, 0:ncol], in0=sg[q][:, 0:ncol], in1=psu[q][:, 0:ncol], op=ALU.mult), reads=[bsg[q], bpu[q]], writes=[bao[a_]])
                                S.dma(g.aT.ap()[f, :, cbase + b0 * 128: cbase + b0 * 128 + ncol], ao[a_][:, 0:ncol], reads=[bao[a_]], key=bao[a_])
                                cnt += 1
                S.barrier()

        def phase_ffn_b(l, j, sub, src, dst, do_ctx=True):
            with ExitStack() as st:
                wds = T(st, "wd_sb", [128, NFF, D], BF16); bw = Buf()
                wl = WLoad(st, D, "fb")
                for f in range(NFF):
                    wl.load(wds[:, f, :], g.wd.ap()[l, j, f * 128:(f + 1) * 128, :], bw)
                po = PostNorm(st, "fb")
                asb = [T(st, "fb_a%d" % i, [128, NFF, 512], BF16) for i in range(2)]; ba = [Buf(), Buf()]
                py = [[P(st, "fb_py%d_%d" % (i, hf), [128, 512]) for hf in range(2)] for i in range(2)]
                bpy = [[Buf(), Buf()] for i in range(2)]
                blk = 0; cnt = 0
                for stream in ([0, 1] if do_ctx else [0]):
                    with ExitStack() as st2:
                        gg, bgg = load_post(st2, l, sub, stream, 0.5, "fb%d" % stream)
                        ntile, cbase = stream_info(stream)
                        for b0 in range(0, ntile, 4):
                            nt_ = min(4, ntile - b0)
                            ncol = nt_ * 128
                            a_, ba_ = asb[blk % 2], ba[blk % 2]
                            blk += 1
                            S.dma(a_[:, :, 0:ncol], g.aT.ap()[:, :, cbase + b0 * 128: cbase + b0 * 128 + ncol].rearrange("f p c -> p f c"), writes=[ba_], key=ba_)
                            for i in range(nt_):
                                q = cnt % 2
                                cnt += 1
                                for hf in range(2):
                                    for f in range(NFF):
                                        S.op('pe', lambda e, f=f, hf=hf, i=i, q=q: e.matmul(py[q][hf][:], a_[:, f, i * 128:(i + 1) * 128], wds[:, f, hf * 512:(hf + 1) * 512], start=(f == 0), stop=(f == NFF - 1)),
                                             reads=[ba_, bw], writes=[bpy[q][hf]])
                                po.run(xtile_ap(src[stream], stream, b0 + i), xtile_ap(dst[stream], stream, b0 + i), py[q], bpy[q], gg, bgg)
                S.barrier()

        g.phase_mod = phase_mod; g.phase_ffn_a = phase_ffn_a; g.phase_ffn_b = phase_ffn_b
        from_mixer = build_mixer(g)

        phase_mod()
        cur = [g.x, g.ctx]
        scrs = [g.xs, g.cs]
        for l in range(depth):
            last = (l == depth - 1)
            final_dst = [out, g.cs]
            phase_ffn_a(l, 0, 0, cur)
            phase_ffn_b(l, 0, 0, cur, scrs)
            cur = scrs
            if dbg is not None and dbg.get("stop") == "ffn0":
                break
            from_mixer(l, cur, scrs, last)
            if dbg is not None and dbg.get("stop") == "mixer":
                break
            phase_ffn_a(l, 1, 2, cur, do_ctx=not last)
            phase_ffn_b(l, 1, 2, cur, final_dst if last else scrs, do_ctx=not last)
        S.barrier()
        g.nins = S.nins
    return nc, g


def build_mixer(g):
    T, P, S, AP, xtile_ap = g.T, g.P, g.S, g.AP, g.xtile_ap
    L, NT, LT = g.L, g.NT, g.LT
    nc = g.nc
    NTT = LT // 128

    def sap(t, tot, off, *dims, parts=128):
        return bass.AP(t, off, [[tot, parts]] + [list(d) for d in dims])

    def phase_mixer_in(l, src, last):
        with ExitStack() as st:
            win1 = T(st, "win1", [128, 8, 512], BF16); wkr = T(st, "wkr", [128, 8, 128], BF16); wkrr = T(st, "wkrr", [128, 8, 128], BF16)
            wuq = T(st, "wuq", [128, 2, 1536], BF16); wuqr = T(st, "wuqr", [128, 2, 512], BF16); wuqrr = T(st, "wuqrr", [128, 2, 512], BF16)
            wukn = T(st, "wukn", [128, 2, 1024], BF16); wuv = T(st, "wuv", [128, 2, 1024], BF16)
            nrm_s = T(st, "nrm_s", [128, 4], F32)
            bw = Buf(); bn = Buf()
            S.dma(nrm_s[:, 0:2], g.q_norm.ap()[l:l + 1, :].rearrange("o (j p) -> p (o j)", p=128), writes=[bn], key=bn, slow=True)
            S.dma(nrm_s[:, 2:4], g.kv_norm.ap()[l:l + 1, :].rearrange("o (j p) -> p (o j)", p=128), writes=[bn], key=bn, slow=True)
            wl = g.WLoad(st, 512, "mi")
            for k in range(8):
                wl.load(win1[:, k, :], g.w_in.ap()[l, k * 128:(k + 1) * 128, 0:512], bw)
            stk = T(st, "stk", [128, 8, 64], F32); bstk = Buf()
            S.dma(stk[:], g.w_in.ap()[l, :, 512:576].rearrange("(k p) c -> p k c", p=128), writes=[bstk], key=bstk)
            for dup in range(2):
                S.op('dve', lambda e, dup=dup: e.tensor_copy(out=wkr[:, :, dup * 64:(dup + 1) * 64], in_=stk[:]), reads=[bstk], writes=[bw])
                for b in range(2):
                    for hf in range(2):
                        dc = dup * 64 + 32 * b + 16 * hf
                        sc_ = 32 * b + 16 * (1 - hf)
                        sgn = -1.0 if hf == 0 else 1.0
                        S.op('dve', lambda e, dc=dc, sc_=sc_, sgn=sgn: e.tensor_scalar(out=wkrr[:, :, dc:dc + 16], in0=stk[:, :, sc_:sc_ + 16], scalar1=sgn, scalar2=None, op0=ALU.mult), reads=[bstk], writes=[bw])
            stq = T(st, "stq", [128, 2048], F32); bstq = Buf()
            for j in range(2):
                S.dma(stq[:, 0:1536], g.w_uq.ap()[l, j * 128:(j + 1) * 128, :], writes=[bstq], key=bstq)
                S.op('dve', lambda e, j=j: e.tensor_scalar(out=wuq[:, j, :], in0=stq[:, 0:1536], scalar1=nrm_s[:, j:j + 1], scalar2=None, op0=ALU.mult), reads=[bstq, bn], writes=[bw])
                S.op('dve', lambda e, j=j: e.tensor_copy(out=sap(wuqr, 1024, j * 512, [64, 8], [1, 64]), in_=sap(wuq, 3072, j * 1536 + 128, [192, 8], [1, 64])), reads=[bw], writes=[bw])
                for b in range(2):
                    for hf in range(2):
                        dc = 32 * b + 16 * hf
                        sc_ = 32 * b + 16 * (1 - hf)
                        sgn = -1.0 if hf == 0 else 1.0
                        S.op('dve', lambda e, j=j, dc=dc, sc_=sc_, sgn=sgn: e.tensor_scalar(out=sap(wuqrr, 1024, j * 512 + dc, [64, 8], [1, 16]), in0=sap(wuq, 3072, j * 1536 + 128 + sc_, [192, 8], [1, 16]), scalar1=sgn, scalar2=None, op0=ALU.mult), reads=[bw], writes=[bw])
            for j in range(2):
                S.dma(stq[:, :], g.w_ukv.ap()[l, j * 128:(j + 1) * 128, :], writes=[bstq], key=bstq)
                S.op('dve', lambda e, j=j: e.tensor_scalar(out=sap(wukn, 2048, j * 1024, [128, 8], [1, 128]), in0=sap(stq, 2048, 0, [256, 8], [1, 128]), scalar1=nrm_s[:, 2 + j:3 + j], scalar2=None, op0=ALU.mult), reads=[bstq, bn], writes=[bw])
                S.op('dve', lambda e, j=j: e.tensor_scalar(out=sap(wuv, 2048, j * 1024, [128, 8], [1, 128]), in0=sap(stq, 2048, 128, [256, 8], [1, 128]), scalar1=nrm_s[:, 2 + j:3 + j], scalar2=None, op0=ALU.mult), reads=[bstq, bn], writes=[bw])

            pn = g.PreNorm(st, "mi")
            hT = [T(st, "mi_hT%d" % i, [128, 8, 512], BF16) for i in range(2)]; bhT = [Buf(), Buf()]
            cqT = T(st, "mi_cqT", [128, 2, 512], BF16); ckvT = T(st, "mi_ckvT", [128, 2, 512], BF16); blat = Buf()
            lat = [T(st, "mi_lat%d" % i, [128, 512], BF16) for i in range(2)]; blt = [Buf(), Buf()]
            lss = [T(st, "mi_lss%d" % i, [128, 8], F32) for i in range(2)]; blss = [Buf(), Buf()]
            junk = T(st, "mi_junk", [128, 256], BF16); bj = Buf()
            cos_sb = T(st, "mi_cos", [128, 512], F32); sin_sb = T(st, "mi_sin", [128, 512], F32); brope = Buf()
            t1 = [T(st, "mi_t1%d" % i, [128, 512], F32) for i in range(2)]; bt1 = [Buf(), Buf()]
            t2 = [T(st, "mi_t2%d" % i, [128, 512], F32) for i in range(2)]; bt2 = [Buf(), Buf()]
            ob = [T(st, "mi_ob%d" % i, [128, 512], BF16) for i in range(4)]; bob = [Buf() for _ in range(4)]
            vsb = [T(st, "mi_v%d" % i, [128, D], BF16) for i in range(2)]; bvs = [Buf(), Buf()]
            ps_lat = P(st, "mi_pslat", [128, 512]); bpl = Buf()
            pt2 = P(st, "mi_pt2", [128, 8, 128], BF16); bpt2 = Buf()
            psA = [P(st, "mi_psA%d" % i, [128, 512]) for i in range(2)]; bpA = [Buf(), Buf()]
            psB = [P(st, "mi_psB%d" % i, [128, 512]) for i in range(2)]; bpB = [Buf(), Buf()]
            cnt = {'a': 0, 'o': 0, 'r': 0, 'v': 0}

            def evac_copy(ps_ap, bps, dst_dram_ap, ncol):
                o = cnt['o'] % 4; cnt['o'] += 1
                eng = 'act' if cnt['o'] % 2 else 'dve'
                if eng == 'act':
                    S.op('act', lambda e: e.copy(out=ob[o][:, 0:ncol], in_=ps_ap), reads=[bps], writes=[bob[o]])
                else:
                    S.op('dve', lambda e: e.tensor_copy(out=ob[o][:, 0:ncol], in_=ps_ap), reads=[bps], writes=[bob[o]])
                S.dma(dst_dram_ap, ob[o][:, 0:ncol], reads=[bob[o]], key=bob[o])

            def evac_rope(psa, bpa, psb, bpb, dst_dram_ap, ncol):
                o = cnt['o'] % 4; cnt['o'] += 1
                r_ = cnt['r'] % 2; cnt['r'] += 1
                S.op('dve', lambda e: e.tensor_tensor(out=t1[r_][:, 0:ncol], in0=psa, in1=cos_sb[:, 0:ncol], op=ALU.mult), reads=[bpa, brope], writes=[bt1[r_]])
                S.op('dve', lambda e: e.tensor_tensor(out=t2[r_][:, 0:ncol], in0=psb, in1=sin_sb[:, 0:ncol], op=ALU.mult), reads=[bpb, brope], writes=[bt2[r_]])
                S.op('pool', lambda e: e.tensor_tensor(out=ob[o][:, 0:ncol], in0=t1[r_][:, 0:ncol], in1=t2[r_][:, 0:ncol], op=ALU.add), reads=[bt1[r_], bt2[r_]], writes=[bob[o]])
                S.dma(dst_dram_ap, ob[o][:, 0:ncol], reads=[bob[o]], key=bob[o])

            def mm_group(ps, bps, lhs_fn, rhs_fn, nk, rd):
                for k in range(nk):
                    S.op('pe', lambda e, k=k: e.matmul(ps, lhs_fn(k), rhs_fn(k), start=(k == 0), stop=(k == nk - 1)), reads=rd, writes=[bps])

            blk = 0
            for stream in [0, 1]:
                with ExitStack() as st2:
                    gsc, bgs, sh, bsh = g.load_pre(st2, l, 1, stream, "mi%d" % stream)
                    ntile, cbase = g.stream_info(stream)
                    blocks = [(b0, min(4, ntile - b0)) for b0 in range(0, ntile, 4)]
                    pend = [pn.run1(xtile_ap(src[stream], stream, blocks[0][0] + i), gsc, bgs, sh, bsh) for i in range(blocks[0][1])]
                    for bi, (b0, nt_) in enumerate(blocks):
                        ncol = nt_ * 128
                        c0 = cbase + b0 * 128
                        hb_, bhb = hT[blk % 2], bhT[blk % 2]
                        blk += 1
                        if stream == 0:
                            S.dma(cos_sb[:, 0:ncol], g.ropec.ap()[:, c0:c0 + ncol], writes=[brope], key=brope)
                            S.dma(sin_sb[:, 0:ncol], g.ropes.ap()[:, c0:c0 + ncol], writes=[brope], key=brope)
                        for i, j_ in enumerate(pend):
                            pn.run2(j_, hb_, bhb, i * 128)
                        if bi + 1 < len(blocks):
                            pend = [pn.run1(xtile_ap(src[stream], stream, blocks[bi + 1][0] + i), gsc, bgs, sh, bsh) for i in range(blocks[bi + 1][1])]
                        S.dma(g.hT.ap()[:, :, c0:c0 + ncol].rearrange("k p c -> p k c"), hb_[:, :, 0:ncol], reads=[bhb], key=bhb)
                        for i in range(nt_):
                            q = cnt['a'] % 2; cnt['a'] += 1
                            mm_group(ps_lat[:], bpl, lambda k: hb_[:, k, i * 128:(i + 1) * 128], lambda k: win1[:, k, :], 8, [bhb, bw])
                            for hf in range(2):
                                S.op('act', lambda e, hf=hf, q=q: e.activation(out=junk[:], in_=ps_lat[:, hf * 256:(hf + 1) * 256], func=AF.Square, accum_out=lss[q][:, hf:hf + 1]), reads=[bpl], writes=[bj, blss[q]])
                            g.rstd_ops(lss[q][:, 0:2], lss[q][:, 2:4], lss[q][:, 4:6], blss[q], blss[q], 256)
                            for hf in range(2):
                                S.op('dve', lambda e, hf=hf, q=q: e.tensor_scalar(out=lat[q][:, hf * 256:(hf + 1) * 256], in0=ps_lat[:, hf * 256:(hf + 1) * 256], scalar1=lss[q][:, 4 + hf:5 + hf], scalar2=None, op0=ALU.mult), reads=[bpl, blss[q]], writes=[blt[q]])
                            for j4 in range(4):
                                S.op('pe', lambda e, j4=j4, q=q: e.transpose(pt2[:, j4, :], lat[q][:, j4 * 128:(j4 + 1) * 128], g.ident[:]), reads=[blt[q], g.b_id], writes=[bpt2])
                            S.op('act', lambda e, i=i: e.copy(out=cqT[:, :, i * 128:(i + 1) * 128], in_=pt2[:, 0:2, :]), reads=[bpt2], writes=[blat])
                            S.op('act', lambda e, i=i: e.copy(out=ckvT[:, :, i * 128:(i + 1) * 128], in_=pt2[:, 2:4, :]), reads=[bpt2], writes=[blat])
                        mm_group(psA[0][:, 0:ncol], bpA[0], lambda k: wkr[:, k, :], lambda k: hb_[:, k, 0:ncol], 8, [bhb, bw])
                        if stream == 0:
                            mm_group(psB[0][:, 0:ncol], bpB[0], lambda k: wkrr[:, k, :], lambda k: hb_[:, k, 0:ncol], 8, [bhb, bw])
                            evac_rope(psA[0][:, 0:ncol], bpA[0], psB[0][:, 0:ncol], bpB[0], g.kr.ap()[:, c0:c0 + ncol], ncol)
                        else:
                            evac_copy(psA[0][:, 0:ncol], bpA[0], g.kr.ap()[:, c0:c0 + ncol], ncol)
                        if stream == 0 or not last:
                            for h in range(H):
                                q = h % 2
                                mm_group(psA[q][:, 0:ncol], bpA[q], lambda k, h=h: wuq[:, k, h * 192:h * 192 + 128], lambda k: cqT[:, k, 0:ncol], 2, [blat, bw])
                                evac_copy(psA[q][:, 0:ncol], bpA[q], g.qn.ap()[h, :, c0:c0 + ncol], ncol)
                            for pr in range(4):
                                q = pr % 2
                                mm_group(psA[q][:, 0:ncol], bpA[q], lambda k, pr=pr: wuqr[:, k, pr * 128:(pr + 1) * 128], lambda k: cqT[:, k, 0:ncol], 2, [blat, bw])
                                if stream == 0:
                                    mm_group(psB[q][:, 0:ncol], bpB[q], lambda k, pr=pr: wuqrr[:, k, pr * 128:(pr + 1) * 128], lambda k: cqT[:, k, 0:ncol], 2, [blat, bw])
                                    evac_rope(psA[q][:, 0:ncol], bpA[q], psB[q][:, 0:ncol], bpB[q], g.qr.ap()[pr, :, c0:c0 + ncol], ncol)
                                else:
                                    evac_copy(psA[q][:, 0:ncol], bpA[q], g.qr.ap()[pr, :, c0:c0 + ncol], ncol)
                        for h in range(H):
                            q = h % 2
                            mm_group(psB[q][:, 0:ncol], bpB[q], lambda k, h=h: wukn[:, k, h * 128:(h + 1) * 128], lambda k: ckvT[:, k, 0:ncol], 2, [blat, bw])
                            evac_copy(psB[q][:, 0:ncol], bpB[q], g.kn.ap()[h, :, c0:c0 + ncol], ncol)
                        for i in range(nt_):
                            vq = cnt['v'] % 2; cnt['v'] += 1
                            for hf in range(2):
                                mm_group(psA[hf][:], bpA[hf], lambda k, i=i: ckvT[:, k, i * 128:(i + 1) * 128], lambda k, hf=hf: wuv[:, k, hf * 512:(hf + 1) * 512], 2, [blat, bw])
                                if hf == 0:
                                    S.op('act', lambda e, vq=vq: e.copy(out=vsb[vq][:, 0:512], in_=psA[0][:]), reads=[bpA[0]], writes=[bvs[vq]])
                                else:
                                    S.op('dve', lambda e, vq=vq: e.tensor_copy(out=vsb[vq][:, 512:1024], in_=psA[1][:]), reads=[bpA[1]], writes=[bvs[vq]])
                            S.dma(g.v.ap()[c0 // 128 + i, :, :], vsb[vq][:], reads=[bvs[vq]], key=bvs[vq])
            S.barrier()

    def phase_attn(l, last):
        with ExitStack() as st:
            ones = T(st, "at_ones", [128, 128], F32); bones = Buf()
            S.op('pool', lambda e: e.memset(ones[:], 1.0), writes=[bones])
            accP = [T(st, "at_accP%d" % i, [128, 512], F32) for i in range(2)]; baP = [Buf(), Buf()]
            accD = [T(st, "at_accD%d" % i, [128, 512], F32) for i in range(2)]; baD = [Buf(), Buf()]
            kr_sb = T(st, "at_kr", [128, LT], BF16); bkr = Buf()
            S.dma(kr_sb[:], g.kr.ap()[:, :], writes=[bkr], key=bkr)
            kn_sb = [T(st, "at_kn%d" % i, [128, LT], BF16) for i in range(2)]; bkn = [Buf(), Buf()]
            v_sb = [T(st, "at_v%d" % i, [128, NTT, 128], BF16) for i in range(2)]; bv = [Buf(), Buf()]
            qn_sb = [T(st, "at_qn%d" % i, [128, 512], BF16) for i in range(2)]; bqn = [Buf(), Buf()]
            qr_sb = [T(st, "at_qr%d" % i, [128, 512], BF16) for i in range(2)]; bqr = [Buf(), Buf()]
            pT = [T(st, "at_pT%d" % i, [128, 512], BF16) for i in range(3)]; bpT = [Buf() for _ in range(3)]
            rden = [T(st, "at_rd%d" % i, [128, 512], F32) for i in range(2)]; brd = [Buf(), Buf()]
            osb = [T(st, "at_o%d" % i, [128, 512], BF16) for i in range(2)]; bos = [Buf(), Buf()]
            ps_s = [P(st, "at_pss%d" % i, [128, 512]) for i in range(3)]; bps = [Buf() for _ in range(3)]
            ps_o = [P(st, "at_pso%d" % i, [128, 512]) for i in range(2)]; bpo = [Buf(), Buf()]
            ps_d = [P(st, "at_psd%d" % i, [128, 512]) for i in range(2)]; bpd = [Buf(), Buf()]
            qb = 0; sc = 0
            for h in range(H):
                hq = h % 2
                hb = 64 * (h % 2)
                S.dma(kn_sb[hq][:], g.kn.ap()[h, :, :], writes=[bkn[hq]], key=bkn[hq])
                for t0 in range(0, NTT, 8):
                    t1_ = min(NTT, t0 + 8)
                    S.dma(v_sb[hq][:, t0:t1_, :], g.v.ap()[t0:t1_, :, h * 128:(h + 1) * 128].rearrange("t p c -> p t c"), writes=[bv[hq]], key=bv[hq])
                blocks = [(c0, 512, list(range(NTT))) for c0 in range(0, L, 512)]
                if not last:
                    blocks.append((L, LC, list(range(NT, NTT))))
                for (c0, ncol, kts) in blocks:
                    q2 = qb % 2; qb += 1
                    S.dma(qn_sb[q2][:, 0:ncol], g.qn.ap()[h, :, c0:c0 + ncol], writes=[bqn[q2]], key=bqn[q2])
                    S.dma(qr_sb[q2][0:64, 0:ncol], g.qr.ap()[h // 2, 64 * (h % 2):64 * (h % 2) + 64, c0:c0 + ncol], writes=[bqr[q2]], key=bqr[q2])
                    pend = None
                    nk = len(kts)

                    def pv(idx, s3, kt):
                        S.op('pe', lambda e: e.matmul(ps_o[q2][:, 0:ncol], v_sb[hq][:, kt, :], pT[s3][:, 0:ncol], start=(idx == 0), stop=(idx == nk - 1)), reads=[bv[hq], bpT[s3]], writes=[bpo[q2]])

                    for idx, kt in enumerate(kts):
                        s3 = sc % 3; sc += 1
                        S.op('pe', lambda e, kt=kt, s3=s3: e.matmul(ps_s[s3][:, 0:ncol], kn_sb[hq][:, kt * 128:(kt + 1) * 128], qn_sb[q2][:, 0:ncol], start=True, stop=False), reads=[bkn[hq], bqn[q2]], writes=[bps[s3]])
                        S.op('pe', lambda e, kt=kt, s3=s3: e.matmul(ps_s[s3][:, 0:ncol], kr_sb[0:64, kt * 128:(kt + 1) * 128], qr_sb[q2][0:64, 0:ncol], start=False, stop=True), reads=[bkr, bqr[q2]], writes=[bps[s3]])
                        S.op('act', lambda e, s3=s3: e.activation(out=pT[s3][:, 0:ncol], in_=ps_s[s3][:, 0:ncol], func=AF.Exp, scale=float(SCALE)), reads=[bps[s3]], writes=[bpT[s3]])
                        if idx % 2 == 0:
                            if idx < 2:
                                S.op('pool', lambda e, s3=s3: e.tensor_copy(out=accP[q2][:, 0:ncol], in_=pT[s3][:, 0:ncol]), reads=[bpT[s3]], writes=[baP[q2]])
                            else:
                                S.op('pool', lambda e, s3=s3: e.tensor_tensor(out=accP[q2][:, 0:ncol], in0=accP[q2][:, 0:ncol], in1=pT[s3][:, 0:ncol], op=ALU.add), reads=[bpT[s3]], writes=[baP[q2]])
                        else:
                            if idx < 2:
                                S.op('dve', lambda e, s3=s3: e.tensor_copy(out=accD[q2][:, 0:ncol], in_=pT[s3][:, 0:ncol]), reads=[bpT[s3]], writes=[baD[q2]])
                            else:
                                S.op('dve', lambda e, s3=s3: e.tensor_tensor(out=accD[q2][:, 0:ncol], in0=accD[q2][:, 0:ncol], in1=pT[s3][:, 0:ncol], op=ALU.add), reads=[bpT[s3]], writes=[baD[q2]])
                        if pend is not None:
                            pv(*pend)
                        pend = (idx, s3, kt)
                    pv(*pend)
                    S.op('pe', lambda e: e.matmul(ps_d[q2][:, 0:ncol], ones[:], accP[q2][:, 0:ncol], start=True, stop=(nk < 2)), reads=[bones, baP[q2]], writes=[bpd[q2]])
                    if nk >= 2:
                        S.op('pe', lambda e: e.matmul(ps_d[q2][:, 0:ncol], ones[:], accD[q2][:, 0:ncol], start=False, stop=True), reads=[bones, baD[q2]], writes=[bpd[q2]])
                    S.op('dve', lambda e: e.reciprocal(out=rden[q2][:, 0:ncol], in_=ps_d[q2][:, 0:ncol]), reads=[bpd[q2]], writes=[brd[q2]])
                    S.op('dve', lambda e: e.tensor_tensor(out=osb[q2][:, 0:ncol], in0=ps_o[q2][:, 0:ncol], in1=rden[q2][:, 0:ncol], op=ALU.mult), reads=[bpo[q2], brd[q2]], writes=[bos[q2]])
                    S.dma(g.oT.ap()[h, :, c0:c0 + ncol], osb[q2][:, 0:ncol], reads=[bos[q2]], key=bos[q2])
            S.barrier()

    def phase_merge(l, src, dst, last):
        with ExitStack() as st:
            wglu = T(st, "mg_glu", [128, 8, D], BF16); wos5 = T(st, "mg_os5", [128, 8, D], BF16)
            womla = T(st, "mg_omla", [128, 8, D], BF16); wout = T(st, "mg_out", [128, 8, D], BF16)
            wgate = T(st, "mg_gate", [128, 8, 2 * D], BF16)
            glub = T(st, "mg_glub", [128, 8], F32)
            bw = Buf(); bgb = Buf()
            S.dma(glub[:], g.glu_b.ap()[l:l + 1, :].rearrange("o (j p) -> p (o j)", p=128), writes=[bgb], key=bgb, slow=True)
            wl = g.WLoad(st, 1024, "mg")
            for k in range(8):
                rs = slice(k * 128, (k + 1) * 128)
                wl.load(wglu[:, k, :], g.glu_w.ap()[l, rs, :], bw)
                wl.load(wos5[:, k, :], g.w_o_s5.ap()[l, rs, :], bw)
                wl.load(womla[:, k, :], g.w_o_mla.ap()[l, rs, :], bw)
                wl.load(wout[:, k, :], g.w_out.ap()[l, rs, :], bw)
                wl.load(wgate[:, k, 0:1024], g.w_in.ap()[l, rs, 1600:2624], bw)
                wl.load(wgate[:, k, 1024:2048], g.w_in.ap()[l, rs, 2624:3648], bw)
            po = g.PostNorm(st, "mg")
            hT = [T(st, "mg_hT%d" % i, [128, 8, 512], BF16) for i in range(1)] * 2; bh = [Buf()] * 2
            zT = [T(st, "mg_zT%d" % i, [128, 8, 512], BF16) for i in range(1)] * 2; bz = [Buf()] * 2
            oT = [T(st, "mg_oT%d" % i, [128, 8, 512], BF16) for i in range(1)] * 2; bo = [Buf()] * 2
            s5o = T(st, "mg_s5o", [128, 8, 512], BF16); bs5 = [Buf() for _ in range(8)]
            mrg = T(st, "mg_mrg", [128, 8, 512], BF16); bmr = [Buf() for _ in range(8)]
            sig = [T(st, "mg_sig%d" % i, [128, 512], F32) for i in range(2)]; bsig = [Buf(), Buf()]
            ta = [T(st, "mg_ta%d" % i, [128, 512], F32) for i in range(2)]; bta = [Buf(), Buf()]
            tb = [T(st, "mg_tb%d" % i, [128, 512], F32) for i in range(2)]; btb = [Buf(), Buf()]
            psA = [P(st, "mg_psA%d" % i, [128, 512]) for i in range(2)]; bpA = [Buf(), Buf()]
            psB = [P(st, "mg_psB%d" % i, [128, 512]) for i in range(2)]; bpB = [Buf(), Buf()]
            py = [[P(st, "mg_py%d_%d" % (i, hf), [128, 512]) for hf in range(2)] for i in range(2)]
            bpy = [[Buf(), Buf()] for i in range(2)]
            blk = 0; cnt = 0; sc = 0
            for stream in ([0] if last else [0, 1]):
                with ExitStack() as st2:
                    gg, bgg = g.load_post(st2, l, 1, stream, 1.0, "mg%d" % stream)
                    ntile, cbase = g.stream_info(stream)
                    for b0 in range(0, ntile, 4):
                        nt_ = min(4, ntile - b0)
                        ncol = nt_ * 128
                        c0 = cbase + b0 * 128
                        q = blk % 2; blk += 1
                        S.dma(hT[q][:, :, 0:ncol], g.hT.ap()[:, :, c0:c0 + ncol].rearrange("k p c -> p k c"), writes=[bh[q]], key=bh[q])
                        S.dma(zT[q][:, :, 0:ncol], g.zT.ap()[:, :, c0:c0 + ncol].rearrange("k p c -> p k c"), writes=[bz[q]], key=bz[q])
                        S.dma(oT[q][:, :, 0:ncol], g.oT.ap()[:, :, c0:c0 + ncol].rearrange("k p c -> p k c"), writes=[bo[q]], key=bo[q])
                        for fo in range(8):
                            a = sc % 2; sc += 1
                            for k in range(8):
                                S.op('pe', lambda e, k=k, fo=fo, a=a: e.matmul(psA[a][:, 0:ncol], wglu[:, k, fo * 128:(fo + 1) * 128], zT[q][:, k, 0:ncol], start=(k == 0), stop=(k == 7)), reads=[bw, bz[q]], writes=[bpA[a]])
                            S.op('act', lambda e, fo=fo, a=a: e.activation(out=sig[a][:, 0:ncol], in_=psA[a][:, 0:ncol], func=AF.Sigmoid, bias=glub[:, fo:fo + 1]), reads=[bpA[a], bgb], writes=[bsig[a]])
                            S.op('dve', lambda e, fo=fo, a=a: e.tensor_tensor(out=s5o[:, fo, 0:ncol], in0=zT[q][:, fo, 0:ncol], in1=sig[a][:, 0:ncol], op=ALU.mult), reads=[bz[q], bsig[a]], writes=[bs5[fo]])
                        for do in range(8):
                            a = sc % 2; sc += 1
                            cs_ = slice(do * 128, (do + 1) * 128)
                            for k in range(8):
                                S.op('pe', lambda e, k=k, a=a, cs_=cs_: e.matmul(psA[a][:, 0:ncol], wgate[:, k, 1024 + do * 128:1024 + (do + 1) * 128], hT[q][:, k, 0:ncol], start=(k == 0), stop=(k == 7)), reads=[bw, bh[q]], writes=[bpA[a]])
                            for k in range(8):
                                S.op('pe', lambda e, k=k, a=a, cs_=cs_: e.matmul(psB[a][:, 0:ncol], wos5[:, k, cs_], s5o[:, k, 0:ncol], start=(k == 0), stop=(k == 7)), reads=[bw] + bs5, writes=[bpB[a]])
                            S.op('act', lambda e, a=a: e.activation(out=sig[a][:, 0:ncol], in_=psA[a][:, 0:ncol], func=AF.Sigmoid), reads=[bpA[a]], writes=[bsig[a]])
                            S.op('dve', lambda e, a=a: e.tensor_tensor(out=ta[a][:, 0:ncol], in0=psB[a][:, 0:ncol], in1=sig[a][:, 0:ncol], op=ALU.mult), reads=[bpB[a], bsig[a]], writes=[bta[a]])
                            a2 = sc % 2; sc += 1
                            for k in range(8):
                                S.op('pe', lambda e, k=k, a2=a2: e.matmul(psA[a2][:, 0:ncol], wgate[:, k, do * 128:(do + 1) * 128], hT[q][:, k, 0:ncol], start=(k == 0), stop=(k == 7)), reads=[bw, bh[q]], writes=[bpA[a2]])
                            for k in range(8):
                                S.op('pe', lambda e, k=k, a2=a2, cs_=cs_: e.matmul(psB[a2][:, 0:ncol], womla[:, k, cs_], oT[q][:, k, 0:ncol], start=(k == 0), stop=(k == 7)), reads=[bw, bo[q]], writes=[bpB[a2]])
                            S.op('act', lambda e, a2=a2: e.activation(out=sig[a2][:, 0:ncol], in_=psA[a2][:, 0:ncol], func=AF.Sigmoid), reads=[bpA[a2]], writes=[bsig[a2]])
                            S.op('dve', lambda e, a2=a2: e.tensor_tensor(out=tb[a2][:, 0:ncol], in0=psB[a2][:, 0:ncol], in1=sig[a2][:, 0:ncol], op=ALU.mult), reads=[bpB[a2], bsig[a2]], writes=[btb[a2]])
                            S.op('pool', lambda e, a=a, a2=a2, do=do: e.tensor_tensor(out=mrg[:, do, 0:ncol], in0=ta[a][:, 0:ncol], in1=tb[a2][:, 0:ncol], op=ALU.add), reads=[bta[a], btb[a2]], writes=[bmr[do]])
                        for i in range(nt_):
                            yq = cnt % 2; cnt += 1
                            for hf in range(2):
                                for k in range(8):
                                    S.op('pe', lambda e, k=k, hf=hf, i=i, yq=yq: e.matmul(py[yq][hf][:], mrg[:, k, i * 128:(i + 1) * 128], wout[:, k, hf * 512:(hf + 1) * 512], start=(k == 0), stop=(k == 7)), reads=[bw] + bmr, writes=[bpy[yq][hf]])
                            po.run(xtile_ap(src[stream], stream, b0 + i), xtile_ap(dst[stream], stream, b0 + i), py[yq], bpy[yq], gg, bgg)
            S.barrier()

    g.sap = sap
    phase_s5 = build_s5(g)

    def run(l, src, dst, last):
        ph = g.dbg.get("mix", "iasm") if g.dbg else "iasm"
        if "i" in ph:
            phase_mixer_in(l, src, last)
        if "a" in ph:
            phase_attn(l, last)
        if "s" in ph:
            phase_s5(l, last)
        if "m" in ph:
            phase_merge(l, src, dst, last)
    return run


def build_s5(g):
    T, P, S, AP, sap = g.T, g.P, g.S, g.AP, g.sap
    L, NT, LT = g.L, g.NT, g.LT
    NKB = L // 1024
    NCH = L // 8
    NCC = LC // 8
    NS = NCH + NCC
    NSTEP = int(math.ceil(math.log2(NS)))
    HALF_PI = math.pi / 2

    def phase_s5(l, last):
        with ExitStack() as st:
            Are = T(st, "s5_are", [128, 64], F32); Aim = T(st, "s5_aim", [128, 64], F32); Dl = T(st, "s5_dl", [128, 64], F32)
            bA = Buf()
            for d in range(2):
                for h in range(2):
                    off = ((l * 2 + d) * 64 + 32 * h) * 64
                    S.dma(Are[64 * h:64 * h + 64, d * 32:(d + 1) * 32], AP(g.a_re, off, [[1, 64], [64, 32]]), writes=[bA], key=bA, slow=True)
                    S.dma(Aim[64 * h:64 * h + 64, d * 32:(d + 1) * 32], AP(g.a_im, off, [[1, 64], [64, 32]]), writes=[bA], key=bA, slow=True)
                    S.dma(Dl[64 * h:64 * h + 64, d * 32:(d + 1) * 32], AP(g.log_dt, (l * 2 + d) * 64 + 32 * h, [[0, 64], [1, 32]]), writes=[bA], key=bA, slow=True)
            S.op('act', lambda e: e.activation(out=Dl[:], in_=Dl[:], func=AF.Exp), reads=[bA], writes=[bA])
            rho = T(st, "s5_rho", [128, 64], F32); th = T(st, "s5_th", [128, 64], F32); bT = Buf()
            S.op('dve', lambda e: e.tensor_tensor(out=rho[:], in0=Are[:], in1=Dl[:], op=ALU.mult), reads=[bA], writes=[bT])
            S.op('dve', lambda e: e.tensor_tensor(out=th[:], in0=Aim[:], in1=Dl[:], op=ALU.mult), reads=[bA], writes=[bT])
            Zre = T(st, "s5_zre", [128, 128], F32); Zim = T(st, "s5_zim", [128, 128], F32)
            mg = T(st, "s5_mg", [128, 128], F32); cs = T(st, "s5_cs", [128, 128], F32); t1 = T(st, "s5_t1", [128, 128], F32); t2 = T(st, "s5_t2", [128, 128], F32)
            bZ = Buf()
            S.op('act', lambda e: e.activation(out=mg[:, 0:64], in_=rho[:], func=AF.Exp, scale=1.0 / 16), reads=[bT], writes=[bZ])
            S.op('act', lambda e: e.activation(out=mg[:, 64:128], in_=rho[:], func=AF.Exp, scale=-1.0 / 16), reads=[bT], writes=[bZ])
            S.op('dve', lambda e: e.tensor_scalar(out=t1[:, 0:64], in0=th[:], scalar1=1.0 / 16, scalar2=HALF_PI, op0=ALU.mult, op1=ALU.add), reads=[bT], writes=[bZ])
            S.op('act', lambda e: e.activation(out=cs[:, 0:64], in_=t1[:, 0:64], func=AF.Sin), reads=[bZ], writes=[bZ])
            S.op('act', lambda e: e.activation(out=cs[:, 64:128], in_=th[:], func=AF.Sin, scale=1.0 / 16), reads=[bT, bZ], writes=[bZ])
            for kind in range(2):
                S.op('dve', lambda e, kind=kind: e.tensor_tensor(out=Zre[:, kind * 64:(kind + 1) * 64], in0=mg[:, kind * 64:(kind + 1) * 64], in1=cs[:, 0:64], op=ALU.mult), reads=[bZ], writes=[bZ])
                S.op('dve', lambda e, kind=kind: e.tensor_tensor(out=Zim[:, kind * 64:(kind + 1) * 64], in0=mg[:, kind * 64:(kind + 1) * 64], in1=cs[:, 64:128], op=ALU.mult), reads=[bZ], writes=[bZ])

            def csq(ore, oim, ire, iim, rd, wr):
                S.op('dve', lambda e: e.tensor_tensor(out=t1[:, 0:ire.shape[1]], in0=ire, in1=ire, op=ALU.mult), reads=rd, writes=[bZ])
                S.op('dve', lambda e: e.tensor_tensor(out=t2[:, 0:ire.shape[1]], in0=iim, in1=iim, op=ALU.mult), reads=rd, writes=[bZ])
                S.op('dve', lambda e: e.scalar_tensor_tensor(out=oim, in0=ire, scalar=2.0, in1=iim, op0=ALU.mult, op1=ALU.mult), reads=rd + [bZ], writes=wr)
                S.op('dve', lambda e: e.tensor_tensor(out=ore, in0=t1[:, 0:ire.shape[1]], in1=t2[:, 0:ire.shape[1]], op=ALU.subtract), reads=[bZ] + rd, writes=wr)

            for _ in range(4):
                csq(Zre[:], Zim[:], Zre[:], Zim[:], [bZ], [bZ])
            PWre = T(st, "s5_pwre", [128, 9, 128], F32); PWim = T(st, "s5_pwim", [128, 9, 128], F32)
            PRre = T(st, "s5_prre", [128, 9, 128], F32); PRim = T(st, "s5_prim", [128, 9, 128], F32)
            bP = Buf()
            S.op('pool', lambda e: e.memset(PWre[:, 0, :], 1.0), writes=[bP])
            S.op('pool', lambda e: e.memset(PWim[:, 0, :], 0.0), writes=[bP])
            S.op('dve', lambda e: e.tensor_copy(out=PWre[:, 1, :], in_=Zre[:]), reads=[bZ], writes=[bP])
            S.op('dve', lambda e: e.tensor_copy(out=PWim[:, 1, :], in_=Zim[:]), reads=[bZ], writes=[bP])
            for m in range(2, 9):
                S.op('dve', lambda e, m=m: e.tensor_tensor(out=t1[:], in0=PWre[:, m - 1, :], in1=Zre[:], op=ALU.mult), reads=[bP, bZ], writes=[bZ])
                S.op('dve', lambda e, m=m: e.tensor_tensor(out=t2[:], in0=PWim[:, m - 1, :], in1=Zim[:], op=ALU.mult), reads=[bP, bZ], writes=[bZ])
                S.op('dve', lambda e, m=m: e.tensor_tensor(out=PWre[:, m, :], in0=t1[:], in1=t2[:], op=ALU.subtract), reads=[bZ], writes=[bP])
                S.op('dve', lambda e, m=m: e.tensor_tensor(out=t1[:], in0=PWre[:, m - 1, :], in1=Zim[:], op=ALU.mult), reads=[bP, bZ], writes=[bZ])
                S.op('dve', lambda e, m=m: e.tensor_tensor(out=t2[:], in0=PWim[:, m - 1, :], in1=Zre[:], op=ALU.mult), reads=[bP, bZ], writes=[bZ])
                S.op('dve', lambda e, m=m: e.tensor_tensor(out=PWim[:, m, :], in0=t1[:], in1=t2[:], op=ALU.add), reads=[bZ], writes=[bP])
            for m in range(9):
                S.op('pool', lambda e, m=m: e.tensor_copy(out=PRre[:, m, :], in_=PWre[:, 8 - m, :]), reads=[bP], writes=[bP])
                S.op('pool', lambda e, m=m: e.tensor_copy(out=PRim[:, m, :], in_=PWim[:, 8 - m, :]), reads=[bP], writes=[bP])
            KPre = T(st, "s5_kpre", [128, NSTEP, 64], F32); KPim = T(st, "s5_kpim", [128, NSTEP, 64], F32); KPnim = T(st, "s5_kpnim", [128, NSTEP, 64], F32)
            bK = Buf()
            S.op('dve', lambda e: e.tensor_copy(out=KPre[:, 0, :], in_=PWre[:, 8, 0:64]), reads=[bP], writes=[bK])
            S.op('dve', lambda e: e.tensor_copy(out=KPim[:, 0, :], in_=PWim[:, 8, 0:64]), reads=[bP], writes=[bK])
            for j in range(1, NSTEP):
                csq(KPre[:, j, :], KPim[:, j, :], KPre[:, j - 1, :], KPim[:, j - 1, :], [bK], [bK])
            S.op('dve', lambda e: e.tensor_scalar(out=KPnim[:], in0=KPim[:], scalar1=-1.0, scalar2=None, op0=ALU.mult), reads=[bK], writes=[bK])
            cre = T(st, "s5_cre", [128, 64], F32); cim = T(st, "s5_cim", [128, 64], F32); bC = Buf()
            xr = t1[:, 0:64]; den = t1[:, 64:128]; u1 = t2[:, 0:64]; u2 = t2[:, 64:128]
            S.op('dve', lambda e: e.tensor_scalar(out=xr, in0=PWre[:, 1, 0:64], scalar1=-1.0, scalar2=None, op0=ALU.add), reads=[bP], writes=[bZ])
            S.op('dve', lambda e: e.tensor_tensor(out=den, in0=Are[:], in1=Are[:], op=ALU.mult), reads=[bA], writes=[bZ])
            S.op('dve', lambda e: e.tensor_tensor(out=u1, in0=Aim[:], in1=Aim[:], op=ALU.mult), reads=[bA], writes=[bZ])
            S.op('dve', lambda e: e.tensor_tensor(out=den, in0=den, in1=u1, op=ALU.add), reads=[bZ], writes=[bZ])
            S.op('dve', lambda e: e.reciprocal(out=den, in_=den), reads=[bZ], writes=[bZ])
            S.op('dve', lambda e: e.tensor_tensor(out=u1, in0=xr, in1=Are[:], op=ALU.mult), reads=[bZ, bA], writes=[bZ])
            S.op('dve', lambda e: e.tensor_tensor(out=u2, in0=PWim[:, 1, 0:64], in1=Aim[:], op=ALU.mult), reads=[bP, bA], writes=[bZ])
            S.op('dve', lambda e: e.tensor_tensor(out=u1, in0=u1, in1=u2, op=ALU.add), reads=[bZ], writes=[bZ])
            S.op('dve', lambda e: e.tensor_tensor(out=cre[:], in0=u1, in1=den, op=ALU.mult), reads=[bZ], writes=[bC])
            S.op('dve', lambda e: e.tensor_tensor(out=u1, in0=PWim[:, 1, 0:64], in1=Are[:], op=ALU.mult), reads=[bP, bA, bC], writes=[bZ])
            S.op('dve', lambda e: e.tensor_tensor(out=u2, in0=xr, in1=Aim[:], op=ALU.mult), reads=[bZ, bA], writes=[bZ])
            S.op('dve', lambda e: e.tensor_tensor(out=u1, in0=u1, in1=u2, op=ALU.subtract), reads=[bZ], writes=[bZ])
            S.op('dve', lambda e: e.tensor_tensor(out=cim[:], in0=u1, in1=den, op=ALU.mult), reads=[bZ], writes=[bC])
            Bre = T(st, "s5_bre", [128, 1024], F32); Bim = T(st, "s5_bim", [128, 1024], F32); bB = Buf()
            Bbre = T(st, "s5_bbre", [128, 1024], F32); Bbim = T(st, "s5_bbim", [128, 1024], F32); bBb = Buf()
            tb1 = T(st, "s5_tb1", [128, 1024], F32); tb2 = T(st, "s5_tb2", [128, 1024], F32); bt = Buf()
            for d in range(2):
                for h in range(2):
                    off = ((l * 2 + d) * 64 + 32 * h) * 1024
                    S.dma(sap(Bre, 1024, 64 * h * 1024 + d * 512, [16, 32], [1, 16], parts=64), AP(g.b_re, off, [[16, 64], [1024, 32], [1, 16]]), writes=[bB], key=bB)
                    S.dma(sap(Bim, 1024, 64 * h * 1024 + d * 512, [16, 32], [1, 16], parts=64), AP(g.b_im, off, [[16, 64], [1024, 32], [1, 16]]), writes=[bB], key=bB)
            cre_b = sap(cre, 64, 0, [1, 64], [0, 16]); cim_b = sap(cim, 64, 0, [1, 64], [0, 16])
            B3 = lambda t_: sap(t_, 1024, 0, [16, 64], [1, 16])
            S.op('dve', lambda e: e.tensor_tensor(out=B3(tb1), in0=B3(Bre), in1=cre_b, op=ALU.mult), reads=[bB, bC], writes=[bt])
            S.op('dve', lambda e: e.tensor_tensor(out=B3(tb2), in0=B3(Bim), in1=cim_b, op=ALU.mult), reads=[bB, bC], writes=[bt])
            S.op('dve', lambda e: e.tensor_tensor(out=Bbre[:], in0=tb1[:], in1=tb2[:], op=ALU.subtract), reads=[bt], writes=[bBb])
            S.op('dve', lambda e: e.tensor_tensor(out=B3(tb1), in0=B3(Bim), in1=cre_b, op=ALU.mult), reads=[bB, bC, bBb], writes=[bt])
            S.op('dve', lambda e: e.tensor_tensor(out=B3(tb2), in0=B3(Bre), in1=cim_b, op=ALU.mult), reads=[bB, bC], writes=[bt])
            S.op('dve', lambda e: e.tensor_tensor(out=Bbim[:], in0=tb1[:], in1=tb2[:], op=ALU.add), reads=[bt], writes=[bBb])
            Cre, Cim = Bre, Bim
            craw = [T(st, "s5_craw%d" % i, [128, 128], F32) for i in range(2)]; bcr = [Buf(), Buf()]
            psC_full = P(st, "s5_psC", [128, 512]); psC = psC_full[:, 0:128]; bpc = Buf()
            ci = 0
            for d in range(2):
                for (src_t, dstC) in ((g.c_re, Cre), (g.c_im, Cim)):
                    for pb in range(4):
                        q = ci % 2; ci += 1
                        S.dma(sap(craw[q], 128, 0, [64, 2], [1, 64]), AP(src_t, ((l * 2 + d) * 64 + 8 * pb) * 1024, [[64, 128], [32 * 1024, 2], [1, 64]]), writes=[bcr[q]], key=bcr[q])
                        S.op('pe', lambda e, q=q: e.matmul(psC, craw[q][:], g.identf_sb[:], start=True, stop=True), reads=[bcr[q], g.b_id], writes=[bpc])
                        S.op('act', lambda e, d=d, pb=pb, dstC=dstC: e.copy(out=dstC[:, d * 512 + pb * 128: d * 512 + (pb + 1) * 128], in_=psC), reads=[bpc, bBb], writes=[bB])
            dcol = T(st, "s5_dcol", [128, 64], F32); bD = Buf()
            for tau in range(8):
                S.dma(dcol[16 * tau:16 * tau + 16, :], AP(g.s5_d, l * 1024, [[1, 16], [16, 64]]), writes=[bD], key=bD, slow=True)
            mkl = T(st, "s5_mkl", [128, 128], F32); mku = T(st, "s5_mku", [128, 128], F32); bM = Buf()
            S.dma(mkl[:], g.maskl.ap()[:, :], writes=[bM], key=bM)
            S.dma(mku[:], g.masku.ap()[:, :], writes=[bM], key=bM)

            U = T(st, "s5_U", [128, 32, NS], BF16); bU = [Buf() for _ in range(32)]
            zsb = T(st, "s5_z", [128, NKB + 1, 8, 256], BF16); bz = Buf()
            usb = T(st, "s5_usb", [128, 32, 8, 16], BF16); bus = Buf()
            winu = T(st, "s5_winu", [128, 8, 512], BF16); bwu = Buf()
            hblk = [T(st, "s5_hb%d" % i, [128, 8, 512], BF16) for i in range(1)] * 2; bhb = [Buf()] * 2
            fam = {}
            for nm in ("L", "LT", "R"):
                for d in range(2):
                    for ri in range(2):
                        fam[(nm, d, ri)] = T(st, "s5_f%s%d%d" % (nm, d, ri), [128, 8, 128], BF16)
            bfam = Buf()
            ftmp = [tb1, tb2]; bft = bt
            WS = {}; bWS = Buf()
            for d in range(2):
                for ri in range(2):
                    for h in range(2):
                        WS[(d, ri, h)] = T(st, "s5_ws%d%d%d" % (d, ri, h), [128, 128], BF16)
                        S.op('pool', lambda e, d=d, ri=ri, h=h: e.memset(WS[(d, ri, h)][:], 0.0), writes=[bWS])
            PAD = {}; bPAD = Buf()
            for nm in ("LT", "R"):
                for d in range(2):
                    for ri in range(2):
                        for h in range(2):
                            PAD[(nm, d, ri, h)] = T(st, "s5_pd%s%d%d%d" % (nm, d, ri, h), [128, 128], BF16)
                            S.op('pool', lambda e, k_=(nm, d, ri, h): e.memset(PAD[k_][:], 0.0), writes=[bPAD])
            toep = [T(st, "s5_tp%d" % h, [128, 128], BF16) for h in range(2)]; btp = [Buf(), Buf()]
            tpf = [T(st, "s5_tpf%d" % i, [128, 128], F32) for i in range(2)]; btf = Buf()
            XA = {}; XB = {}; bX = {}
            for d in range(2):
                for ri in range(2):
                    XA[(d, ri)] = T(st, "s5_xa%d%d" % (d, ri), [128, NS], F32)
                    XB[(d, ri)] = T(st, "s5_xb%d%d" % (d, ri), [128, NS], F32)
                    bX[(d, ri, 0)] = Buf(); bX[(d, ri, 1)] = Buf()
            Xf = {}; bXf = Buf()
            for d in range(2):
                for ri in range(2):
                    Xf[(d, ri)] = T(st, "s5_xf%d%d" % (d, ri), [128, NS + 1], BF16)
                    S.op('pool', lambda e, d=d, ri=ri: e.memset(Xf[(d, ri)][:], 0.0), writes=[bXf])
            gx = T(st, "s5_gx", [128, NKB * 128], F32); gt = T(st, "s5_gt", [128, NKB * 128], F32); gsg = T(st, "s5_gsg", [128, NKB * 128], F32); bg_ = Buf()
            gxc = T(st, "s5_gxc", [32, 128], F32); gtc = T(st, "s5_gtc", [32, 128], F32); gsc = T(st, "s5_gsc", [32, 128], F32); bgc = Buf()
            zo = [T(st, "s5_zo%d" % i, [128, 1024], BF16) for i in range(2)]; bzo = [Buf(), Buf()]
            zc = T(st, "s5_zc", [128, 256], BF16); bzc = Buf()
            ps_u = P(st, "s5_psu", [128, 512]); bpu = Buf()
            ps_t = [P(st, "s5_pst%d" % i, [128, 8, 128], BF16) for i in range(2)]; bpt = [Buf(), Buf()]
            ps_S = [P(st, "s5_psS%d" % i, [128, 512]) for i in range(2)]; bpS = [Buf(), Buf()]
            ps_TS = P(st, "s5_psTS", [128, 512]); bpSc = Buf(); bpT = bpSc
            ps_y = P(st, "s5_psy", [128, 4, 128]); bpy = Buf()
            ps_yc = psC_full
            wl = g.WLoad(st, 256, "s5", n=2)

            def cprod(out_re, out_im, pre, pim, xre, xim, conj, negim):
                a, b = ftmp[0], ftmp[1]
                A4 = sap(a, 1024, 0, [128, 8], [16, 8], [1, 16]); B4 = sap(b, 1024, 0, [128, 8], [16, 8], [1, 16])
                S.op('dve', lambda e: e.tensor_tensor(out=A4, in0=xre, in1=pre, op=ALU.mult), reads=[bP, bB, bBb], writes=[bft])
                S.op('dve', lambda e: e.tensor_tensor(out=B4, in0=xim, in1=pim, op=ALU.mult), reads=[bP, bB, bBb], writes=[bft])
                S.op('dve', lambda e: e.tensor_tensor(out=out_re, in0=A4, in1=B4, op=(ALU.add if conj else ALU.subtract)), reads=[bft], writes=[bfam])
                S.op('dve', lambda e: e.tensor_tensor(out=A4, in0=xim, in1=pre, op=ALU.mult), reads=[bP, bB, bBb, bfam], writes=[bft])
                S.op('dve', lambda e: e.tensor_tensor(out=B4, in0=xre, in1=pim, op=ALU.mult), reads=[bP, bB, bBb], writes=[bft])
                if not negim:
                    S.op('dve', lambda e: e.tensor_tensor(out=out_im, in0=A4, in1=B4, op=(ALU.subtract if conj else ALU.add)), reads=[bft], writes=[bfam])
                else:
                    if conj:
                        S.op('dve', lambda e: e.tensor_tensor(out=out_im, in0=B4, in1=A4, op=ALU.subtract), reads=[bft], writes=[bfam])
                    else:
                        S.op('dve', lambda e: e.scalar_tensor_tensor(out=out_im, in0=A4, scalar=-1.0, in1=B4, op0=ALU.mult, op1=ALU.subtract), reads=[bft], writes=[bfam])

            def ptab(tre, tim, m0, kind, d, p0):
                off = m0 * 128 + kind * 64 + d * 32 + p0
                return sap(tre, 9 * 128, off, [1, 8], [128, 8], [0, 16]), sap(tim, 9 * 128, off, [1, 8], [128, 8], [0, 16])

            def xtab(tre, tim, d, p0):
                off = d * 512 + p0 * 16
                return sap(tre, 1024, off, [16, 8], [0, 8], [1, 16]), sap(tim, 1024, off, [16, 8], [0, 8], [1, 16])

            F4 = lambda t_: sap(t_, 1024, 0, [128, 8], [16, 8], [1, 16])

            zi = 0
            s5stage = (g.dbg or {}).get("s5stage", 9)
            for hp in range(2 if s5stage >= 1 else 0):
                for k in range(8):
                    wl.load(winu[:, k, 0:256], g.w_in.ap()[l, k * 128:(k + 1) * 128, 576 + 256 * hp:576 + 256 * hp + 256], bwu)
                    wl.load(winu[:, k, 256:512], g.w_in.ap()[l, k * 128:(k + 1) * 128, 576 + 512 + 256 * hp:576 + 512 + 256 * hp + 256], bwu)
                nb = 0
                blocks = [(0, kb, half) for kb in range(NKB) for half in range(2)] + [(1, 0, 0)]
                for (stream, kb, half) in blocks:
                    hb_, bh_ = hblk[nb % 2], bhb[nb % 2]; nb += 1
                    if stream == 0:
                        c0 = kb * 1024 + half * 512
                        S.dma(hb_[:], g.hT.ap()[:, :, c0:c0 + 512].rearrange("k p c -> p k c"), writes=[bh_], key=bh_)
                        for t4 in range(4):
                            tau = half * 4 + t4
                            for k in range(8):
                                S.op('pe', lambda e, k=k, t4=t4: e.matmul(ps_u[:], hb_[:, k, t4 * 128:(t4 + 1) * 128], winu[:, k, :], start=(k == 0), stop=(k == 7)), reads=[bh_, bwu], writes=[bpu])
                            S.op('act', lambda e, tau=tau: e.copy(out=sap(usb, 4096, tau * 16, [128, 32], [1, 16]), in_=sap(ps_u, 512, 0, [16, 32], [1, 16])), reads=[bpu], writes=[bus])
                        if half == 1:
                            for g8 in range(4):
                                q = g8 % 2
                                for gi in range(8):
                                    gl = g8 * 8 + gi
                                    S.op('pe', lambda e, gl=gl, gi=gi, q=q: e.transpose(ps_t[q][:, gi, :], sap(usb, 4096, gl * 128, [1, 128]), g.ident[:]), reads=[bus, g.b_id], writes=[bpt[q]])
                                S.op('dve', lambda e, g8=g8, q=q, kb=kb: e.tensor_copy(out=U[:, g8 * 8:(g8 + 1) * 8, kb * 128:(kb + 1) * 128], in_=ps_t[q][:]), reads=[bpt[q]], writes=bU[g8 * 8:(g8 + 1) * 8])
                    else:
                        S.dma(hb_[:, :, 0:256], g.hT.ap()[:, :, L:L + 256].rearrange("k p c -> p k c"), writes=[bh_], key=bh_)
                        for tau in range(8):
                            for k in range(8):
                                S.op('pe', lambda e, k=k, tau=tau: e.matmul(ps_u[0:32, :], sap(hb_, 4096, k * 512 + tau, [8, 32]), winu[:, k, :], start=(k == 0), stop=(k == 7)), reads=[bh_, bwu], writes=[bpu])
                            S.op('act', lambda e, tau=tau: e.copy(out=sap(usb, 4096, tau * 16, [128, 32], [1, 16], parts=32), in_=sap(ps_u, 512, 0, [16, 32], [1, 16], parts=32)), reads=[bpu], writes=[bus])
                        for g8 in range(4):
                            q = g8 % 2
                            for gi in range(8):
                                gl = g8 * 8 + gi
                                S.op('pe', lambda e, gl=gl, gi=gi, q=q: e.transpose(ps_t[q][:, gi, 0:32], sap(usb, 4096, gl * 128, [1, 128], parts=32), g.ident[0:32, 0:32]), reads=[bus, g.b_id], writes=[bpt[q]])
                            S.op('dve', lambda e, g8=g8, q=q: e.tensor_copy(out=U[:, g8 * 8:(g8 + 1) * 8, NCH:NCH + NCC], in_=ps_t[q][:, :, 0:32]), reads=[bpt[q]], writes=bU[g8 * 8:(g8 + 1) * 8])

                for qp in range(2 if s5stage >= 2 else 0):
                    p0 = 16 * hp + 8 * qp
                    for d in range(2):
                        bre_, bim_ = xtab(Bbre, Bbim, d, p0)
                        cre_, cim_ = xtab(Cre, Cim, d, p0)
                        if d == 0:
                            pL = ptab(PRre, PRim, 1, 0, d, p0)
                            pLT = ptab(PWre, PWim, 1, 1, d, p0)
                            pR = ptab(PWre, PWim, 1, 0, d, p0)
                        else:
                            pL = ptab(PWre, PWim, 0, 0, d, p0)
                            pLT = ptab(PRre, PRim, 0, 1, d, p0)
                            pR = ptab(PRre, PRim, 0, 0, d, p0)
                        cprod(F4(fam[("L", d, 0)]), F4(fam[("L", d, 1)]), pL[0], pL[1], bre_, bim_, False, False)
                        cprod(F4(fam[("LT", d, 0)]), F4(fam[("LT", d, 1)]), pLT[0], pLT[1], bre_, bim_, True, False)
                        cprod(F4(fam[("R", d, 0)]), F4(fam[("R", d, 1)]), pR[0], pR[1], cre_, cim_, False, True)
                    for i in range(8 if s5stage >= 3 else 0):
                        p = p0 + i
                        pl = 8 * qp + i
                        for d in range(2):
                            for ri in range(2):
                                S.op('pe', lambda e, d=d, ri=ri, i=i: e.transpose(ps_t[0][:, d * 2 + ri, :], fam[("L", d, ri)][:, i, :], g.ident[:]), reads=[bfam, g.b_id], writes=[bpt[0]])
                        for d in range(2):
                            for ri in range(2):
                                for h in range(2):
                                    S.op('act', lambda e, d=d, ri=ri, h=h: e.copy(out=WS[(d, ri, h)][:, 64 * h:64 * h + 64], in_=ps_t[0][:, d * 2 + ri, 64 * h:64 * h + 64]), reads=[bpt[0]], writes=[bWS])
                                    for nm in ("LT", "R"):
                                        S.op('pool', lambda e, d=d, ri=ri, h=h, nm=nm, i=i: e.tensor_copy(out=PAD[(nm, d, ri, h)][64 * h:64 * h + 64, :], in_=fam[(nm, d, ri)][64 * h:64 * h + 64, i, :]), reads=[bfam], writes=[bPAD])
                        s5sub = (g.dbg or {}).get("s5sub", 63)
                        for d in range(2 if s5sub & 1 else 0):
                            for ri in range(2):
                                q = ri
                                for h in range(2):
                                    gl = h * 16 + pl
                                    S.op('pe', lambda e, d=d, ri=ri, h=h, gl=gl, q=q: e.matmul(ps_S[q][:, 0:NCH], WS[(d, ri, h)][:], U[:, gl, 0:NCH], start=(h == 0), stop=(h == 1)), reads=[bWS, bU[gl]], writes=[bpS[q]])
                                for h in range(2):
                                    gl = h * 16 + pl
                                    S.op('pe', lambda e, d=d, ri=ri, h=h, gl=gl: e.matmul(ps_TS[:, 256 + (d * 2 + ri) * 32:256 + (d * 2 + ri + 1) * 32], WS[(d, ri, h)][:], U[:, gl, NCH:NS], start=(h == 0), stop=(h == 1)), reads=[bWS, bU[gl]], writes=[bpSc])
                                xoff = NCC if d == 0 else 0
                                coff = 0 if d == 0 else NCH
                                S.op('act', lambda e, d=d, ri=ri, q=q, xoff=xoff: e.copy(out=XA[(d, ri)][:, xoff:xoff + NCH], in_=ps_S[q][:, 0:NCH]), reads=[bpS[q]], writes=[bX[(d, ri, 0)]])
                                S.op('act', lambda e, d=d, ri=ri, coff=coff: e.copy(out=XA[(d, ri)][:, coff:coff + NCC], in_=ps_TS[:, 256 + (d * 2 + ri) * 32:256 + (d * 2 + ri + 1) * 32]), reads=[bpSc], writes=[bX[(d, ri, 0)]])
                        for d in range(2 if s5sub & 2 else 0):
                            cur = 0
                            for j in range(NSTEP):
                                sft = 1 << j
                                if sft >= NS:
                                    break
                                src = (XA, XB)[cur]; dstt = (XB, XA)[cur]
                                ar = KPre[:, j, d * 32 + p:d * 32 + p + 1]; ai = KPim[:, j, d * 32 + p:d * 32 + p + 1]; nai = KPnim[:, j, d * 32 + p:d * 32 + p + 1]
                                if d == 0:
                                    lo, hi = slice(sft, NS), slice(0, NS - sft)
                                    keep = slice(0, sft)
                                else:
                                    lo, hi = slice(0, NS - sft), slice(sft, NS)
                                    keep = slice(NS - sft, NS)
                                rds = [bX[(d, 0, cur)], bX[(d, 1, cur)], bK]
                                S.op('dve', lambda e, src=src, dstt=dstt, lo=lo, hi=hi, ar=ar: e.scalar_tensor_tensor(out=dstt[(d, 0)][:, lo], in0=src[(d, 0)][:, hi], scalar=ar, in1=src[(d, 0)][:, lo], op0=ALU.mult, op1=ALU.add), reads=rds, writes=[bX[(d, 0, 1 - cur)]])
                                S.op('dve', lambda e, src=src, dstt=dstt, lo=lo, hi=hi, nai=nai: e.scalar_tensor_tensor(out=dstt[(d, 0)][:, lo], in0=src[(d, 1)][:, hi], scalar=nai, in1=dstt[(d, 0)][:, lo], op0=ALU.mult, op1=ALU.add), reads=rds, writes=[bX[(d, 0, 1 - cur)]])
                                S.op('dve', lambda e, src=src, dstt=dstt, lo=lo, hi=hi, ai=ai: e.scalar_tensor_tensor(out=dstt[(d, 1)][:, lo], in0=src[(d, 0)][:, hi], scalar=ai, in1=src[(d, 1)][:, lo], op0=ALU.mult, op1=ALU.add), reads=rds, writes=[bX[(d, 1, 1 - cur)]])
                                S.op('dve', lambda e, src=src, dstt=dstt, lo=lo, hi=hi, ar=ar: e.scalar_tensor_tensor(out=dstt[(d, 1)][:, lo], in0=src[(d, 1)][:, hi], scalar=ar, in1=dstt[(d, 1)][:, lo], op0=ALU.mult, op1=ALU.add), reads=rds, writes=[bX[(d, 1, 1 - cur)]])
                                for ri in range(2):
                                    S.op('act', lambda e, src=src, dstt=dstt, keep=keep, ri=ri: e.copy(out=dstt[(d, ri)][:, keep], in_=src[(d, ri)][:, keep]), reads=[bX[(d, ri, cur)]], writes=[bX[(d, ri, 1 - cur)]])
                                cur = 1 - cur
                            fin = (XA, XB)[cur]
                            for ri in range(2):
                                doff = 1 if d == 0 else 0
                                S.op('act', lambda e, ri=ri, fin=fin, doff=doff: e.copy(out=Xf[(d, ri)][:, doff:doff + NS], in_=fin[(d, ri)][:]), reads=[bX[(d, ri, cur)]], writes=[bXf])
                        for h in range(2 if s5sub & 4 else 0):
                            gl = h * 16 + pl
                            gglob = 32 * h + p
                            for d in range(2):
                                S.op('pe', lambda e, d=d, h=h: e.matmul(ps_TS[:, d * 128:(d + 1) * 128], PAD[("LT", d, 0, h)][:], PAD[("R", d, 0, h)][:], start=True, stop=False), reads=[bPAD], writes=[bpT])
                                S.op('pe', lambda e, d=d, h=h: e.matmul(ps_TS[:, d * 128:(d + 1) * 128], PAD[("LT", d, 1, h)][:], PAD[("R", d, 1, h)][:], start=False, stop=True), reads=[bPAD], writes=[bpT])
                            S.op('dve', lambda e: e.tensor_tensor(out=tpf[0][:], in0=ps_TS[:, 0:128], in1=mkl[:], op=ALU.mult), reads=[bpT, bM], writes=[btf])
                            S.op('dve', lambda e: e.tensor_tensor(out=tpf[1][:], in0=ps_TS[:, 128:256], in1=mku[:], op=ALU.mult), reads=[bpT, bM], writes=[btf])
                            S.op('dve', lambda e: e.tensor_tensor(out=tpf[0][:], in0=tpf[0][:], in1=tpf[1][:], op=ALU.add), reads=[btf], writes=[btf])
                            S.op('dve', lambda e, h=h, gglob=gglob: e.scalar_tensor_tensor(out=toep[h][:], in0=g.identf_sb[:], scalar=dcol[:, gglob:gglob + 1], in1=tpf[0][:], op0=ALU.mult, op1=ALU.add), reads=[btf, bD, g.b_id], writes=[btp[h]])
                            for kb in range(NKB if s5sub & 8 else 0):
                                mm = [(U[:, gl, kb * 128:(kb + 1) * 128], toep[h][:], [bU[gl], btp[h]])]
                                for ri in range(2):
                                    mm.append((Xf[(0, ri)][:, NCC + kb * 128:NCC + (kb + 1) * 128], PAD[("R", 0, ri, h)][:], [bXf, bPAD]))
                                    mm.append((Xf[(1, ri)][:, kb * 128 + 1:(kb + 1) * 128 + 1], PAD[("R", 1, ri, h)][:], [bXf, bPAD]))
                                for mi, (lh, rh, rd) in enumerate(mm):
                                    S.op('pe', lambda e, lh=lh, rh=rh, mi=mi, kb=kb: e.matmul(ps_y[:, kb, :], lh, rh, start=(mi == 0), stop=(mi == len(mm) - 1)), reads=rd, writes=[bpy])
                            mm = [(U[:, gl, NCH:NS], toep[h][:], [bU[gl], btp[h]])]
                            for ri in range(2):
                                mm.append((Xf[(0, ri)][:, 0:NCC], PAD[("R", 0, ri, h)][:], [bXf, bPAD]))
                                mm.append((Xf[(1, ri)][:, NCH + 1:NS + 1], PAD[("R", 1, ri, h)][:], [bXf, bPAD]))
                            for mi, (lh, rh, rd) in enumerate(mm if s5sub & 16 else []):
                                S.op('pe', lambda e, lh=lh, rh=rh, mi=mi: e.matmul(ps_yc[0:32, 0:128], lh, rh, start=(mi == 0), stop=(mi == len(mm) - 1)), reads=rd, writes=[bpc])
                            for (ps_ap, x_, t_, s_, bb, zdst, bps_) in (
                                (sap(ps_y, 512, 0, [1, NKB * 128]), gx[:], gt[:], gsg[:], bg_, sap(zsb, (NKB + 1) * 2048, h * 128 + i * 16, [2048, NKB], [256, 8], [1, 16]), bpy),
                                (sap(ps_yc, 512, 0, [1, 128], parts=32), gxc[:], gtc[:], gsc[:], bgc, sap(zsb, (NKB + 1) * 2048, NKB * 2048 + h * 128 + i * 16, [256, 8], [1, 16], parts=32), bpc))[0:(2 if s5sub & 32 else 0)]:
                                glen = (g.dbg or {}).get("glen", 9)
                                S.op('act', lambda e: e.copy(out=x_, in_=ps_ap), reads=[bps_], writes=[bb])
                                if g.dbg and "dbg_gx" in g.dbg and bb is bg_:
                                    S.dma(g.dbg_gx.ap()[gl + 32 * hp, :, :], gx[:], reads=[bb], key=bb)
                                if glen < 2: continue
                                S.op('dve', lambda e: e.tensor_tensor(out=t_, in0=x_, in1=x_, op=ALU.mult), reads=[bb], writes=[bb])
                                if glen < 3: continue
                                S.op('dve', lambda e: e.tensor_scalar(out=t_, in0=t_, scalar1=0.044715, scalar2=1.0, op0=ALU.mult, op1=ALU.add), reads=[bb], writes=[bb])
                                S.op('dve', lambda e: e.tensor_tensor(out=t_, in0=t_, in1=x_, op=ALU.mult), reads=[bb], writes=[bb])
                                S.op('dve', lambda e: e.tensor_scalar(out=t_, in0=t_, scalar1=1.5957691216057308, scalar2=30.0, op0=ALU.mult, op1=ALU.min), reads=[bb], writes=[bb])
                                S.op('dve', lambda e: e.tensor_scalar(out=t_, in0=t_, scalar1=-30.0, scalar2=None, op0=ALU.max), reads=[bb], writes=[bb])
                                if glen < 4: continue
                                S.op('act', lambda e: e.activation(out=s_, in_=t_, func=AF.Exp, scale=-1.0), reads=[bb], writes=[bb])
                                S.op('dve', lambda e: e.tensor_scalar(out=s_, in0=s_, scalar1=1.0, scalar2=None, op0=ALU.add), reads=[bb], writes=[bb])
                                S.op('dve', lambda e: e.reciprocal(out=s_, in_=s_), reads=[bb], writes=[bb])
                                if glen < 5: continue
                                if bb is bg_:
                                    for kb in range(NKB):
                                        S.op('dve', lambda e, kb=kb: e.tensor_tensor(out=sap(zsb, (NKB + 1) * 2048, kb * 2048 + h * 128 + i * 16, [256, 8], [1, 16]), in0=sap(gx, NKB * 128, kb * 128, [16, 8], [1, 16]), in1=sap(gsg, NKB * 128, kb * 128, [16, 8], [1, 16]), op=ALU.mult), reads=[bb], writes=[bz])
                                else:
                                    S.op('dve', lambda e: e.tensor_tensor(out=zdst, in0=sap(gxc, 128, 0, [16, 8], [1, 16], parts=32), in1=sap(gsc, 128, 0, [16, 8], [1, 16], parts=32), op=ALU.mult), reads=[bb], writes=[bz])
                    for jf in range(2):
                        chunk = 2 * hp + qp + 4 * jf
                        for kb in range(NKB):
                            q = zi % 2; zi += 1
                            for tau in range(8):
                                S.op('pe', lambda e, tau=tau, kb=kb, jf=jf, q=q: e.transpose(ps_t[q][:, tau, :], zsb[:, kb, tau, jf * 128:(jf + 1) * 128], g.ident[:]), reads=[bz, g.b_id], writes=[bpt[q]])
                            S.op('act', lambda e, q=q: e.copy(out=zo[q][:], in_=ps_t[q][:]), reads=[bpt[q]], writes=[bzo[q]])
                            S.dma(g.zT.ap()[chunk, :, kb * 1024:(kb + 1) * 1024], zo[q][:], reads=[bzo[q]], key=bzo[q])
                        q = zi % 2; zi += 1
                        for tau in range(8):
                            S.op('pe', lambda e, tau=tau, jf=jf, q=q: e.transpose(ps_t[q][:, tau, 0:32], zsb[0:32, NKB, tau, jf * 128:(jf + 1) * 128], g.ident[0:32, 0:32]), reads=[bz, g.b_id], writes=[bpt[q]])
                        S.op('act', lambda e, q=q: e.copy(out=sap(zc, 256, 0, [1, 8], [8, 32]), in_=ps_t[q][:, :, 0:32]), reads=[bpt[q]], writes=[bzc])
                        S.dma(g.zT.ap()[chunk, :, L:L + 256], zc[:], reads=[bzc], key=bzc)
            S.barrier()
    return phase_s5


def rope_tables(L):
    t = np.arange(L)
    row = (t // 64).astype(np.float32)
    col = (t % 64).astype(np.float32)
    nf = 16
    inv = (np.float32(10000.0) ** (-np.arange(nf, dtype=np.float32) / np.float32(nf))).astype(np.float32)
    ar = row[:, None] * inv
    ac = col[:, None] * inv
    ang = np.concatenate([ar, ar, ac, ac], axis=-1).astype(np.float32)
    cos = np.cos(ang).astype(np.float32)
    sin = np.sin(ang).astype(np.float32)
    idx = np.arange(L).reshape(L // 1024, 8, 128)
    kb, tau, p = np.meshgrid(np.arange(L // 1024), np.arange(8), np.arange(128), indexing='ij')
    tok = (1024 * kb + 8 * p + tau).reshape(-1)
    cosT = np.ascontiguousarray(cos[tok].T)
    sinT = np.ascontiguousarray(sin[tok].T)
    return np.concatenate([cosT, cosT], 0), np.concatenate([sinT, sinT], 0)


_CACHE = {}


def const_inputs(L):
    cosT, sinT = rope_tables(L)
    tri = np.kron(np.triu(np.ones((8, 8), np.float32)), np.ones((16, 16), np.float32))
    return {"rope_cos": cosT, "rope_sin": sinT, "identf": np.eye(128, dtype=np.float32),
            "maskl": tri, "masku": np.ascontiguousarray(tri.T)}


def kernel(**inputs):
    L = inputs["x"].shape[1]
    depth = inputs["ada_w"].shape[0]
    B = inputs["x"].shape[0]
    key = (L, depth)
    if key not in _CACHE:
        _CACHE[key] = build_program(L, depth)
    nc, g = _CACHE[key]
    consts = const_inputs(L)
    shared = {k: np.ascontiguousarray(v) for k, v in inputs.items() if k not in ("x", "c", "ctx", "c_ctx")}
    shared["c_ctx"] = np.ascontiguousarray(inputs["c_ctx"].reshape(1, D))
    shared.update(consts)
    in_maps = []
    for b in range(B):
        m = dict(shared)
        m["x"] = np.ascontiguousarray(inputs["x"][b])
        m["c"] = np.ascontiguousarray(inputs["c"][b:b + 1])
        m["ctx"] = np.ascontiguousarray(inputs["ctx"][b])
        in_maps.append(m)
    res = run_bass_kernel_spmd(nc, in_maps, core_ids=list(range(B)))
    return np.stack([r["out"] for r in res.results], axis=0)
```

```python
import math
from contextlib import ExitStack
import numpy as np
import ml_dtypes
import concourse.bass as bass
import concourse.mybir as mybir
from concourse.bass_utils import run_bass_kernel_spmd

F32 = mybir.dt.float32
BF16 = mybir.dt.bfloat16
AF = mybir.ActivationFunctionType
ALU = mybir.AluOpType

D = 1024
DFF = 2816
NFF = 22
H = 8
LC = 256
EPS = 1e-6
NADA = 9
INC = 3648
SCALE = (128 + 64) ** -0.5
SQD = 32.0


class Sem:
    def __init__(self, h, inc):
        self.h = h
        self.inc = inc
        self.issued = 0


class Buf:
    def __init__(self):
        self.w = None
        self.r = {}


class Sched:
    def __init__(self, nc, stack, ndsem=40):
        self.nc = nc
        self.engs = {'pe': nc.tensor, 'act': nc.scalar, 'dve': nc.vector, 'pool': nc.gpsimd, 'sp': nc.sync}
        self.esem = {}
        for k in self.engs:
            self.esem[k] = Sem(stack.enter_context(nc.semaphore("e_" + k)), 1)
        self.dfree = [Sem(stack.enter_context(nc.semaphore("d%d" % i)), 16) for i in range(ndsem)]
        self.dall = list(self.dfree)
        self.dmap = {}
        self.seen = {k: {} for k in self.engs}
        self.nins = 0

    def _wait(self, eng, deps):
        E = self.engs[eng]
        seen = self.seen[eng]
        for s, n in deps.items():
            if s is self.esem['pe'] and eng == 'pe':
                continue
            val = n * s.inc if s.inc == 1 else s.issued * 16
            if seen.get(s, 0) < val:
                E.wait_ge(s.h, val)
                seen[s] = val
                self.nins += 1

    def _deps(self, reads, writes):
        deps = {}
        for b in reads:
            if b.w is not None:
                s, n = b.w
                if deps.get(s, 0) < n:
                    deps[s] = n
        for b in writes:
            if b.w is not None:
                s, n = b.w
                if deps.get(s, 0) < n:
                    deps[s] = n
            for s, n in b.r.items():
                if deps.get(s, 0) < n:
                    deps[s] = n
        return deps

    def _mark(self, s, reads, writes):
        n = s.issued
        for b in reads:
            b.r[s] = n
        for b in writes:
            b.w = (s, n)
            b.r = {}

    def op(self, eng, fn, reads=(), writes=()):
        self._wait(eng, self._deps(reads, writes))
        ins = fn(self.engs[eng])
        s = self.esem[eng]
        ins.then_inc(s.h, 1)
        s.issued += 1
        self.nins += 1
        self._mark(s, reads, writes)

    def dsem(self, key):
        s = self.dmap.get(key)
        if s is None:
            s = self.dfree.pop()
            self.dmap[key] = s
        return s

    def dma(self, out, in_, reads=(), writes=(), key=None, eng='sp', slow=False):
        self._wait(eng, self._deps(reads, writes))
        s = self.dsem(key)
        E = self.engs[eng]
        if slow:
            ins = E.dma_start(out=out, in_=in_, allow_slow_non_contiguous=True)
        else:
            ins = E.dma_start(out=out, in_=in_)
        ins.then_inc(s.h, 16)
        s.issued += 1
        self.nins += 1
        self._mark(s, reads, writes)

    def barrier(self):
        sems = list(self.esem.values()) + self.dall
        for eng in self.engs:
            deps = {s: s.issued for s in sems if s.issued > 0 and s is not self.esem[eng]}
            E = self.engs[eng]
            seen = self.seen[eng]
            for s, n in deps.items():
                val = n * s.inc
                if seen.get(s, 0) < val:
                    E.wait_ge(s.h, val)
                    seen[s] = val
                    self.nins += 1
        for k, s in self.dmap.items():
            self.dfree.append(s)
        self.dmap = {}


class Ctx:
    pass


def build_program(L, depth, dbg=None):
    nc = bass.Bass("TRN2", target_bir_lowering=False)
    NT = L // 128
    NKB = L // 1024
    LT = L + LC
    NCH = L // 8
    NCC = LC // 8

    def din(name, shape, dt=F32):
        return nc.dram_tensor(name, list(shape), dt, kind="ExternalInput")

    g = Ctx()
    g.dbg = dbg
    g.x = din("x", [L, D]); g.c = din("c", [1, D]); g.ctx = din("ctx", [LC, D]); g.c_ctx = din("c_ctx", [1, D])
    g.ada_w = din("ada_w", [depth, D, NADA * D]); g.ada_b = din("ada_b", [depth, NADA * D])
    g.norm_pre = din("norm_pre", [depth, 3, D]); g.norm_post = din("norm_post", [depth, 3, D])
    g.wg = din("ffn_w_gate", [depth, 2, D, DFF]); g.wu = din("ffn_w_up", [depth, 2, D, DFF])
    g.wd = din("ffn_w_down", [depth, 2, DFF, D])
    g.w_in = din("w_in", [depth, D, INC]); g.q_norm = din("q_norm", [depth, 256])
    g.w_uq = din("w_uq", [depth, 256, 1536]); g.kv_norm = din("kv_norm", [depth, 256])
    g.w_ukv = din("w_ukv", [depth, 256, 2048]); g.w_o_mla = din("w_o_mla", [depth, D, D])
    g.a_re = din("s5_a_re", [depth, 2, 64, 64]); g.a_im = din("s5_a_im", [depth, 2, 64, 64])
    g.log_dt = din("s5_log_dt", [depth, 2, 64])
    g.b_re = din("s5_b_re", [depth, 2, 64, 64, 16]); g.b_im = din("s5_b_im", [depth, 2, 64, 64, 16])
    g.c_re = din("s5_c_re", [depth, 2, 64, 16, 64]); g.c_im = din("s5_c_im", [depth, 2, 64, 16, 64])
    g.s5_d = din("s5_d", [depth, D]); g.glu_w = din("glu_w", [depth, D, D]); g.glu_b = din("glu_b", [depth, D])
    g.w_o_s5 = din("w_o_s5", [depth, D, D]); g.w_out = din("w_out", [depth, D, D])
    g.ropec = din("rope_cos", [128, L]); g.ropes = din("rope_sin", [128, L])
    g.identf = din("identf", [128, 128]); g.maskl = din("maskl", [128, 128]); g.masku = din("masku", [128, 128])
    out = nc.dram_tensor("out", [L, D], F32, kind="ExternalOutput")

    def scr(name, shape, dt):
        if dbg is not None and name in dbg:
            return nc.dram_tensor(name, list(shape), dt, kind="ExternalOutput")
        return nc.dram_tensor(name, list(shape), dt)

    g.xs = scr("xs", [L, D], F32); g.cs = scr("cs", [LC, D], F32)
    g.mod = scr("mod", [depth, 2, NADA * D], F32)
    g.aT = scr("aT", [NFF, 128, LT], BF16)
    g.hT = scr("hT", [8, 128, LT], BF16)
    g.qn = scr("qn", [H, 128, LT], BF16); g.qr = scr("qr", [4, 128, LT], BF16)
    g.kn = scr("kn", [H, 128, LT], BF16); g.kr = scr("kr", [128, LT], BF16)
    g.v = scr("v", [LT // 128, 128, D], BF16)
    g.oT = scr("oT", [H, 128, LT], BF16)
    g.zT = scr("zT", [8, 128, LT], BF16)
    if dbg is not None and "dbg_gx" in dbg:
        g.dbg_gx = nc.dram_tensor("dbg_gx", [64, 128, (L // 1024) * 128], F32, kind="ExternalOutput")

    def AP(t, off, dims):
        return bass.AP(t, off, [list(d) for d in dims])

    def xtile_ap(t, stream, i):
        if stream == 0:
            kb, tau = divmod(i, 8)
            return AP(t, (1024 * kb + tau) * D, [[8 * D, 128], [1, D]])
        return AP(t, i * 128 * D, [[D, 128], [1, D]])

    with ExitStack() as gs:
        S = Sched(nc, gs)
        sb_id = gs.enter_context(nc.sbuf_tensor("identb", [128, 128], BF16))
        sb_idf = gs.enter_context(nc.sbuf_tensor("identf_sb", [128, 128], F32))
        b_id = Buf()
        S.dma(sb_idf[:], g.identf.ap()[:, :], writes=[b_id], key=b_id)
        S.op('pool', lambda e: e.tensor_copy(out=sb_id[:], in_=sb_idf[:]), reads=[b_id], writes=[b_id])
        g.ident = sb_id; g.identf_sb = sb_idf; g.b_id = b_id

        uid = [0]

        def T(st, name, shape, dt):
            uid[0] += 1
            return st.enter_context(nc.sbuf_tensor("%s_u%d" % (name, uid[0]), list(shape), dt))

        def P(st, name, shape, dt=F32):
            uid[0] += 1
            return st.enter_context(nc.psum_tensor("%s_u%d" % (name, uid[0]), list(shape), dt))

        def rstd_ops(ss_ap, tmp_ap, out_ap, bss, brs, n):
            S.op('dve', lambda e: e.tensor_scalar(out=tmp_ap, in0=ss_ap, scalar1=1.0 / n, scalar2=EPS, op0=ALU.mult, op1=ALU.add), reads=[bss], writes=[brs])
            S.op('act', lambda e: e.activation(out=tmp_ap, in_=tmp_ap, func=AF.Sqrt), reads=[brs], writes=[brs])
            S.op('dve', lambda e: e.reciprocal(out=out_ap, in_=tmp_ap), reads=[brs], writes=[brs])

        def load_bc(st, name, src_ap, key_b):
            t = T(st, name, [128, D], F32)
            S.dma(t[:], src_ap.partition_broadcast(128), writes=[key_b], key=key_b)
            return t

        def stream_info(stream):
            if stream == 0:
                return NT, 0
            return LC // 128, L

        def phase_mod():
            with ExitStack() as st:
                cT = T(st, "cT", [128, 8, 2], F32); bcT = Buf()
                S.dma(cT[:, :, 0], g.c.ap().rearrange("o (j p) -> p (o j)", p=128), writes=[bcT], key=bcT, slow=True)
                S.dma(cT[:, :, 1], g.c_ctx.ap().rearrange("o (j p) -> p (o j)", p=128), writes=[bcT], key=bcT, slow=True)
                S.op('act', lambda e: e.activation(out=cT[:], in_=cT[:], func=AF.Silu), reads=[bcT], writes=[bcT])
                GW = 1536
                stg = [[T(st, "adst%d_%d" % (s_, k), [128, GW], F32) for k in range(8)] for s_ in range(2)]
                bst = [[Buf() for k in range(8)] for s_ in range(2)]
                adab = T(st, "adab", [2, NADA * D], F32); badab = Buf()
                modsb = [T(st, "modsb%d" % i, [2, GW], F32) for i in range(2)]; bmod = [Buf(), Buf()]
                ps = [P(st, "psm%d" % i, [128, 512]) for i in range(3)]; bps = [Buf() for _ in range(3)]
                gi = 0
                for l in range(depth):
                    for r_ in range(2):
                        S.dma(adab[r_:r_ + 1, :], g.ada_b.ap()[l:l + 1, :], reads=[], writes=[badab], key=badab)
                    for grp in range(NADA * D // GW):
                        sl = gi % 2
                        for k in range(8):
                            S.dma(stg[sl][k][:], g.ada_w.ap()[l, k * 128:(k + 1) * 128, grp * GW:(grp + 1) * GW], writes=[bst[sl][k]], key=bst[sl][k])
                        for nt in range(3):
                            for k in range(8):
                                S.op('pe', lambda e, k=k, nt=nt, sl=sl: e.matmul(ps[nt][0:2, :], cT[:, k, :], stg[sl][k][:, nt * 512:(nt + 1) * 512], start=(k == 0), stop=(k == 7)),
                                     reads=[bcT, bst[sl][k]], writes=[bps[nt]])
                            S.op('dve', lambda e, nt=nt, sl=sl, grp=grp: e.tensor_tensor(out=modsb[sl][:, nt * 512:(nt + 1) * 512], in0=ps[nt][0:2, :], in1=adab[:, grp * GW + nt * 512: grp * GW + (nt + 1) * 512], op=ALU.add),
                                 reads=[bps[nt], badab], writes=[bmod[sl]])
                        S.dma(g.mod.ap()[l, :, grp * GW:(grp + 1) * GW], modsb[sl][:], reads=[bmod[sl]], key=bmod[sl])
                        gi += 1
                S.barrier()

        def load_pre(st, l, sub, stream, tag):
            bsc = Buf(); bsh = Buf(); bg = Buf()
            sc = load_bc(st, "sc" + tag, g.mod.ap()[l, stream:stream + 1, (3 * sub + 1) * D:(3 * sub + 2) * D], bsc)
            sh = load_bc(st, "sh" + tag, g.mod.ap()[l, stream:stream + 1, (3 * sub) * D:(3 * sub + 1) * D], bsh)
            gp = load_bc(st, "gp" + tag, g.norm_pre.ap()[l, sub:sub + 1, :], bg)
            S.op('dve', lambda e: e.scalar_tensor_tensor(out=sc[:], in0=sc[:], scalar=1.0, in1=gp[:], op0=ALU.add, op1=ALU.mult), reads=[bsc, bg], writes=[bsc])
            return sc, bsc, sh, bsh

        def load_post(st, l, sub, stream, weight, tag):
            bga = Buf(); bg = Buf()
            ga = load_bc(st, "ga" + tag, g.mod.ap()[l, stream:stream + 1, (3 * sub + 2) * D:(3 * sub + 3) * D], bga)
            gp = load_bc(st, "gq" + tag, g.norm_post.ap()[l, sub:sub + 1, :], bg)
            S.op('dve', lambda e: e.scalar_tensor_tensor(out=ga[:], in0=ga[:], scalar=float(weight), in1=gp[:], op0=ALU.mult, op1=ALU.mult), reads=[bga, bg], writes=[bga])
            return ga, bga

        class PreNorm:
            def __init__(self, st, tag, nhb=4):
                self.xt = [T(st, "pn_x%s%d" % (tag, i), [128, D], F32) for i in range(2)]
                self.bx = [Buf(), Buf()]
                self.junk = T(st, "pn_j" + tag, [128, D], BF16); self.bj = Buf()
                self.ss = [T(st, "pn_ss%s%d" % (tag, i), [128, 4], F32) for i in range(2)]; self.bss = [Buf(), Buf()]
                self.tmp = [T(st, "pn_t%s%d" % (tag, i), [128, D], F32) for i in range(2)]; self.bt = [Buf(), Buf()]
                self.hb = [T(st, "pn_h%s%d" % (tag, i), [128, D], BF16) for i in range(nhb)]; self.bh = [Buf() for _ in range(nhb)]
                self.pt = [P(st, "pn_pt%s%d" % (tag, i), [128, 8, 128], BF16) for i in range(2)]; self.bpt = [Buf(), Buf()]
                self.n = 0
                self.m = 0

            def run1(self, src_ap, gsc, bgs, sh, bsh):
                i = self.n % 2
                j = self.n % len(self.hb)
                self.n += 1
                xt, bx, ss, bss, tmp, bt, hb, bh = self.xt[i], self.bx[i], self.ss[i], self.bss[i], self.tmp[i], self.bt[i], self.hb[j], self.bh[j]
                S.dma(xt[:], src_ap, writes=[bx], key=bx)
                S.op('act', lambda e: e.activation(out=self.junk[:], in_=xt[:], func=AF.Square, accum_out=ss[:, 0:1]), reads=[bx], writes=[self.bj, bss])
                rstd_ops(ss[:, 0:1], ss[:, 1:2], ss[:, 2:3], bss, bss, D)
                S.op('dve', lambda e: e.scalar_tensor_tensor(out=tmp[:], in0=xt[:], scalar=ss[:, 2:3], in1=gsc[:], op0=ALU.mult, op1=ALU.mult), reads=[bx, bss, bgs], writes=[bt])
                S.op('pool', lambda e: e.tensor_tensor(out=hb[:], in0=tmp[:], in1=sh[:], op=ALU.add), reads=[bt, bsh], writes=[bh])
                return j

            def run2(self, j, hT, bhT, col0):
                k = self.m % 2
                self.m += 1
                hb, bh, pt, bpt = self.hb[j], self.bh[j], self.pt[k], self.bpt[k]
                for jj in range(8):
                    S.op('pe', lambda e, jj=jj: e.transpose(pt[:, jj, :], hb[:, jj * 128:(jj + 1) * 128], g.ident[:]), reads=[bh, g.b_id], writes=[bpt])
                S.op('act', lambda e: e.copy(out=hT[:, :, col0:col0 + 128], in_=pt[:]), reads=[bpt], writes=[bhT])

            def run(self, src_ap, gsc, bgs, sh, bsh, hT, bhT, col0):
                self.run2(self.run1(src_ap, gsc, bgs, sh, bsh), hT, bhT, col0)

        class PostNorm:
            def __init__(self, st, tag):
                self.xt = [T(st, "po_x%s%d" % (tag, i), [128, D], F32) for i in range(2)]; self.bx = [Buf(), Buf()]
                self.junk = T(st, "po_j" + tag, [128, 512], BF16); self.bj = Buf()
                self.ss = [T(st, "po_ss%s%d" % (tag, i), [128, 8], F32) for i in range(2)]; self.bss = [Buf(), Buf()]
                self.tmp = [T(st, "po_t%s%d" % (tag, i), [128, D], F32) for i in range(2)]; self.bt = [Buf(), Buf()]
                self.n = 0

            def run(self, src_ap, dst_ap, py, bpy, gg, bgg):
                i = self.n % 2
                self.n += 1
                xt, bx, ss, bss, tmp, bt = self.xt[i], self.bx[i], self.ss[i], self.bss[i], self.tmp[i], self.bt[i]
                S.dma(xt[:], src_ap, writes=[bx], key=bx)
                for hf in range(2):
                    S.op('act', lambda e, hf=hf: e.activation(out=self.junk[:], in_=py[hf][:], func=AF.Square, accum_out=ss[:, hf:hf + 1]), reads=[bpy[hf]], writes=[self.bj, bss])
                S.op('dve', lambda e: e.tensor_tensor(out=ss[:, 2:3], in0=ss[:, 0:1], in1=ss[:, 1:2], op=ALU.add), reads=[bss], writes=[bss])
                rstd_ops(ss[:, 2:3], ss[:, 3:4], ss[:, 4:5], bss, bss, D)
                for hf in range(2):
                    S.op('dve', lambda e, hf=hf: e.scalar_tensor_tensor(out=tmp[:, hf * 512:(hf + 1) * 512], in0=py[hf][:], scalar=ss[:, 4:5], in1=gg[:, hf * 512:(hf + 1) * 512], op0=ALU.mult, op1=ALU.mult),
                         reads=[bpy[hf], bss, bgg], writes=[bt])
                S.op('pool', lambda e: e.tensor_tensor(out=tmp[:], in0=tmp[:], in1=xt[:], op=ALU.add), reads=[bt, bx], writes=[bt])
                S.dma(dst_ap, tmp[:], reads=[bt], key=bt)

        class WLoad:
            def __init__(self, st, width, tag, n=3):
                self.stg = [T(st, "wst%s%d" % (tag, i), [128, width], F32) for i in range(n)]
                self.b = [Buf() for _ in range(n)]
                self.n = 0
                self.width = width

            def load(self, dst_ap, src_ap, bdst, rows=128, cols=None, eng=None, scale_ap=None, bscale=None, neg=False):
                i = self.n % len(self.stg)
                self.n += 1
                cols = self.width if cols is None else cols
                stg, b = self.stg[i], self.b[i]
                S.dma(stg[0:rows, 0:cols], src_ap, writes=[b], key=b)
                if scale_ap is not None:
                    S.op('dve', lambda e: e.tensor_scalar(out=dst_ap, in0=stg[0:rows, 0:cols], scalar1=scale_ap, scalar2=None, op0=ALU.mult), reads=[b, bscale], writes=[bdst])
                else:
                    eng = eng or ('pool' if self.n % 2 else 'dve')
                    S.op(eng, lambda e: e.tensor_copy(out=dst_ap, in_=stg[0:rows, 0:cols]), reads=[b], writes=[bdst])

        g.T = T; g.P = P; g.S = S; g.AP = AP; g.xtile_ap = xtile_ap; g.load_pre = load_pre; g.load_post = load_post
        g.PreNorm = PreNorm; g.PostNorm = PostNorm; g.WLoad = WLoad; g.rstd_ops = rstd_ops; g.stream_info = stream_info
        g.L = L; g.NT = NT; g.LT = LT; g.depth = depth; g.nc = nc

        def phase_ffn_a(l, j, sub, src, do_ctx=True):
            with ExitStack() as st:
                wgs = T(st, "wg_sb", [128, 8, DFF], BF16); wus = T(st, "wu_sb", [128, 8, DFF], BF16)
                bw = Buf()
                wl = WLoad(st, 1408, "fa")
                for k in range(8):
                    for hf in range(2):
                        wl.load(wgs[:, k, hf * 1408:(hf + 1) * 1408], g.wg.ap()[l, j, k * 128:(k + 1) * 128, hf * 1408:(hf + 1) * 1408], bw)
                        wl.load(wus[:, k, hf * 1408:(hf + 1) * 1408], g.wu.ap()[l, j, k * 128:(k + 1) * 128, hf * 1408:(hf + 1) * 1408], bw)
                pn = PreNorm(st, "fa")
                hT = [T(st, "fa_hT%d" % i, [128, 8, 512], BF16) for i in range(2)]; bhT = [Buf(), Buf()]
                psg = [P(st, "psg%d" % i, [128, 512]) for i in range(2)]; bpg = [Buf(), Buf()]
                psu = [P(st, "psu%d" % i, [128, 512]) for i in range(2)]; bpu = [Buf(), Buf()]
                sg = [T(st, "fa_sg%d" % i, [128, 512], F32) for i in range(2)]; bsg = [Buf(), Buf()]
                ao = [T(st, "fa_ao%d" % i, [128, 512], BF16) for i in range(4)]; bao = [Buf() for _ in range(4)]
                blk = 0
                cnt = 0
                for stream in ([0, 1] if do_ctx else [0]):
                    with ExitStack() as st2:
                        gsc, bgs, sh, bsh = load_pre(st2, l, sub, stream, "fa%d" % stream)
                        ntile, cbase = stream_info(stream)
                        sap = src[stream]
                        blocks = [(b0, min(4, ntile - b0)) for b0 in range(0, ntile, 4)]
                        pend = [pn.run1(xtile_ap(sap, stream, blocks[0][0] + i), gsc, bgs, sh, bsh) for i in range(blocks[0][1])]
                        for bi, (b0, nt_) in enumerate(blocks):
                            ncol = nt_ * 128
                            hb_, bhb = hT[blk % 2], bhT[blk % 2]
                            blk += 1
                            for i, j_ in enumerate(pend):
                                pn.run2(j_, hb_, bhb, i * 128)
                            if bi + 1 < len(blocks):
                                pend = [pn.run1(xtile_ap(sap, stream, blocks[bi + 1][0] + i), gsc, bgs, sh, bsh) for i in range(blocks[bi + 1][1])]
                            for f in range(NFF):
                                q = cnt % 2
                                for k in range(8):
                                    S.op('pe', lambda e, k=k, f=f, q=q: e.matmul(psg[q][:, 0:ncol], wgs[:, k, f * 128:(f + 1) * 128], hb_[:, k, 0:ncol], start=(k == 0), stop=(k == 7)), reads=[bw, bhb], writes=[bpg[q]])
                                for k in range(8):
                                    S.op('pe', lambda e, k=k, f=f, q=q: e.matmul(psu[q][:, 0:ncol], wus[:, k, f * 128:(f + 1) * 128], hb_[:, k, 0:ncol], start=(k == 0), stop=(k == 7)), reads=[bw, bhb], writes=[bpu[q]])
                                S.op('act', lambda e, q=q: e.activation(out=sg[q][:, 0:ncol], in_=psg[q][:, 0:ncol], func=AF.Silu), reads=[bpg[q]], writes=[bsg[q]])
                                a_ = cnt % 4
                                S.op('dve', lambda e, q=q, a_=a_: e.tensor_tensor(out=ao[a_][:, 0:ncol], in0=sg[q][:, 0:ncol], in1=psu[q][:, 0:ncol], op=ALU.mult), reads=[bsg[q], bpu[q]], writes=[bao[a_]])
                                S.dma(g.aT.ap()[f, :, cbase + b0 * 128: cbase + b0 * 128 + ncol], ao[a_][:, 0:ncol], reads=[bao[a_]], key=bao[a_])
                                cnt += 1
                S.barrier()

        def phase_ffn_b(l, j, sub, src, dst, do_ctx=True):
            with ExitStack() as st:
                wds = T(st, "wd_sb", [128, NFF, D], BF16); bw = Buf()
                wl = WLoad(st, D, "fb")
                for f in range(NFF):
                    wl.load(wds[:, f, :], g.wd.ap()[l, j, f * 128:(f + 1) * 128, :], bw)
                po = PostNorm(st, "fb")
                asb = [T(st, "fb_a%d" % i, [128, NFF, 512], BF16) for i in range(2)]; ba = [Buf(), Buf()]
                py = [[P(st, "fb_py%d_%d" % (i, hf), [128, 512]) for hf in range(2)] for i in range(2)]
                bpy = [[Buf(), Buf()] for i in range(2)]
                blk = 0; cnt = 0
                for stream in ([0, 1] if do_ctx else [0]):
                    with ExitStack() as st2:
                        gg, bgg = load_post(st2, l, sub, stream, 0.5, "fb%d" % stream)
                        ntile, cbase = stream_info(stream)
                        for b0 in range(0, ntile, 4):
                            nt_ = min(4, ntile - b0)
                            ncol = nt_ * 128
                            a_, ba_ = asb[blk % 2], ba[blk % 2]
                            blk += 1
                            S.dma(a_[:, :, 0:ncol], g.aT.ap()[:, :, cbase + b0 * 128: cbase + b0 * 128 + ncol].rearrange("f p c -> p f c"), writes=[ba_], key=ba_)
                            for i in range(nt_):
                                q = cnt % 2
                                cnt += 1
                                for hf in range(2):
                                    for f in range(NFF):
                                        S.op('pe', lambda e, f=f, hf=hf, i=i, q=q: e.matmul(py[q][hf][:], a_[:, f, i * 128:(i + 1) * 128], wds[:, f, hf * 512:(hf + 1) * 512], start=(f == 0), stop=(f == NFF - 1)),
                                             reads=[ba_, bw], writes=[bpy[q][hf]])
                                po.run(xtile_ap(src[stream], stream, b0 + i), xtile_ap(dst[stream], stream, b0 + i), py[q], bpy[q], gg, bgg)
                S.barrier()

        g.phase_mod = phase_mod; g.phase_ffn_a = phase_ffn_a; g.phase_ffn_b = phase_ffn_b
        from_mixer = build_mixer(g)

        phase_mod()
        cur = [g.x, g.ctx]
        scrs = [g.xs, g.cs]
        for l in range(depth):
            last = (l == depth - 1)
            final_dst = [out, g.cs]
            phase_ffn_a(l, 0, 0, cur)
            phase_ffn_b(l, 0, 0, cur, scrs)
            cur = scrs
            if dbg is not None and dbg.get("stop") == "ffn0":
                break
            from_mixer(l, cur, scrs, last)
            if dbg is not None and dbg.get("stop") == "mixer":
                break
            phase_ffn_a(l, 1, 2, cur, do_ctx=not last)
            phase_ffn_b(l, 1, 2, cur, final_dst if last else scrs, do_ctx=not last)
        S.barrier()
        g.nins = S.nins
    return nc, g


def build_mixer(g):
    T, P, S, AP, xtile_ap = g.T, g.P, g.S, g.AP, g.xtile_ap
    L, NT, LT = g.L, g.NT, g.LT
    nc = g.nc
    NTT = LT // 128

    def sap(t, tot, off, *dims, parts=128):
        return bass.AP(t, off, [[tot, parts]] + [list(d) for d in dims])

    def phase_mixer_in(l, src, last):
        with ExitStack() as st:
            win1 = T(st, "win1", [128, 8, 512], BF16); wkr = T(st, "wkr", [128, 8, 128], BF16); wkrr = T(st, "wkrr", [128, 8, 128], BF16)
            wuq = T(st, "wuq", [128, 2, 1536], BF16); wuqr = T(st, "wuqr", [128, 2, 512], BF16); wuqrr = T(st, "wuqrr", [128, 2, 512], BF16)
            wukn = T(st, "wukn", [128, 2, 1024], BF16); wuv = T(st, "wuv", [128, 2, 1024], BF16)
            nrm_s = T(st, "nrm_s", [128, 4], F32)
            bw = Buf(); bn = Buf()
            S.dma(nrm_s[:, 0:2], g.q_norm.ap()[l:l + 1, :].rearrange("o (j p) -> p (o j)", p=128), writes=[bn], key=bn, slow=True)
            S.dma(nrm_s[:, 2:4], g.kv_norm.ap()[l:l + 1, :].rearrange("o (j p) -> p (o j)", p=128), writes=[bn], key=bn, slow=True)
            wl = g.WLoad(st, 512, "mi")
            for k in range(8):
                wl.load(win1[:, k, :], g.w_in.ap()[l, k * 128:(k + 1) * 128, 0:512], bw)
            stk = T(st, "stk", [128, 8, 64], F32); bstk = Buf()
            S.dma(stk[:], g.w_in.ap()[l, :, 512:576].rearrange("(k p) c -> p k c", p=128), writes=[bstk], key=bstk)
            for dup in range(2):
                S.op('dve', lambda e, dup=dup: e.tensor_copy(out=wkr[:, :, dup * 64:(dup + 1) * 64], in_=stk[:]), reads=[bstk], writes=[bw])
                for b in range(2):
                    for hf in range(2):
                        dc = dup * 64 + 32 * b + 16 * hf
                        sc_ = 32 * b + 16 * (1 - hf)
                        sgn = -1.0 if hf == 0 else 1.0
                        S.op('dve', lambda e, dc=dc, sc_=sc_, sgn=sgn: e.tensor_scalar(out=wkrr[:, :, dc:dc + 16], in0=stk[:, :, sc_:sc_ + 16], scalar1=sgn, scalar2=None, op0=ALU.mult), reads=[bstk], writes=[bw])
            stq = T(st, "stq", [128, 2048], F32); bstq = Buf()
            for j in range(2):
                S.dma(stq[:, 0:1536], g.w_uq.ap()[l, j * 128:(j + 1) * 128, :], writes=[bstq], key=bstq)
                S.op('dve', lambda e, j=j: e.tensor_scalar(out=wuq[:, j, :], in0=stq[:, 0:1536], scalar1=nrm_s[:, j:j + 1], scalar2=None, op0=ALU.mult), reads=[bstq, bn], writes=[bw])
                S.op('dve', lambda e, j=j: e.tensor_copy(out=sap(wuqr, 1024, j * 512, [64, 8], [1, 64]), in_=sap(wuq, 3072, j * 1536 + 128, [192, 8], [1, 64])), reads=[bw], writes=[bw])
                for b in range(2):
                    for hf in range(2):
                        dc = 32 * b + 16 * hf
                        sc_ = 32 * b + 16 * (1 - hf)
                        sgn = -1.0 if hf == 0 else 1.0
                        S.op('dve', lambda e, j=j, dc=dc, sc_=sc_, sgn=sgn: e.tensor_scalar(out=sap(wuqrr, 1024, j * 512 + dc, [64, 8], [1, 16]), in0=sap(wuq, 3072, j * 1536 + 128 + sc_, [192, 8], [1, 16]), scalar1=sgn, scalar2=None, op0=ALU.mult), reads=[bw], writes=[bw])
            for j in range(2):
                S.dma(stq[:, :], g.w_ukv.ap()[l, j * 128:(j + 1) * 128, :], writes=[bstq], key=bstq)
                S.op('dve', lambda e, j=j: e.tensor_scalar(out=sap(wukn, 2048, j * 1024, [128, 8], [1, 128]), in0=sap(stq, 2048, 0, [256, 8], [1, 128]), scalar1=nrm_s[:, 2 + j:3 + j], scalar2=None, op0=ALU.mult), reads=[bstq, bn], writes=[bw])
                S.op('dve', lambda e, j=j: e.tensor_scalar(out=sap(wuv, 2048, j * 1024, [128, 8], [1, 128]), in0=sap(stq, 2048, 128, [256, 8], [1, 128]), scalar1=nrm_s[:, 2 + j:3 + j], scalar2=None, op0=ALU.mult), reads=[bstq, bn], writes=[bw])

            pn = g.PreNorm(st, "mi")
            hT = [T(st, "mi_hT%d" % i, [128, 8, 512], BF16) for i in range(2)]; bhT = [Buf(), Buf()]
            cqT = T(st, "mi_cqT", [128, 2, 512], BF16); ckvT = T(st, "mi_ckvT", [128, 2, 512], BF16); blat = Buf()
            lat = [T(st, "mi_lat%d" % i, [128, 512], BF16) for i in range(2)]; blt = [Buf(), Buf()]
            lss = [T(st, "mi_lss%d" % i, [128, 8], F32) for i in range(2)]; blss = [Buf(), Buf()]
            junk = T(st, "mi_junk", [128, 256], BF16); bj = Buf()
            cos_sb = T(st, "mi_cos", [128, 512], F32); sin_sb = T(st, "mi_sin", [128, 512], F32); brope = Buf()
            t1 = [T(st, "mi_t1%d" % i, [128, 512], F32) for i in range(2)]; bt1 = [Buf(), Buf()]
            t2 = [T(st, "mi_t2%d" % i, [128, 512], F32) for i in range(2)]; bt2 = [Buf(), Buf()]
            ob = [T(st, "mi_ob%d" % i, [128, 512], BF16) for i in range(4)]; bob = [Buf() for _ in range(4)]
            vsb = [T(st, "mi_v%d" % i, [128, D], BF16) for i in range(2)]; bvs = [Buf(), Buf()]
            ps_lat = P(st, "mi_pslat", [128, 512]); bpl = Buf()
            pt2 = P(st, "mi_pt2", [128, 8, 128], BF16); bpt2 = Buf()
            psA = [P(st, "mi_psA%d" % i, [128, 512]) for i in range(2)]; bpA = [Buf(), Buf()]
            psB = [P(st, "mi_psB%d" % i, [128, 512]) for i in range(2)]; bpB = [Buf(), Buf()]
            cnt = {'a': 0, 'o': 0, 'r': 0, 'v': 0}

            def evac_copy(ps_ap, bps, dst_dram_ap, ncol):
                o = cnt['o'] % 4; cnt['o'] += 1
                eng = 'act' if cnt['o'] % 2 else 'dve'
                if eng == 'act':
                    S.op('act', lambda e: e.copy(out=ob[o][:, 0:ncol], in_=ps_ap), reads=[bps], writes=[bob[o]])
                else:
                    S.op('dve', lambda e: e.tensor_copy(out=ob[o][:, 0:ncol], in_=ps_ap), reads=[bps], writes=[bob[o]])
                S.dma(dst_dram_ap, ob[o][:, 0:ncol], reads=[bob[o]], key=bob[o])

            def evac_rope(psa, bpa, psb, bpb, dst_dram_ap, ncol):
                o = cnt['o'] % 4; cnt['o'] += 1
                r_ = cnt['r'] % 2; cnt['r'] += 1
                S.op('dve', lambda e: e.tensor_tensor(out=t1[r_][:, 0:ncol], in0=psa, in1=cos_sb[:, 0:ncol], op=ALU.mult), reads=[bpa, brope], writes=[bt1[r_]])
                S.op('dve', lambda e: e.tensor_tensor(out=t2[r_][:, 0:ncol], in0=psb, in1=sin_sb[:, 0:ncol], op=ALU.mult), reads=[bpb, brope], writes=[bt2[r_]])
                S.op('pool', lambda e: e.tensor_tensor(out=ob[o][:, 0:ncol], in0=t1[r_][:, 0:ncol], in1=t2[r_][:, 0:ncol], op=ALU.add), reads=[bt1[r_], bt2[r_]], writes=[bob[o]])
                S.dma(dst_dram_ap, ob[o][:, 0:ncol], reads=[bob[o]], key=bob[o])

            def mm_group(ps, bps, lhs_fn, rhs_fn, nk, rd):
                for k in range(nk):
                    S.op('pe', lambda e, k=k: e.matmul(ps, lhs_fn(k), rhs_fn(k), start=(k == 0), stop=(k == nk - 1)), reads=rd, writes=[bps])

            blk = 0
            for stream in [0, 1]:
                with ExitStack() as st2:
                    gsc, bgs, sh, bsh = g.load_pre(st2, l, 1, stream, "mi%d" % stream)
                    ntile, cbase = g.stream_info(stream)
                    blocks = [(b0, min(4, ntile - b0)) for b0 in range(0, ntile, 4)]
                    pend = [pn.run1(xtile_ap(src[stream], stream, blocks[0][0] + i), gsc, bgs, sh, bsh) for i in range(blocks[0][1])]
                    for bi, (b0, nt_) in enumerate(blocks):
                        ncol = nt_ * 128
                        c0 = cbase + b0 * 128
                        hb_, bhb = hT[blk % 2], bhT[blk % 2]
                        blk += 1
                        if stream == 0:
                            S.dma(cos_sb[:, 0:ncol], g.ropec.ap()[:, c0:c0 + ncol], writes=[brope], key=brope)
                            S.dma(sin_sb[:, 0:ncol], g.ropes.ap()[:, c0:c0 + ncol], writes=[brope], key=brope)
                        for i, j_ in enumerate(pend):
                            pn.run2(j_, hb_, bhb, i * 128)
                        if bi + 1 < len(blocks):
                            pend = [pn.run1(xtile_ap(src[stream], stream, blocks[bi + 1][0] + i), gsc, bgs, sh, bsh) for i in range(blocks[bi + 1][1])]
                        S.dma(g.hT.ap()[:, :, c0:c0 + ncol].rearrange("k p c -> p k c"), hb_[:, :, 0:ncol], reads=[bhb], key=bhb)
                        for i in range(nt_):
                            q = cnt['a'] % 2; cnt['a'] += 1
                            mm_group(ps_lat[:], bpl, lambda k: hb_[:, k, i * 128:(i + 1) * 128], lambda k: win1[:, k, :], 8, [bhb, bw])
                            for hf in range(2):
                                S.op('act', lambda e, hf=hf, q=q: e.activation(out=junk[:], in_=ps_lat[:, hf * 256:(hf + 1) * 256], func=AF.Square, accum_out=lss[q][:, hf:hf + 1]), reads=[bpl], writes=[bj, blss[q]])
                            g.rstd_ops(lss[q][:, 0:2], lss[q][:, 2:4], lss[q][:, 4:6], blss[q], blss[q], 256)
                            for hf in range(2):
                                S.op('dve', lambda e, hf=hf, q=q: e.tensor_scalar(out=lat[q][:, hf * 256:(hf + 1) * 256], in0=ps_lat[:, hf * 256:(hf + 1) * 256], scalar1=lss[q][:, 4 + hf:5 + hf], scalar2=None, op0=ALU.mult), reads=[bpl, blss[q]], writes=[blt[q]])
                            for j4 in range(4):
                                S.op('pe', lambda e, j4=j4, q=q: e.transpose(pt2[:, j4, :], lat[q][:, j4 * 128:(j4 + 1) * 128], g.ident[:]), reads=[blt[q], g.b_id], writes=[bpt2])
                            S.op('act', lambda e, i=i: e.copy(out=cqT[:, :, i * 128:(i + 1) * 128], in_=pt2[:, 0:2, :]), reads=[bpt2], writes=[blat])
                            S.op('act', lambda e, i=i: e.copy(out=ckvT[:, :, i * 128:(i + 1) * 128], in_=pt2[:, 2:4, :]), reads=[bpt2], writes=[blat])
                        mm_group(psA[0][:, 0:ncol], bpA[0], lambda k: wkr[:, k, :], lambda k: hb_[:, k, 0:ncol], 8, [bhb, bw])
                        if stream == 0:
                            mm_group(psB[0][:, 0:ncol], bpB[0], lambda k: wkrr[:, k, :], lambda k: hb_[:, k, 0:ncol], 8, [bhb, bw])
                            evac_rope(psA[0][:, 0:ncol], bpA[0], psB[0][:, 0:ncol], bpB[0], g.kr.ap()[:, c0:c0 + ncol], ncol)
                        else:
                            evac_copy(psA[0][:, 0:ncol], bpA[0], g.kr.ap()[:, c0:c0 + ncol], ncol)
                        if stream == 0 or not last:
                            for h in range(H):
                                q = h % 2
                                mm_group(psA[q][:, 0:ncol], bpA[q], lambda k, h=h: wuq[:, k, h * 192:h * 192 + 128], lambda k: cqT[:, k, 0:ncol], 2, [blat, bw])
                                evac_copy(psA[q][:, 0:ncol], bpA[q], g.qn.ap()[h, :, c0:c0 + ncol], ncol)
                            for pr in range(4):
                                q = pr % 2
                                mm_group(psA[q][:, 0:ncol], bpA[q], lambda k, pr=pr: wuqr[:, k, pr * 128:(pr + 1) * 128], lambda k: cqT[:, k, 0:ncol], 2, [blat, bw])
                                if stream == 0:
                                    mm_group(psB[q][:, 0:ncol], bpB[q], lambda k, pr=pr: wuqrr[:, k, pr * 128:(pr + 1) * 128], lambda k: cqT[:, k, 0:ncol], 2, [blat, bw])
                                    evac_rope(psA[q][:, 0:ncol], bpA[q], psB[q][:, 0:ncol], bpB[q], g.qr.ap()[pr, :, c0:c0 + ncol], ncol)
                                else:
                                    evac_copy(psA[q][:, 0:ncol], bpA[q], g.qr.ap()[pr, :, c0:c0 + ncol], ncol)
                        for h in range(H):
                            q = h % 2
                            mm_group(psB[q][:, 0:ncol], bpB[q], lambda k, h=h: wukn[:, k, h * 128:(h + 1) * 128], lambda k: ckvT[:, k, 0:ncol], 2, [blat, bw])
                            evac_copy(psB[q][:, 0:ncol], bpB[q], g.kn.ap()[h, :, c0:c0 + ncol], ncol)
                        for i in range(nt_):
                            vq = cnt['v'] % 2; cnt['v'] += 1
                            for hf in range(2):
                                mm_group(psA[hf][:], bpA[hf], lambda k, i=i: ckvT[:, k, i * 128:(i + 1) * 128], lambda k, hf=hf: wuv[:, k, hf * 512:(hf + 1) * 512], 2, [blat, bw])
                                if hf == 0:
                                    S.op('act', lambda e, vq=vq: e.copy(out=vsb[vq][:, 0:512], in_=psA[0][:]), reads=[bpA[0]], writes=[bvs[vq]])
                                else:
                                    S.op('dve', lambda e, vq=vq: e.tensor_copy(out=vsb[vq][:, 512:1024], in_=psA[1][:]), reads=[bpA[1]], writes=[bvs[vq]])
                            S.dma(g.v.ap()[c0 // 128 + i, :, :], vsb[vq][:], reads=[bvs[vq]], key=bvs[vq])
            S.barrier()

    def phase_attn(l, last):
        with ExitStack() as st:
            ones = T(st, "at_ones", [128, 128], F32); bones = Buf()
            S.op('pool', lambda e: e.memset(ones[:], 1.0), writes=[bones])
            accP = [T(st, "at_accP%d" % i, [128, 512], F32) for i in range(2)]; baP = [Buf(), Buf()]
            accD = [T(st, "at_accD%d" % i, [128, 512], F32) for i in range(2)]; baD = [Buf(), Buf()]
            kr_sb = T(st, "at_kr", [128, LT], BF16); bkr = Buf()
            S.dma(kr_sb[:], g.kr.ap()[:, :], writes=[bkr], key=bkr)
            kn_sb = [T(st, "at_kn%d" % i, [128, LT], BF16) for i in range(2)]; bkn = [Buf(), Buf()]
            v_sb = [T(st, "at_v%d" % i, [128, NTT, 128], BF16) for i in range(2)]; bv = [Buf(), Buf()]
            qn_sb = [T(st, "at_qn%d" % i, [128, 512], BF16) for i in range(2)]; bqn = [Buf(), Buf()]
            qr_sb = [T(st, "at_qr%d" % i, [128, 512], BF16) for i in range(2)]; bqr = [Buf(), Buf()]
            pT = [T(st, "at_pT%d" % i, [128, 512], BF16) for i in range(4)]; bpT = [Buf() for _ in range(4)]
            rden = [T(st, "at_rd%d" % i, [128, 512], F32) for i in range(2)]; brd = [Buf(), Buf()]
            osb = [T(st, "at_o%d" % i, [128, 512], BF16) for i in range(2)]; bos = [Buf(), Buf()]
            ps_s = [P(st, "at_pss%d" % i, [128, 512]) for i in range(4)]; bps = [Buf() for _ in range(4)]
            ps_o = [P(st, "at_pso%d" % i, [128, 512]) for i in range(2)]; bpo = [Buf(), Buf()]
            ps_d = [P(st, "at_psd%d" % i, [128, 512]) for i in range(1)] * 2; bpd = [Buf()] * 2
            qb = 0; sc = 0
            for h in range(H):
                hq = h % 2
                hb = 64 * (h % 2)
                S.dma(kn_sb[hq][:], g.kn.ap()[h, :, :], writes=[bkn[hq]], key=bkn[hq])
                for t0 in range(0, NTT, 8):
                    t1_ = min(NTT, t0 + 8)
                    S.dma(v_sb[hq][:, t0:t1_, :], g.v.ap()[t0:t1_, :, h * 128:(h + 1) * 128].rearrange("t p c -> p t c"), writes=[bv[hq]], key=bv[hq])
                blocks = [(c0, 512, list(range(NTT))) for c0 in range(0, L, 512)]
                if not last:
                    blocks.append((L, LC, list(range(NT, NTT))))
                for (c0, ncol, kts) in blocks:
                    q2 = qb % 2; qb += 1
                    S.dma(qn_sb[q2][:, 0:ncol], g.qn.ap()[h, :, c0:c0 + ncol], writes=[bqn[q2]], key=bqn[q2])
                    S.dma(qr_sb[q2][0:64, 0:ncol], g.qr.ap()[h // 2, 64 * (h % 2):64 * (h % 2) + 64, c0:c0 + ncol], writes=[bqr[q2]], key=bqr[q2])
                    nk = len(kts)

                    def pv(idx, s3, kt):
                        S.op('pe', lambda e: e.matmul(ps_o[q2][:, 0:ncol], v_sb[hq][:, kt, :], pT[s3][:, 0:ncol], start=(idx == 0), stop=(idx == nk - 1)), reads=[bv[hq], bpT[s3]], writes=[bpo[q2]])

                    groups = [kts[i:i + 2] for i in range(0, nk, 2)]
                    pendg = None
                    idx0 = 0
                    for gi_, grp in enumerate(groups):
                        slots = [(gi_ % 2) * 2 + j for j in range(len(grp))]
                        S._wait('pe', S._deps([bkn[hq], bqn[q2], bkr, bqr[q2]], [bps[s_] for s_ in slots]))
                        for j, kt in enumerate(grp):
                            s3 = slots[j]
                            S.op('pe', lambda e, kt=kt, s3=s3: e.matmul(ps_s[s3][:, 0:ncol], kn_sb[hq][:, kt * 128:(kt + 1) * 128], qn_sb[q2][:, 0:ncol], start=True, stop=False), reads=[bkn[hq], bqn[q2]], writes=[bps[s3]])
                            S.op('pe', lambda e, kt=kt, s3=s3: e.matmul(ps_s[s3][:, 0:ncol], kr_sb[0:64, kt * 128:(kt + 1) * 128], qr_sb[q2][0:64, 0:ncol], start=False, stop=True), reads=[bkr, bqr[q2]], writes=[bps[s3]])
                        for j, kt in enumerate(grp):
                            s3 = slots[j]
                            idx = idx0 + j
                            S.op('act', lambda e, s3=s3: e.activation(out=pT[s3][:, 0:ncol], in_=ps_s[s3][:, 0:ncol], func=AF.Exp, scale=float(SCALE)), reads=[bps[s3]], writes=[bpT[s3]])
                            if idx % 2 == 0:
                                if idx < 2:
                                    S.op('pool', lambda e, s3=s3: e.tensor_copy(out=accP[q2][:, 0:ncol], in_=pT[s3][:, 0:ncol]), reads=[bpT[s3]], writes=[baP[q2]])
                                else:
                                    S.op('pool', lambda e, s3=s3: e.tensor_tensor(out=accP[q2][:, 0:ncol], in0=accP[q2][:, 0:ncol], in1=pT[s3][:, 0:ncol], op=ALU.add), reads=[bpT[s3]], writes=[baP[q2]])
                            else:
                                if idx < 2:
                                    S.op('dve', lambda e, s3=s3: e.tensor_copy(out=accD[q2][:, 0:ncol], in_=pT[s3][:, 0:ncol]), reads=[bpT[s3]], writes=[baD[q2]])
                                else:
                                    S.op('dve', lambda e, s3=s3: e.tensor_tensor(out=accD[q2][:, 0:ncol], in0=accD[q2][:, 0:ncol], in1=pT[s3][:, 0:ncol], op=ALU.add), reads=[bpT[s3]], writes=[baD[q2]])
                        if pendg is not None:
                            S._wait('pe', S._deps([bv[hq]] + [bpT[p_[1]] for p_ in pendg], [bpo[q2]]))
                            for p_ in pendg:
                                pv(*p_)
                        pendg = [(idx0 + j, slots[j], kt) for j, kt in enumerate(grp)]
                        idx0 += len(grp)
                    S._wait('pe', S._deps([bv[hq]] + [bpT[p_[1]] for p_ in pendg], [bpo[q2]]))
                    for p_ in pendg:
                        pv(*p_)
                    S.op('pe', lambda e: e.matmul(ps_d[q2][:, 0:ncol], ones[:], accP[q2][:, 0:ncol], start=True, stop=(nk < 2)), reads=[bones, baP[q2]], writes=[bpd[q2]])
                    if nk >= 2:
                        S.op('pe', lambda e: e.matmul(ps_d[q2][:, 0:ncol], ones[:], accD[q2][:, 0:ncol], start=False, stop=True), reads=[bones, baD[q2]], writes=[bpd[q2]])
                    S.op('dve', lambda e: e.reciprocal(out=rden[q2][:, 0:ncol], in_=ps_d[q2][:, 0:ncol]), reads=[bpd[q2]], writes=[brd[q2]])
                    S.op('dve', lambda e: e.tensor_tensor(out=osb[q2][:, 0:ncol], in0=ps_o[q2][:, 0:ncol], in1=rden[q2][:, 0:ncol], op=ALU.mult), reads=[bpo[q2], brd[q2]], writes=[bos[q2]])
                    S.dma(g.oT.ap()[h, :, c0:c0 + ncol], osb[q2][:, 0:ncol], reads=[bos[q2]], key=bos[q2])
            S.barrier()

    def phase_merge(l, src, dst, last):
        with ExitStack() as st:
            wglu = T(st, "mg_glu", [128, 8, D], BF16); wos5 = T(st, "mg_os5", [128, 8, D], BF16)
            womla = T(st, "mg_omla", [128, 8, D], BF16); wout = T(st, "mg_out", [128, 8, D], BF16)
            wgate = T(st, "mg_gate", [128, 8, 2 * D], BF16)
            glub = T(st, "mg_glub", [128, 8], F32)
            bw = Buf(); bgb = Buf()
            S.dma(glub[:], g.glu_b.ap()[l:l + 1, :].rearrange("o (j p) -> p (o j)", p=128), writes=[bgb], key=bgb, slow=True)
            wl = g.WLoad(st, 1024, "mg")
            for k in range(8):
                rs = slice(k * 128, (k + 1) * 128)
                wl.load(wglu[:, k, :], g.glu_w.ap()[l, rs, :], bw)
                wl.load(wos5[:, k, :], g.w_o_s5.ap()[l, rs, :], bw)
                wl.load(womla[:, k, :], g.w_o_mla.ap()[l, rs, :], bw)
                wl.load(wout[:, k, :], g.w_out.ap()[l, rs, :], bw)
                wl.load(wgate[:, k, 0:1024], g.w_in.ap()[l, rs, 1600:2624], bw)
                wl.load(wgate[:, k, 1024:2048], g.w_in.ap()[l, rs, 2624:3648], bw)
            po = g.PostNorm(st, "mg")
            hT = [T(st, "mg_hT%d" % i, [128, 8, 512], BF16) for i in range(1)] * 2; bh = [Buf()] * 2
            zT = [T(st, "mg_zT%d" % i, [128, 8, 512], BF16) for i in range(1)] * 2; bz = [Buf()] * 2
            oT = [T(st, "mg_oT%d" % i, [128, 8, 512], BF16) for i in range(1)] * 2; bo = [Buf()] * 2
            s5o = T(st, "mg_s5o", [128, 8, 512], BF16); bs5 = [Buf() for _ in range(8)]
            mrg = T(st, "mg_mrg", [128, 8, 512], BF16); bmr = [Buf() for _ in range(8)]
            sig = [T(st, "mg_sig%d" % i, [128, 512], F32) for i in range(2)]; bsig = [Buf(), Buf()]
            ta = [T(st, "mg_ta%d" % i, [128, 512], F32) for i in range(2)]; bta = [Buf(), Buf()]
            tb = [T(st, "mg_tb%d" % i, [128, 512], F32) for i in range(2)]; btb = [Buf(), Buf()]
            psA = [P(st, "mg_psA%d" % i, [128, 512]) for i in range(2)]; bpA = [Buf(), Buf()]
            psB = [P(st, "mg_psB%d" % i, [128, 512]) for i in range(2)]; bpB = [Buf(), Buf()]
            py = [[P(st, "mg_py%d_%d" % (i, hf), [128, 512]) for hf in range(2)] for i in range(2)]
            bpy = [[Buf(), Buf()] for i in range(2)]
            blk = 0; cnt = 0; sc = 0
            for stream in ([0] if last else [0, 1]):
                with ExitStack() as st2:
                    gg, bgg = g.load_post(st2, l, 1, stream, 1.0, "mg%d" % stream)
                    ntile, cbase = g.stream_info(stream)
                    for b0 in range(0, ntile, 4):
                        nt_ = min(4, ntile - b0)
                        ncol = nt_ * 128
                        c0 = cbase + b0 * 128
                        q = blk % 2; blk += 1
                        S.dma(hT[q][:, :, 0:ncol], g.hT.ap()[:, :, c0:c0 + ncol].rearrange("k p c -> p k c"), writes=[bh[q]], key=bh[q])
                        S.dma(zT[q][:, :, 0:ncol], g.zT.ap()[:, :, c0:c0 + ncol].rearrange("k p c -> p k c"), writes=[bz[q]], key=bz[q])
                        S.dma(oT[q][:, :, 0:ncol], g.oT.ap()[:, :, c0:c0 + ncol].rearrange("k p c -> p k c"), writes=[bo[q]], key=bo[q])
                        for fo in range(8):
                            a = sc % 2; sc += 1
                            for k in range(8):
                                S.op('pe', lambda e, k=k, fo=fo, a=a: e.matmul(psA[a][:, 0:ncol], wglu[:, k, fo * 128:(fo + 1) * 128], zT[q][:, k, 0:ncol], start=(k == 0), stop=(k == 7)), reads=[bw, bz[q]], writes=[bpA[a]])
                            S.op('act', lambda e, fo=fo, a=a: e.activation(out=sig[a][:, 0:ncol], in_=psA[a][:, 0:ncol], func=AF.Sigmoid, bias=glub[:, fo:fo + 1]), reads=[bpA[a], bgb], writes=[bsig[a]])
                            S.op('dve', lambda e, fo=fo, a=a: e.tensor_tensor(out=s5o[:, fo, 0:ncol], in0=zT[q][:, fo, 0:ncol], in1=sig[a][:, 0:ncol], op=ALU.mult), reads=[bz[q], bsig[a]], writes=[bs5[fo]])
                        for do in range(8):
                            a = sc % 2; sc += 1
                            cs_ = slice(do * 128, (do + 1) * 128)
                            for k in range(8):
                                S.op('pe', lambda e, k=k, a=a, cs_=cs_: e.matmul(psA[a][:, 0:ncol], wgate[:, k, 1024 + do * 128:1024 + (do + 1) * 128], hT[q][:, k, 0:ncol], start=(k == 0), stop=(k == 7)), reads=[bw, bh[q]], writes=[bpA[a]])
                            for k in range(8):
                                S.op('pe', lambda e, k=k, a=a, cs_=cs_: e.matmul(psB[a][:, 0:ncol], wos5[:, k, cs_], s5o[:, k, 0:ncol], start=(k == 0), stop=(k == 7)), reads=[bw] + bs5, writes=[bpB[a]])
                            S.op('act', lambda e, a=a: e.activation(out=sig[a][:, 0:ncol], in_=psA[a][:, 0:ncol], func=AF.Sigmoid), reads=[bpA[a]], writes=[bsig[a]])
                            S.op('dve', lambda e, a=a: e.tensor_tensor(out=ta[a][:, 0:ncol], in0=psB[a][:, 0:ncol], in1=sig[a][:, 0:ncol], op=ALU.mult), reads=[bpB[a], bsig[a]], writes=[bta[a]])
                            a2 = sc % 2; sc += 1
                            for k in range(8):
                                S.op('pe', lambda e, k=k, a2=a2: e.matmul(psA[a2][:, 0:ncol], wgate[:, k, do * 128:(do + 1) * 128], hT[q][:, k, 0:ncol], start=(k == 0), stop=(k == 7)), reads=[bw, bh[q]], writes=[bpA[a2]])
                            for k in range(8):
                                S.op('pe', lambda e, k=k, a2=a2, cs_=cs_: e.matmul(psB[a2][:, 0:ncol], womla[:, k, cs_], oT[q][:, k, 0:ncol], start=(k == 0), stop=(k == 7)), reads=[bw, bo[q]], writes=[bpB[a2]])
                            S.op('act', lambda e, a2=a2: e.activation(out=sig[a2][:, 0:ncol], in_=psA[a2][:, 0:ncol], func=AF.Sigmoid), reads=[bpA[a2]], writes=[bsig[a2]])
                            S.op('dve', lambda e, a2=a2: e.tensor_tensor(out=tb[a2][:, 0:ncol], in0=psB[a2][:, 0:ncol], in1=sig[a2][:, 0:ncol], op=ALU.mult), reads=[bpB[a2], bsig[a2]], writes=[btb[a2]])
                            S.op('pool', lambda e, a=a, a2=a2, do=do: e.tensor_tensor(out=mrg[:, do, 0:ncol], in0=ta[a][:, 0:ncol], in1=tb[a2][:, 0:ncol], op=ALU.add), reads=[bta[a], btb[a2]], writes=[bmr[do]])
                        for i in range(nt_):
                            yq = cnt % 2; cnt += 1
                            for hf in range(2):
                                for k in range(8):
                                    S.op('pe', lambda e, k=k, hf=hf, i=i, yq=yq: e.matmul(py[yq][hf][:], mrg[:, k, i * 128:(i + 1) * 128], wout[:, k, hf * 512:(hf + 1) * 512], start=(k == 0), stop=(k == 7)), reads=[bw] + bmr, writes=[bpy[yq][hf]])
                            po.run(xtile_ap(src[stream], stream, b0 + i), xtile_ap(dst[stream], stream, b0 + i), py[yq], bpy[yq], gg, bgg)
            S.barrier()

    g.sap = sap
    phase_s5 = build_s5(g)

    def run(l, src, dst, last):
        ph = g.dbg.get("mix", "iasm") if g.dbg else "iasm"
        if "i" in ph:
            phase_mixer_in(l, src, last)
        if "a" in ph:
            phase_attn(l, last)
        if "s" in ph:
            phase_s5(l, last)
        if "m" in ph:
            phase_merge(l, src, dst, last)
    return run


def build_s5(g):
    T, P, S, AP, sap = g.T, g.P, g.S, g.AP, g.sap
    L, NT, LT = g.L, g.NT, g.LT
    NKB = L // 1024
    NCH = L // 8
    NCC = LC // 8
    NS = NCH + NCC
    NSTEP = int(math.ceil(math.log2(NS)))
    HALF_PI = math.pi / 2

    def phase_s5(l, last):
        with ExitStack() as st:
            Are = T(st, "s5_are", [128, 64], F32); Aim = T(st, "s5_aim", [128, 64], F32); Dl = T(st, "s5_dl", [128, 64], F32)
            bA = Buf()
            for d in range(2):
                for h in range(2):
                    off = ((l * 2 + d) * 64 + 32 * h) * 64
                    S.dma(Are[64 * h:64 * h + 64, d * 32:(d + 1) * 32], AP(g.a_re, off, [[1, 64], [64, 32]]), writes=[bA], key=bA, slow=True)
                    S.dma(Aim[64 * h:64 * h + 64, d * 32:(d + 1) * 32], AP(g.a_im, off, [[1, 64], [64, 32]]), writes=[bA], key=bA, slow=True)
                    S.dma(Dl[64 * h:64 * h + 64, d * 32:(d + 1) * 32], AP(g.log_dt, (l * 2 + d) * 64 + 32 * h, [[0, 64], [1, 32]]), writes=[bA], key=bA, slow=True)
            S.op('act', lambda e: e.activation(out=Dl[:], in_=Dl[:], func=AF.Exp), reads=[bA], writes=[bA])
            rho = T(st, "s5_rho", [128, 64], F32); th = T(st, "s5_th", [128, 64], F32); bT = Buf()
            S.op('dve', lambda e: e.tensor_tensor(out=rho[:], in0=Are[:], in1=Dl[:], op=ALU.mult), reads=[bA], writes=[bT])
            S.op('dve', lambda e: e.tensor_tensor(out=th[:], in0=Aim[:], in1=Dl[:], op=ALU.mult), reads=[bA], writes=[bT])
            Zre = T(st, "s5_zre", [128, 128], F32); Zim = T(st, "s5_zim", [128, 128], F32)
            mg = T(st, "s5_mg", [128, 128], F32); cs = T(st, "s5_cs", [128, 128], F32); t1 = T(st, "s5_t1", [128, 128], F32); t2 = T(st, "s5_t2", [128, 128], F32)
            bZ = Buf()
            S.op('act', lambda e: e.activation(out=mg[:, 0:64], in_=rho[:], func=AF.Exp, scale=1.0 / 16), reads=[bT], writes=[bZ])
            S.op('act', lambda e: e.activation(out=mg[:, 64:128], in_=rho[:], func=AF.Exp, scale=-1.0 / 16), reads=[bT], writes=[bZ])
            S.op('dve', lambda e: e.tensor_scalar(out=t1[:, 0:64], in0=th[:], scalar1=1.0 / 16, scalar2=HALF_PI, op0=ALU.mult, op1=ALU.add), reads=[bT], writes=[bZ])
            S.op('act', lambda e: e.activation(out=cs[:, 0:64], in_=t1[:, 0:64], func=AF.Sin), reads=[bZ], writes=[bZ])
            S.op('act', lambda e: e.activation(out=cs[:, 64:128], in_=th[:], func=AF.Sin, scale=1.0 / 16), reads=[bT, bZ], writes=[bZ])
            for kind in range(2):
                S.op('dve', lambda e, kind=kind: e.tensor_tensor(out=Zre[:, kind * 64:(kind + 1) * 64], in0=mg[:, kind * 64:(kind + 1) * 64], in1=cs[:, 0:64], op=ALU.mult), reads=[bZ], writes=[bZ])
                S.op('dve', lambda e, kind=kind: e.tensor_tensor(out=Zim[:, kind * 64:(kind + 1) * 64], in0=mg[:, kind * 64:(kind + 1) * 64], in1=cs[:, 64:128], op=ALU.mult), reads=[bZ], writes=[bZ])

            def csq(ore, oim, ire, iim, rd, wr):
                S.op('dve', lambda e: e.tensor_tensor(out=t1[:, 0:ire.shape[1]], in0=ire, in1=ire, op=ALU.mult), reads=rd, writes=[bZ])
                S.op('dve', lambda e: e.tensor_tensor(out=t2[:, 0:ire.shape[1]], in0=iim, in1=iim, op=ALU.mult), reads=rd, writes=[bZ])
                S.op('dve', lambda e: e.scalar_tensor_tensor(out=oim, in0=ire, scalar=2.0, in1=iim, op0=ALU.mult, op1=ALU.mult), reads=rd + [bZ], writes=wr)
                S.op('dve', lambda e: e.tensor_tensor(out=ore, in0=t1[:, 0:ire.shape[1]], in1=t2[:, 0:ire.shape[1]], op=ALU.subtract), reads=[bZ] + rd, writes=wr)

            for _ in range(4):
                csq(Zre[:], Zim[:], Zre[:], Zim[:], [bZ], [bZ])
            PWre = T(st, "s5_pwre", [128, 9, 128], F32); PWim = T(st, "s5_pwim", [128, 9, 128], F32)
            PRre = T(st, "s5_prre", [128, 9, 128], F32); PRim = T(st, "s5_prim", [128, 9, 128], F32)
            bP = Buf()
            S.op('pool', lambda e: e.memset(PWre[:, 0, :], 1.0), writes=[bP])
            S.op('pool', lambda e: e.memset(PWim[:, 0, :], 0.0), writes=[bP])
            S.op('dve', lambda e: e.tensor_copy(out=PWre[:, 1, :], in_=Zre[:]), reads=[bZ], writes=[bP])
            S.op('dve', lambda e: e.tensor_copy(out=PWim[:, 1, :], in_=Zim[:]), reads=[bZ], writes=[bP])
            for m in range(2, 9):
                S.op('dve', lambda e, m=m: e.tensor_tensor(out=t1[:], in0=PWre[:, m - 1, :], in1=Zre[:], op=ALU.mult), reads=[bP, bZ], writes=[bZ])
                S.op('dve', lambda e, m=m: e.tensor_tensor(out=t2[:], in0=PWim[:, m - 1, :], in1=Zim[:], op=ALU.mult), reads=[bP, bZ], writes=[bZ])
                S.op('dve', lambda e, m=m: e.tensor_tensor(out=PWre[:, m, :], in0=t1[:], in1=t2[:], op=ALU.subtract), reads=[bZ], writes=[bP])
                S.op('dve', lambda e, m=m: e.tensor_tensor(out=t1[:], in0=PWre[:, m - 1, :], in1=Zim[:], op=ALU.mult), reads=[bP, bZ], writes=[bZ])
                S.op('dve', lambda e, m=m: e.tensor_tensor(out=t2[:], in0=PWim[:, m - 1, :], in1=Zre[:], op=ALU.mult), reads=[bP, bZ], writes=[bZ])
                S.op('dve', lambda e, m=m: e.tensor_tensor(out=PWim[:, m, :], in0=t1[:], in1=t2[:], op=ALU.add), reads=[bZ], writes=[bP])
            for m in range(9):
                S.op('pool', lambda e, m=m: e.tensor_copy(out=PRre[:, m, :], in_=PWre[:, 8 - m, :]), reads=[bP], writes=[bP])
                S.op('pool', lambda e, m=m: e.tensor_copy(out=PRim[:, m, :], in_=PWim[:, 8 - m, :]), reads=[bP], writes=[bP])
            KPre = T(st, "s5_kpre", [128, NSTEP, 64], F32); KPim = T(st, "s5_kpim", [128, NSTEP, 64], F32); KPnim = T(st, "s5_kpnim", [128, NSTEP, 64], F32)
            bK = Buf()
            S.op('dve', lambda e: e.tensor_copy(out=KPre[:, 0, :], in_=PWre[:, 8, 0:64]), reads=[bP], writes=[bK])
            S.op('dve', lambda e: e.tensor_copy(out=KPim[:, 0, :], in_=PWim[:, 8, 0:64]), reads=[bP], writes=[bK])
            for j in range(1, NSTEP):
                csq(KPre[:, j, :], KPim[:, j, :], KPre[:, j - 1, :], KPim[:, j - 1, :], [bK], [bK])
            S.op('dve', lambda e: e.tensor_scalar(out=KPnim[:], in0=KPim[:], scalar1=-1.0, scalar2=None, op0=ALU.mult), reads=[bK], writes=[bK])
            cre = T(st, "s5_cre", [128, 64], F32); cim = T(st, "s5_cim", [128, 64], F32); bC = Buf()
            xr = t1[:, 0:64]; den = t1[:, 64:128]; u1 = t2[:, 0:64]; u2 = t2[:, 64:128]
            S.op('dve', lambda e: e.tensor_scalar(out=xr, in0=PWre[:, 1, 0:64], scalar1=-1.0, scalar2=None, op0=ALU.add), reads=[bP], writes=[bZ])
            S.op('dve', lambda e: e.tensor_tensor(out=den, in0=Are[:], in1=Are[:], op=ALU.mult), reads=[bA], writes=[bZ])
            S.op('dve', lambda e: e.tensor_tensor(out=u1, in0=Aim[:], in1=Aim[:], op=ALU.mult), reads=[bA], writes=[bZ])
            S.op('dve', lambda e: e.tensor_tensor(out=den, in0=den, in1=u1, op=ALU.add), reads=[bZ], writes=[bZ])
            S.op('dve', lambda e: e.reciprocal(out=den, in_=den), reads=[bZ], writes=[bZ])
            S.op('dve', lambda e: e.tensor_tensor(out=u1, in0=xr, in1=Are[:], op=ALU.mult), reads=[bZ, bA], writes=[bZ])
            S.op('dve', lambda e: e.tensor_tensor(out=u2, in0=PWim[:, 1, 0:64], in1=Aim[:], op=ALU.mult), reads=[bP, bA], writes=[bZ])
            S.op('dve', lambda e: e.tensor_tensor(out=u1, in0=u1, in1=u2, op=ALU.add), reads=[bZ], writes=[bZ])
            S.op('dve', lambda e: e.tensor_tensor(out=cre[:], in0=u1, in1=den, op=ALU.mult), reads=[bZ], writes=[bC])
            S.op('dve', lambda e: e.tensor_tensor(out=u1, in0=PWim[:, 1, 0:64], in1=Are[:], op=ALU.mult), reads=[bP, bA, bC], writes=[bZ])
            S.op('dve', lambda e: e.tensor_tensor(out=u2, in0=xr, in1=Aim[:], op=ALU.mult), reads=[bZ, bA], writes=[bZ])
            S.op('dve', lambda e: e.tensor_tensor(out=u1, in0=u1, in1=u2, op=ALU.subtract), reads=[bZ], writes=[bZ])
            S.op('dve', lambda e: e.tensor_tensor(out=cim[:], in0=u1, in1=den, op=ALU.mult), reads=[bZ], writes=[bC])
            Bre = T(st, "s5_bre", [128, 1024], F32); Bim = T(st, "s5_bim", [128, 1024], F32); bB = Buf()
            Bbre = T(st, "s5_bbre", [128, 1024], F32); Bbim = T(st, "s5_bbim", [128, 1024], F32); bBb = Buf()
            tb1 = T(st, "s5_tb1", [128, 1024], F32); tb2 = T(st, "s5_tb2", [128, 1024], F32); bt = Buf()
            for d in range(2):
                for h in range(2):
                    off = ((l * 2 + d) * 64 + 32 * h) * 1024
                    S.dma(sap(Bre, 1024, 64 * h * 1024 + d * 512, [16, 32], [1, 16], parts=64), AP(g.b_re, off, [[16, 64], [1024, 32], [1, 16]]), writes=[bB], key=bB)
                    S.dma(sap(Bim, 1024, 64 * h * 1024 + d * 512, [16, 32], [1, 16], parts=64), AP(g.b_im, off, [[16, 64], [1024, 32], [1, 16]]), writes=[bB], key=bB)
            cre_b = sap(cre, 64, 0, [1, 64], [0, 16]); cim_b = sap(cim, 64, 0, [1, 64], [0, 16])
            B3 = lambda t_: sap(t_, 1024, 0, [16, 64], [1, 16])
            S.op('dve', lambda e: e.tensor_tensor(out=B3(tb1), in0=B3(Bre), in1=cre_b, op=ALU.mult), reads=[bB, bC], writes=[bt])
            S.op('dve', lambda e: e.tensor_tensor(out=B3(tb2), in0=B3(Bim), in1=cim_b, op=ALU.mult), reads=[bB, bC], writes=[bt])
            S.op('dve', lambda e: e.tensor_tensor(out=Bbre[:], in0=tb1[:], in1=tb2[:], op=ALU.subtract), reads=[bt], writes=[bBb])
            S.op('dve', lambda e: e.tensor_tensor(out=B3(tb1), in0=B3(Bim), in1=cre_b, op=ALU.mult), reads=[bB, bC, bBb], writes=[bt])
            S.op('dve', lambda e: e.tensor_tensor(out=B3(tb2), in0=B3(Bre), in1=cim_b, op=ALU.mult), reads=[bB, bC], writes=[bt])
            S.op('dve', lambda e: e.tensor_tensor(out=Bbim[:], in0=tb1[:], in1=tb2[:], op=ALU.add), reads=[bt], writes=[bBb])
            Cre, Cim = Bre, Bim
            craw = [T(st, "s5_craw%d" % i, [128, 128], F32) for i in range(2)]; bcr = [Buf(), Buf()]
            psC_full = P(st, "s5_psC", [128, 512]); psC = psC_full[:, 0:128]; bpc = Buf()
            ci = 0
            for d in range(2):
                for (src_t, dstC) in ((g.c_re, Cre), (g.c_im, Cim)):
                    for pb in range(4):
                        q = ci % 2; ci += 1
                        S.dma(sap(craw[q], 128, 0, [64, 2], [1, 64]), AP(src_t, ((l * 2 + d) * 64 + 8 * pb) * 1024, [[64, 128], [32 * 1024, 2], [1, 64]]), writes=[bcr[q]], key=bcr[q])
                        S.op('pe', lambda e, q=q: e.matmul(psC, craw[q][:], g.identf_sb[:], start=True, stop=True), reads=[bcr[q], g.b_id], writes=[bpc])
                        S.op('act', lambda e, d=d, pb=pb, dstC=dstC: e.copy(out=dstC[:, d * 512 + pb * 128: d * 512 + (pb + 1) * 128], in_=psC), reads=[bpc, bBb], writes=[bB])
            dcol = T(st, "s5_dcol", [128, 64], F32); bD = Buf()
            for tau in range(8):
                S.dma(dcol[16 * tau:16 * tau + 16, :], AP(g.s5_d, l * 1024, [[1, 16], [16, 64]]), writes=[bD], key=bD, slow=True)
            mkl = T(st, "s5_mkl", [128, 128], F32); mku = T(st, "s5_mku", [128, 128], F32); bM = Buf()
            S.dma(mkl[:], g.maskl.ap()[:, :], writes=[bM], key=bM)
            S.dma(mku[:], g.masku.ap()[:, :], writes=[bM], key=bM)

            U = T(st, "s5_U", [128, 32, NS], BF16); bU = [Buf() for _ in range(32)]
            zsb = T(st, "s5_z", [128, NKB + 1, 8, 256], BF16); bz = Buf()
            usb = T(st, "s5_usb", [128, 32, 8, 16], BF16); bus = Buf()
            winu = T(st, "s5_winu", [128, 8, 512], BF16); bwu = Buf()
            hblk = [T(st, "s5_hb%d" % i, [128, 8, 512], BF16) for i in range(1)] * 2; bhb = [Buf()] * 2
            fam = {}
            for nm in ("L", "LT", "R"):
                for d in range(2):
                    for ri in range(2):
                        fam[(nm, d, ri)] = T(st, "s5_f%s%d%d" % (nm, d, ri), [128, 8, 128], BF16)
            bfam = Buf()
            ftmp = [tb1, tb2]; bft = bt
            WS = {}; bWS = Buf()
            for d in range(2):
                for ri in range(2):
                    for h in range(2):
                        WS[(d, ri, h)] = T(st, "s5_ws%d%d%d" % (d, ri, h), [128, 128], BF16)
                        S.op('pool', lambda e, d=d, ri=ri, h=h: e.memset(WS[(d, ri, h)][:], 0.0), writes=[bWS])
            PAD = {}; bPAD = Buf()
            for nm in ("LT", "R"):
                for d in range(2):
                    for ri in range(2):
                        for h in range(2):
                            PAD[(nm, d, ri, h)] = T(st, "s5_pd%s%d%d%d" % (nm, d, ri, h), [128, 128], BF16)
                            S.op('pool', lambda e, k_=(nm, d, ri, h): e.memset(PAD[k_][:], 0.0), writes=[bPAD])
            toep = [T(st, "s5_tp%d" % h, [128, 128], BF16) for h in range(2)]; btp = [Buf(), Buf()]
            tpf = [T(st, "s5_tpf%d" % i, [128, 128], F32) for i in range(2)]; btf = Buf()
            XA = {}; XB = {}; bX = {}
            for d in range(2):
                for ri in range(2):
                    XA[(d, ri)] = T(st, "s5_xa%d%d" % (d, ri), [128, NS], F32)
                    XB[(d, ri)] = T(st, "s5_xb%d%d" % (d, ri), [128, NS], F32)
                    bX[(d, ri, 0)] = Buf(); bX[(d, ri, 1)] = Buf()
            Xf = {}; bXf = Buf()
            for d in range(2):
                for ri in range(2):
                    Xf[(d, ri)] = T(st, "s5_xf%d%d" % (d, ri), [128, NS + 1], BF16)
                    S.op('pool', lambda e, d=d, ri=ri: e.memset(Xf[(d, ri)][:], 0.0), writes=[bXf])
            gx = T(st, "s5_gx", [128, NKB * 128], F32); gt = T(st, "s5_gt", [128, NKB * 128], F32); gsg = T(st, "s5_gsg", [128, NKB * 128], F32); bg_ = Buf()
            gxc = T(st, "s5_gxc", [32, 128], F32); gtc = T(st, "s5_gtc", [32, 128], F32); gsc = T(st, "s5_gsc", [32, 128], F32); bgc = Buf()
            zo = [T(st, "s5_zo%d" % i, [128, 1024], BF16) for i in range(2)]; bzo = [Buf(), Buf()]
            zc = T(st, "s5_zc", [128, 256], BF16); bzc = Buf()
            ps_u = P(st, "s5_psu", [128, 512]); bpu = Buf()
            ps_t = [P(st, "s5_pst%d" % i, [128, 8, 128], BF16) for i in range(2)]; bpt = [Buf(), Buf()]
            ps_S = [P(st, "s5_psS%d" % i, [128, 512]) for i in range(2)]; bpS = [Buf(), Buf()]
            ps_TS = P(st, "s5_psTS", [128, 512]); bpSc = Buf(); bpT = bpSc
            ps_y = P(st, "s5_psy", [128, 4, 128]); bpy = Buf()
            ps_yc = psC_full
            wl = g.WLoad(st, 256, "s5", n=2)

            def cprod(out_re, out_im, pre, pim, xre, xim, conj, negim):
                a, b = ftmp[0], ftmp[1]
                A4 = sap(a, 1024, 0, [128, 8], [16, 8], [1, 16]); B4 = sap(b, 1024, 0, [128, 8], [16, 8], [1, 16])
                S.op('dve', lambda e: e.tensor_tensor(out=A4, in0=xre, in1=pre, op=ALU.mult), reads=[bP, bB, bBb], writes=[bft])
                S.op('dve', lambda e: e.tensor_tensor(out=B4, in0=xim, in1=pim, op=ALU.mult), reads=[bP, bB, bBb], writes=[bft])
                S.op('dve', lambda e: e.tensor_tensor(out=out_re, in0=A4, in1=B4, op=(ALU.add if conj else ALU.subtract)), reads=[bft], writes=[bfam])
                S.op('dve', lambda e: e.tensor_tensor(out=A4, in0=xim, in1=pre, op=ALU.mult), reads=[bP, bB, bBb, bfam], writes=[bft])
                S.op('dve', lambda e: e.tensor_tensor(out=B4, in0=xre, in1=pim, op=ALU.mult), reads=[bP, bB, bBb], writes=[bft])
                if not negim:
                    S.op('dve', lambda e: e.tensor_tensor(out=out_im, in0=A4, in1=B4, op=(ALU.subtract if conj else ALU.add)), reads=[bft], writes=[bfam])
                else:
                    if conj:
                        S.op('dve', lambda e: e.tensor_tensor(out=out_im, in0=B4, in1=A4, op=ALU.subtract), reads=[bft], writes=[bfam])
                    else:
                        S.op('dve', lambda e: e.scalar_tensor_tensor(out=out_im, in0=A4, scalar=-1.0, in1=B4, op0=ALU.mult, op1=ALU.subtract), reads=[bft], writes=[bfam])

            def ptab(tre, tim, m0, kind, d, p0):
                off = m0 * 128 + kind * 64 + d * 32 + p0
                return sap(tre, 9 * 128, off, [1, 8], [128, 8], [0, 16]), sap(tim, 9 * 128, off, [1, 8], [128, 8], [0, 16])

            def xtab(tre, tim, d, p0):
                off = d * 512 + p0 * 16
                return sap(tre, 1024, off, [16, 8], [0, 8], [1, 16]), sap(tim, 1024, off, [16, 8], [0, 8], [1, 16])

            F4 = lambda t_: sap(t_, 1024, 0, [128, 8], [16, 8], [1, 16])

            zi = 0
            s5stage = (g.dbg or {}).get("s5stage", 9)
            for hp in range(2 if s5stage >= 1 else 0):
                for k in range(8):
                    wl.load(winu[:, k, 0:256], g.w_in.ap()[l, k * 128:(k + 1) * 128, 576 + 256 * hp:576 + 256 * hp + 256], bwu)
                    wl.load(winu[:, k, 256:512], g.w_in.ap()[l, k * 128:(k + 1) * 128, 576 + 512 + 256 * hp:576 + 512 + 256 * hp + 256], bwu)
                nb = 0
                blocks = [(0, kb, half) for kb in range(NKB) for half in range(2)] + [(1, 0, 0)]
                for (stream, kb, half) in blocks:
                    hb_, bh_ = hblk[nb % 2], bhb[nb % 2]; nb += 1
                    if stream == 0:
                        c0 = kb * 1024 + half * 512
                        S.dma(hb_[:], g.hT.ap()[:, :, c0:c0 + 512].rearrange("k p c -> p k c"), writes=[bh_], key=bh_)
                        for t4 in range(4):
                            tau = half * 4 + t4
                            for k in range(8):
                                S.op('pe', lambda e, k=k, t4=t4: e.matmul(ps_u[:], hb_[:, k, t4 * 128:(t4 + 1) * 128], winu[:, k, :], start=(k == 0), stop=(k == 7)), reads=[bh_, bwu], writes=[bpu])
                            S.op('act', lambda e, tau=tau: e.copy(out=sap(usb, 4096, tau * 16, [128, 32], [1, 16]), in_=sap(ps_u, 512, 0, [16, 32], [1, 16])), reads=[bpu], writes=[bus])
                        if half == 1:
                            for g8 in range(4):
                                q = g8 % 2
                                for gi in range(8):
                                    gl = g8 * 8 + gi
                                    S.op('pe', lambda e, gl=gl, gi=gi, q=q: e.transpose(ps_t[q][:, gi, :], sap(usb, 4096, gl * 128, [1, 128]), g.ident[:]), reads=[bus, g.b_id], writes=[bpt[q]])
                                S.op('dve', lambda e, g8=g8, q=q, kb=kb: e.tensor_copy(out=U[:, g8 * 8:(g8 + 1) * 8, kb * 128:(kb + 1) * 128], in_=ps_t[q][:]), reads=[bpt[q]], writes=bU[g8 * 8:(g8 + 1) * 8])
                    else:
                        S.dma(hb_[:, :, 0:256], g.hT.ap()[:, :, L:L + 256].rearrange("k p c -> p k c"), writes=[bh_], key=bh_)
                        for tau in range(8):
                            for k in range(8):
                                S.op('pe', lambda e, k=k, tau=tau: e.matmul(ps_u[0:32, :], sap(hb_, 4096, k * 512 + tau, [8, 32]), winu[:, k, :], start=(k == 0), stop=(k == 7)), reads=[bh_, bwu], writes=[bpu])
                            S.op('act', lambda e, tau=tau: e.copy(out=sap(usb, 4096, tau * 16, [128, 32], [1, 16], parts=32), in_=sap(ps_u, 512, 0, [16, 32], [1, 16], parts=32)), reads=[bpu], writes=[bus])
                        for g8 in range(4):
                            q = g8 % 2
                            for gi in range(8):
                                gl = g8 * 8 + gi
                                S.op('pe', lambda e, gl=gl, gi=gi, q=q: e.transpose(ps_t[q][:, gi, 0:32], sap(usb, 4096, gl * 128, [1, 128], parts=32), g.ident[0:32, 0:32]), reads=[bus, g.b_id], writes=[bpt[q]])
                            S.op('dve', lambda e, g8=g8, q=q: e.tensor_copy(out=U[:, g8 * 8:(g8 + 1) * 8, NCH:NCH + NCC], in_=ps_t[q][:, :, 0:32]), reads=[bpt[q]], writes=bU[g8 * 8:(g8 + 1) * 8])

                for qp in range(2 if s5stage >= 2 else 0):
                    p0 = 16 * hp + 8 * qp
                    for d in range(2):
                        bre_, bim_ = xtab(Bbre, Bbim, d, p0)
                        cre_, cim_ = xtab(Cre, Cim, d, p0)
                        if d == 0:
                            pL = ptab(PRre, PRim, 1, 0, d, p0)
                            pLT = ptab(PWre, PWim, 1, 1, d, p0)
                            pR = ptab(PWre, PWim, 1, 0, d, p0)
                        else:
                            pL = ptab(PWre, PWim, 0, 0, d, p0)
                            pLT = ptab(PRre, PRim, 0, 1, d, p0)
                            pR = ptab(PRre, PRim, 0, 0, d, p0)
                        cprod(F4(fam[("L", d, 0)]), F4(fam[("L", d, 1)]), pL[0], pL[1], bre_, bim_, False, False)
                        cprod(F4(fam[("LT", d, 0)]), F4(fam[("LT", d, 1)]), pLT[0], pLT[1], bre_, bim_, True, False)
                        cprod(F4(fam[("R", d, 0)]), F4(fam[("R", d, 1)]), pR[0], pR[1], cre_, cim_, False, True)
                    for i in range(8 if s5stage >= 3 else 0):
                        p = p0 + i
                        pl = 8 * qp + i
                        for d in range(2):
                            for ri in range(2):
                                S.op('pe', lambda e, d=d, ri=ri, i=i: e.transpose(ps_t[0][:, d * 2 + ri, :], fam[("L", d, ri)][:, i, :], g.ident[:]), reads=[bfam, g.b_id], writes=[bpt[0]])
                        for d in range(2):
                            for ri in range(2):
                                for h in range(2):
                                    S.op('act', lambda e, d=d, ri=ri, h=h: e.copy(out=WS[(d, ri, h)][:, 64 * h:64 * h + 64], in_=ps_t[0][:, d * 2 + ri, 64 * h:64 * h + 64]), reads=[bpt[0]], writes=[bWS])
                                    for nm in ("LT", "R"):
                                        S.op('pool', lambda e, d=d, ri=ri, h=h, nm=nm, i=i: e.tensor_copy(out=PAD[(nm, d, ri, h)][64 * h:64 * h + 64, :], in_=fam[(nm, d, ri)][64 * h:64 * h + 64, i, :]), reads=[bfam], writes=[bPAD])
                        s5sub = (g.dbg or {}).get("s5sub", 63)
                        for d in range(2 if s5sub & 1 else 0):
                            for ri in range(2):
                                q = ri
                                for h in range(2):
                                    gl = h * 16 + pl
                                    S.op('pe', lambda e, d=d, ri=ri, h=h, gl=gl, q=q: e.matmul(ps_S[q][:, 0:NCH], WS[(d, ri, h)][:], U[:, gl, 0:NCH], start=(h == 0), stop=(h == 1)), reads=[bWS, bU[gl]], writes=[bpS[q]])
                                for h in range(2):
                                    gl = h * 16 + pl
                                    S.op('pe', lambda e, d=d, ri=ri, h=h, gl=gl: e.matmul(ps_TS[:, 256 + (d * 2 + ri) * 32:256 + (d * 2 + ri + 1) * 32], WS[(d, ri, h)][:], U[:, gl, NCH:NS], start=(h == 0), stop=(h == 1)), reads=[bWS, bU[gl]], writes=[bpSc])
                                xoff = NCC if d == 0 else 0
                                coff = 0 if d == 0 else NCH
                                S.op('act', lambda e, d=d, ri=ri, q=q, xoff=xoff: e.copy(out=XA[(d, ri)][:, xoff:xoff + NCH], in_=ps_S[q][:, 0:NCH]), reads=[bpS[q]], writes=[bX[(d, ri, 0)]])
                                S.op('act', lambda e, d=d, ri=ri, coff=coff: e.copy(out=XA[(d, ri)][:, coff:coff + NCC], in_=ps_TS[:, 256 + (d * 2 + ri) * 32:256 + (d * 2 + ri + 1) * 32]), reads=[bpSc], writes=[bX[(d, ri, 0)]])
                        for d in range(2 if s5sub & 2 else 0):
                            cur = 0
                            for j in range(NSTEP):
                                sft = 1 << j
                                if sft >= NS:
                                    break
                                src = (XA, XB)[cur]; dstt = (XB, XA)[cur]
                                ar = KPre[:, j, d * 32 + p:d * 32 + p + 1]; ai = KPim[:, j, d * 32 + p:d * 32 + p + 1]; nai = KPnim[:, j, d * 32 + p:d * 32 + p + 1]
                                if d == 0:
                                    lo, hi = slice(sft, NS), slice(0, NS - sft)
                                    keep = slice(0, sft)
                                else:
                                    lo, hi = slice(0, NS - sft), slice(sft, NS)
                                    keep = slice(NS - sft, NS)
                                rds = [bX[(d, 0, cur)], bX[(d, 1, cur)], bK]
                                S.op('dve', lambda e, src=src, dstt=dstt, lo=lo, hi=hi, ar=ar: e.scalar_tensor_tensor(out=dstt[(d, 0)][:, lo], in0=src[(d, 0)][:, hi], scalar=ar, in1=src[(d, 0)][:, lo], op0=ALU.mult, op1=ALU.add), reads=rds, writes=[bX[(d, 0, 1 - cur)]])
                                S.op('dve', lambda e, src=src, dstt=dstt, lo=lo, hi=hi, nai=nai: e.scalar_tensor_tensor(out=dstt[(d, 0)][:, lo], in0=src[(d, 1)][:, hi], scalar=nai, in1=dstt[(d, 0)][:, lo], op0=ALU.mult, op1=ALU.add), reads=rds, writes=[bX[(d, 0, 1 - cur)]])
                                S.op('dve', lambda e, src=src, dstt=dstt, lo=lo, hi=hi, ai=ai: e.scalar_tensor_tensor(out=dstt[(d, 1)][:, lo], in0=src[(d, 0)][:, hi], scalar=ai, in1=src[(d, 1)][:, lo], op0=ALU.mult, op1=ALU.add), reads=rds, writes=[bX[(d, 1, 1 - cur)]])
                                S.op('dve', lambda e, src=src, dstt=dstt, lo=lo, hi=hi, ar=ar: e.scalar_tensor_tensor(out=dstt[(d, 1)][:, lo], in0=src[(d, 1)][:, hi], scalar=ar, in1=dstt[(d, 1)][:, lo], op0=ALU.mult, op1=ALU.add), reads=rds, writes=[bX[(d, 1, 1 - cur)]])
                                for ri in range(2):
                                    S.op('act', lambda e, src=src, dstt=dstt, keep=keep, ri=ri: e.copy(out=dstt[(d, ri)][:, keep], in_=src[(d, ri)][:, keep]), reads=[bX[(d, ri, cur)]], writes=[bX[(d, ri, 1 - cur)]])
                                cur = 1 - cur
                            fin = (XA, XB)[cur]
                            for ri in range(2):
                                doff = 1 if d == 0 else 0
                                S.op('act', lambda e, ri=ri, fin=fin, doff=doff: e.copy(out=Xf[(d, ri)][:, doff:doff + NS], in_=fin[(d, ri)][:]), reads=[bX[(d, ri, cur)]], writes=[bXf])
                        for h in range(2 if s5sub & 4 else 0):
                            gl = h * 16 + pl
                            gglob = 32 * h + p
                            for d in range(2):
                                S.op('pe', lambda e, d=d, h=h: e.matmul(ps_TS[:, d * 128:(d + 1) * 128], PAD[("LT", d, 0, h)][:], PAD[("R", d, 0, h)][:], start=True, stop=False), reads=[bPAD], writes=[bpT])
                                S.op('pe', lambda e, d=d, h=h: e.matmul(ps_TS[:, d * 128:(d + 1) * 128], PAD[("LT", d, 1, h)][:], PAD[("R", d, 1, h)][:], start=False, stop=True), reads=[bPAD], writes=[bpT])
                            S.op('dve', lambda e: e.tensor_tensor(out=tpf[0][:], in0=ps_TS[:, 0:128], in1=mkl[:], op=ALU.mult), reads=[bpT, bM], writes=[btf])
                            S.op('dve', lambda e: e.tensor_tensor(out=tpf[1][:], in0=ps_TS[:, 128:256], in1=mku[:], op=ALU.mult), reads=[bpT, bM], writes=[btf])
                            S.op('dve', lambda e: e.tensor_tensor(out=tpf[0][:], in0=tpf[0][:], in1=tpf[1][:], op=ALU.add), reads=[btf], writes=[btf])
                            S.op('dve', lambda e, h=h, gglob=gglob: e.scalar_tensor_tensor(out=toep[h][:], in0=g.identf_sb[:], scalar=dcol[:, gglob:gglob + 1], in1=tpf[0][:], op0=ALU.mult, op1=ALU.add), reads=[btf, bD, g.b_id], writes=[btp[h]])
                            for kb in range(NKB if s5sub & 8 else 0):
                                mm = [(U[:, gl, kb * 128:(kb + 1) * 128], toep[h][:], [bU[gl], btp[h]])]
                                for ri in range(2):
                                    mm.append((Xf[(0, ri)][:, NCC + kb * 128:NCC + (kb + 1) * 128], PAD[("R", 0, ri, h)][:], [bXf, bPAD]))
                                    mm.append((Xf[(1, ri)][:, kb * 128 + 1:(kb + 1) * 128 + 1], PAD[("R", 1, ri, h)][:], [bXf, bPAD]))
                                for mi, (lh, rh, rd) in enumerate(mm):
                                    S.op('pe', lambda e, lh=lh, rh=rh, mi=mi, kb=kb: e.matmul(ps_y[:, kb, :], lh, rh, start=(mi == 0), stop=(mi == len(mm) - 1)), reads=rd, writes=[bpy])
                            mm = [(U[:, gl, NCH:NS], toep[h][:], [bU[gl], btp[h]])]
                            for ri in range(2):
                                mm.append((Xf[(0, ri)][:, 0:NCC], PAD[("R", 0, ri, h)][:], [bXf, bPAD]))
                                mm.append((Xf[(1, ri)][:, NCH + 1:NS + 1], PAD[("R", 1, ri, h)][:], [bXf, bPAD]))
                            for mi, (lh, rh, rd) in enumerate(mm if s5sub & 16 else []):
                                S.op('pe', lambda e, lh=lh, rh=rh, mi=mi: e.matmul(ps_yc[0:32, 0:128], lh, rh, start=(mi == 0), stop=(mi == len(mm) - 1)), reads=rd, writes=[bpc])
                            for (ps_ap, x_, t_, s_, bb, zdst, bps_) in (
                                (sap(ps_y, 512, 0, [1, NKB * 128]), gx[:], gt[:], gsg[:], bg_, sap(zsb, (NKB + 1) * 2048, h * 128 + i * 16, [2048, NKB], [256, 8], [1, 16]), bpy),
                                (sap(ps_yc, 512, 0, [1, 128], parts=32), gxc[:], gtc[:], gsc[:], bgc, sap(zsb, (NKB + 1) * 2048, NKB * 2048 + h * 128 + i * 16, [256, 8], [1, 16], parts=32), bpc))[0:(2 if s5sub & 32 else 0)]:
                                glen = (g.dbg or {}).get("glen", 9)
                                S.op('act', lambda e: e.copy(out=x_, in_=ps_ap), reads=[bps_], writes=[bb])
                                if g.dbg and "dbg_gx" in g.dbg and bb is bg_:
                                    S.dma(g.dbg_gx.ap()[gl + 32 * hp, :, :], gx[:], reads=[bb], key=bb)
                                if glen < 2: continue
                                S.op('dve', lambda e: e.tensor_tensor(out=t_, in0=x_, in1=x_, op=ALU.mult), reads=[bb], writes=[bb])
                                if glen < 3: continue
                                S.op('dve', lambda e: e.tensor_scalar(out=t_, in0=t_, scalar1=0.044715, scalar2=1.0, op0=ALU.mult, op1=ALU.add), reads=[bb], writes=[bb])
                                S.op('dve', lambda e: e.tensor_tensor(out=t_, in0=t_, in1=x_, op=ALU.mult), reads=[bb], writes=[bb])
                                S.op('dve', lambda e: e.tensor_scalar(out=t_, in0=t_, scalar1=1.5957691216057308, scalar2=30.0, op0=ALU.mult, op1=ALU.min), reads=[bb], writes=[bb])
                                S.op('dve', lambda e: e.tensor_scalar(out=t_, in0=t_, scalar1=-30.0, scalar2=None, op0=ALU.max), reads=[bb], writes=[bb])
                                if glen < 4: continue
                                S.op('act', lambda e: e.activation(out=s_, in_=t_, func=AF.Exp, scale=-1.0), reads=[bb], writes=[bb])
                                S.op('dve', lambda e: e.tensor_scalar(out=s_, in0=s_, scalar1=1.0, scalar2=None, op0=ALU.add), reads=[bb], writes=[bb])
                                S.op('dve', lambda e: e.reciprocal(out=s_, in_=s_), reads=[bb], writes=[bb])
                                if glen < 5: continue
                                if bb is bg_:
                                    for kb in range(NKB):
                                        S.op('dve', lambda e, kb=kb: e.tensor_tensor(out=sap(zsb, (NKB + 1) * 2048, kb * 2048 + h * 128 + i * 16, [256, 8], [1, 16]), in0=sap(gx, NKB * 128, kb * 128, [16, 8], [1, 16]), in1=sap(gsg, NKB * 128, kb * 128, [16, 8], [1, 16]), op=ALU.mult), reads=[bb], writes=[bz])
                                else:
                                    S.op('dve', lambda e: e.tensor_tensor(out=zdst, in0=sap(gxc, 128, 0, [16, 8], [1, 16], parts=32), in1=sap(gsc, 128, 0, [16, 8], [1, 16], parts=32), op=ALU.mult), reads=[bb], writes=[bz])
                    for jf in range(2):
                        chunk = 2 * hp + qp + 4 * jf
                        for kb in range(NKB):
                            q = zi % 2; zi += 1
                            for tau in range(8):
                                S.op('pe', lambda e, tau=tau, kb=kb, jf=jf, q=q: e.transpose(ps_t[q][:, tau, :], zsb[:, kb, tau, jf * 128:(jf + 1) * 128], g.ident[:]), reads=[bz, g.b_id], writes=[bpt[q]])
                            S.op('act', lambda e, q=q: e.copy(out=zo[q][:], in_=ps_t[q][:]), reads=[bpt[q]], writes=[bzo[q]])
                            S.dma(g.zT.ap()[chunk, :, kb * 1024:(kb + 1) * 1024], zo[q][:], reads=[bzo[q]], key=bzo[q])
                        q = zi % 2; zi += 1
                        for tau in range(8):
                            S.op('pe', lambda e, tau=tau, jf=jf, q=q: e.transpose(ps_t[q][:, tau, 0:32], zsb[0:32, NKB, tau, jf * 128:(jf + 1) * 128], g.ident[0:32, 0:32]), reads=[bz, g.b_id], writes=[bpt[q]])
                        S.op('act', lambda e, q=q: e.copy(out=sap(zc, 256, 0, [1, 8], [8, 32]), in_=ps_t[q][:, :, 0:32]), reads=[bpt[q]], writes=[bzc])
                        S.dma(g.zT.ap()[chunk, :, L:L + 256], zc[:], reads=[bzc], key=bzc)
            S.barrier()
    return phase_s5


def rope_tables(L):
    t = np.arange(L)
    row = (t // 64).astype(np.float32)
    col = (t % 64).astype(np.float32)
    nf = 16
    inv = (np.float32(10000.0) ** (-np.arange(nf, dtype=np.float32) / np.float32(nf))).astype(np.float32)
    ar = row[:, None] * inv
    ac = col[:, None] * inv
    ang = np.concatenate([ar, ar, ac, ac], axis=-1).astype(np.float32)
    cos = np.cos(ang).astype(np.float32)
    sin = np.sin(ang).astype(np.float32)
    idx = np.arange(L).reshape(L // 1024, 8, 128)
    kb, tau, p = np.meshgrid(np.arange(L // 1024), np.arange(8), np.arange(128), indexing='ij')
    tok = (1024 * kb + 8 * p + tau).reshape(-1)
    cosT = np.ascontiguousarray(cos[tok].T)
    sinT = np.ascontiguousarray(sin[tok].T)
    return np.concatenate([cosT, cosT], 0), np.concatenate([sinT, sinT], 0)


_CACHE = {}


def const_inputs(L):
    cosT, sinT = rope_tables(L)
    tri = np.kron(np.triu(np.ones((8, 8), np.float32)), np.ones((16, 16), np.float32))
    return {"rope_cos": cosT, "rope_sin": sinT, "identf": np.eye(128, dtype=np.float32),
            "maskl": tri, "masku": np.ascontiguousarray(tri.T)}


def kernel(**inputs):
    L = inputs["x"].shape[1]
    depth = inputs["ada_w"].shape[0]
    B = inputs["x"].shape[0]
    key = (L, depth)
    if key not in _CACHE:
        _CACHE[key] = build_program(L, depth)
    nc, g = _CACHE[key]
    consts = const_inputs(L)
    shared = {k: np.ascontiguousarray(v) for k, v in inputs.items() if k not in ("x", "c", "ctx", "c_ctx")}
    shared["c_ctx"] = np.ascontiguousarray(inputs["c_ctx"].reshape(1, D))
    shared.update(consts)
    in_maps = []
    for b in range(B):
        m = dict(shared)
        m["x"] = np.ascontiguousarray(inputs["x"][b])
        m["c"] = np.ascontiguousarray(inputs["c"][b:b + 1])
        m["ctx"] = np.ascontiguousarray(inputs["ctx"][b])
        in_maps.append(m)
    res = run_bass_kernel_spmd(nc, in_maps, core_ids=list(range(B)))
    return np.stack([r["out"] for r in res.results], axis=0)
```
